# Optimizing a Trainium2 kernel written in Bass

```python
import math
import jax, jax.numpy as jnp
from jax import lax
import numpy as np

D_MODEL = 1024
BATCH = 8
SEQ = 2048
DEPTH = 4

PLE_DIM = 256
SSM_WIDTH = 256
SSM_GROUP = 16
SSM_GROUPS = SSM_WIDTH // SSM_GROUP
SSM_STATE = 64
RET_HEADS = 6
RET_HEAD_DIM = 64
RET_WIDTH = RET_HEADS * RET_HEAD_DIM
RET_CHUNK = 128
DIFF_HEADS = 6
DIFF_QK_DIM = 32
DIFF_V_DIM = 64
DIFF_WIDTH = DIFF_HEADS * DIFF_V_DIM
ATTN_BLOCK = 128
MIX_WIDTH = SSM_WIDTH + RET_WIDTH + DIFF_WIDTH
PROJ_SIZES = (SSM_WIDTH,
              RET_WIDTH, RET_WIDTH, RET_WIDTH, RET_WIDTH,
              DIFF_HEADS * 2 * DIFF_QK_DIM,
              DIFF_HEADS * 2 * DIFF_QK_DIM,
              DIFF_WIDTH)
PROJ_WIDTH = sum(PROJ_SIZES)
PROJ_SPLITS = tuple(int(v) for v in np.cumsum(PROJ_SIZES)[:-1])
D_FF = 2816
CONV_WIDTH = 3
EPS = 1e-6

RET_LOG_GAMMA = np.log1p(-(2.0 ** (-5.0 - np.arange(RET_HEADS)))).astype(np.float32)
ALIBI_SLOPES = (2.0 ** (-8.0 * (np.arange(DIFF_HEADS) + 1) / DIFF_HEADS)).astype(np.float32)

kernel_name = "hybrid_s5_retention_diffattn_trunk"


def rmsnorm(x, g):
    xf = x.astype(jnp.float32)
    y = xf * lax.rsqrt(jnp.mean(xf * xf, axis=-1, keepdims=True) + EPS) * g.astype(jnp.float32)
    return y.astype(x.dtype)


def s5_mixer(u, lam_re, lam_im, log_dt, b_re, b_im, c_re, c_im, d_skip, w_glu, b_glu):
    f32 = jnp.float32
    bsz, L, _ = u.shape
    uf = u.astype(f32).reshape(bsz, L, SSM_GROUPS, SSM_GROUP)
    dt = jnp.exp(log_dt.astype(f32))[:, None]
    lr = lam_re.astype(f32)
    li = lam_im.astype(f32)
    mag = jnp.exp(lr * dt)
    ar = mag * jnp.cos(li * dt)
    ai = mag * jnp.sin(li * dt)
    den = lr * lr + li * li
    cr = ((ar - 1.0) * lr + ai * li) / den
    ci = (ai * lr - (ar - 1.0) * li) / den
    br = b_re.astype(f32)
    bi = b_im.astype(f32)
    bbr = cr[..., None] * br - ci[..., None] * bi
    bbi = cr[..., None] * bi + ci[..., None] * br
    bu_re = jnp.einsum('blgh,gph->blgp', uf, bbr)
    bu_im = jnp.einsum('blgh,gph->blgp', uf, bbi)
    a_re = jnp.broadcast_to(ar, bu_re.shape)
    a_im = jnp.broadcast_to(ai, bu_im.shape)

    def combine(e1, e2):
        a1r, a1i, b1r, b1i = e1
        a2r, a2i, b2r, b2i = e2
        return (a2r * a1r - a2i * a1i,
                a2r * a1i + a2i * a1r,
                a2r * b1r - a2i * b1i + b2r,
                a2r * b1i + a2i * b1r + b2i)

    _, _, xr, xi = lax.associative_scan(combine, (a_re, a_im, bu_re, bu_im), axis=1)
    y = (jnp.einsum('blgp,ghp->blgh', xr, c_re.astype(f32))
         - jnp.einsum('blgp,ghp->blgh', xi, c_im.astype(f32)))
    y = y.reshape(bsz, L, SSM_WIDTH) + d_skip.astype(f32) * uf.reshape(bsz, L, SSM_WIDTH)
    z = jax.nn.gelu(y)
    out = z * jax.nn.sigmoid(z @ w_glu.astype(f32) + b_glu.astype(f32))
    return out.astype(u.dtype)


def retention(q, k, v, g, gn_g):
    f32 = jnp.float32
    bsz, L, _ = q.shape
    H, dh, C = RET_HEADS, RET_HEAD_DIM, RET_CHUNK
    nc = L // C
    lg = jnp.asarray(RET_LOG_GAMMA)

    def chunks(t):
        return t.astype(f32).reshape(bsz, nc, C, H, dh).transpose(1, 0, 3, 2, 4)

    qc = chunks(q)
    kc = chunks(k) * (dh ** -0.5)
    vc = chunks(v)
    pos = jnp.arange(C, dtype=f32)
    dist = pos[:, None] - pos[None, :]
    decay = jnp.where(dist[None] >= 0,
                      jnp.exp(jnp.maximum(dist, 0.0)[None] * lg[:, None, None]), 0.0)
    inner = jnp.einsum('nbhcd,nbhsd->nbhcs', qc, kc) * decay
    inner_out = jnp.einsum('nbhcs,nbhse->nbhce', inner, vc)
    q_dec = qc * jnp.exp((pos + 1.0)[None, :] * lg[:, None])[:, :, None]
    k_dec = kc * jnp.exp((C - 1.0 - pos)[None, :] * lg[:, None])[:, :, None]
    chunk_decay = jnp.exp(C * lg)[None, :, None, None]

    def step(state, inp):
        qd, kd, vv = inp
        cross = jnp.einsum('bhcd,bhde->bhce', qd, state)
        state = state * chunk_decay + jnp.einsum('bhsd,bhse->bhde', kd, vv)
        return state, cross

    state0 = jnp.zeros((bsz, H, dh, dh), f32)
    _, cross = lax.scan(step, state0, (q_dec, k_dec, vc))
    o = (inner_out + cross).transpose(1, 0, 3, 2, 4).reshape(bsz, L, H, dh)
    mu = jnp.mean(o, axis=-1, keepdims=True)
    var = jnp.mean(jnp.square(o - mu), axis=-1, keepdims=True)
    o = ((o - mu) * lax.rsqrt(var + EPS)).reshape(bsz, L, RET_WIDTH) * gn_g.astype(f32)
    return (jax.nn.silu(g.astype(f32)) * o).astype(q.dtype)


def diff_attention(q, k, v, lq1, lk1, lq2, lk2, subln_g, lambda_init):
    f32 = jnp.float32
    bsz, L, _ = q.shape
    H, d, e = DIFF_HEADS, DIFF_QK_DIM, DIFF_V_DIM
    qf = q.astype(f32).reshape(bsz, L, H, 2, d) * (d ** -0.5)
    kf = k.astype(f32).reshape(bsz, L, H, 2, d)
    vf = v.astype(f32).reshape(bsz, L, H, e)
    lam = (jnp.exp(jnp.sum(lq1.astype(f32) * lk1.astype(f32)))
           - jnp.exp(jnp.sum(lq2.astype(f32) * lk2.astype(f32))) + lambda_init)
    slopes = jnp.asarray(ALIBI_SLOPES)
    outs = []
    for i in range(L // ATTN_BLOCK):
        q0 = i * ATTN_BLOCK
        kv_len = q0 + ATTN_BLOCK
        qb = qf[:, q0:kv_len]
        kb = kf[:, :kv_len]
        vb = vf[:, :kv_len]
        dist = (jnp.arange(q0, kv_len)[:, None] - jnp.arange(kv_len)[None, :]).astype(f32)
        bias = jnp.where(dist[None] >= 0, -slopes[:, None, None] * dist[None], -jnp.inf)
        s = jnp.einsum('bqhmd,bkhmd->bmhqk', qb, kb) + bias
        a = jax.nn.softmax(s, axis=-1)
        attn = a[:, 0] - lam * a[:, 1]
        outs.append(jnp.einsum('bhqk,bkhe->bqhe', attn, vb))
    o = jnp.concatenate(outs, axis=1)
    o = o * lax.rsqrt(jnp.mean(o * o, axis=-1, keepdims=True) + EPS)
    o = o * subln_g.astype(f32).reshape(H, e) * (1.0 - lambda_init)
    return o.reshape(bsz, L, DIFF_WIDTH).astype(q.dtype)


def conv_gated_mlp(x, w_up, conv_w, conv_b, w_down):
    L = x.shape[1]
    u = x @ w_up
    up = jnp.pad(u, ((0, 0), (CONV_WIDTH - 1, 0), (0, 0)))
    c = (conv_w[0] * up[:, 0:L] + conv_w[1] * up[:, 1:L + 1] + conv_w[2] * up[:, 2:L + 2] + conv_b)
    a, b = jnp.split(c, 2, axis=-1)
    return (jax.nn.silu(a) * b) @ w_down


def setup_inputs(seed: int = 0) -> dict:
    key = jax.random.key(seed)
    ks = jax.random.split(key, 32)
    f32 = jnp.float32
    nrm = lambda k, s: jax.random.normal(k, s, f32)
    gain = lambda k, s: 1.0 + 0.02 * nrm(k, s)
    x = nrm(ks[0], (BATCH, SEQ, D_MODEL))
    p = nrm(ks[1], (DEPTH, BATCH, SEQ, PLE_DIM))
    norm1_g = gain(ks[2], (DEPTH, D_MODEL))
    w_in = nrm(ks[3], (DEPTH, D_MODEL, PROJ_WIDTH)) * D_MODEL ** -0.5
    n_idx = jnp.arange(SSM_STATE, dtype=f32)
    ssm_lam_re = -0.5 + 0.01 * nrm(ks[4], (DEPTH, SSM_GROUPS, SSM_STATE))
    ssm_lam_im = math.pi * n_idx + 0.01 * nrm(ks[5], (DEPTH, SSM_GROUPS, SSM_STATE))
    ssm_log_dt = jax.random.uniform(ks[6], (DEPTH, SSM_GROUPS), f32, math.log(1e-3), math.log(1e-1))
    bsc = (0.5 / SSM_GROUP) ** 0.5
    csc = (0.5 / SSM_STATE) ** 0.5
    ssm_b_re = nrm(ks[7], (DEPTH, SSM_GROUPS, SSM_STATE, SSM_GROUP)) * bsc
    ssm_b_im = nrm(ks[8], (DEPTH, SSM_GROUPS, SSM_STATE, SSM_GROUP)) * bsc
    ssm_c_re = nrm(ks[9], (DEPTH, SSM_GROUPS, SSM_GROUP, SSM_STATE)) * csc
    ssm_c_im = nrm(ks[10], (DEPTH, SSM_GROUPS, SSM_GROUP, SSM_STATE)) * csc
    ssm_d = nrm(ks[11], (DEPTH, SSM_WIDTH))
    ssm_w_glu = nrm(ks[12], (DEPTH, SSM_WIDTH, SSM_WIDTH)) * SSM_WIDTH ** -0.5
    ssm_b_glu = 0.01 * nrm(ks[13], (DEPTH, SSM_WIDTH))
    ret_gn_g = gain(ks[14], (DEPTH, RET_WIDTH))
    diff_lq1 = 0.1 * nrm(ks[15], (DEPTH, DIFF_QK_DIM))
    diff_lk1 = 0.1 * nrm(ks[16], (DEPTH, DIFF_QK_DIM))
    diff_lq2 = 0.1 * nrm(ks[17], (DEPTH, DIFF_QK_DIM))
    diff_lk2 = 0.1 * nrm(ks[18], (DEPTH, DIFF_QK_DIM))
    diff_subln_g = gain(ks[19], (DEPTH, DIFF_WIDTH))
    w_out = nrm(ks[20], (DEPTH, MIX_WIDTH, D_MODEL)) * MIX_WIDTH ** -0.5
    norm2_g = gain(ks[21], (DEPTH, D_MODEL))
    w_up = nrm(ks[22], (DEPTH, D_MODEL, 2 * D_FF)) * D_MODEL ** -0.5
    conv_w = nrm(ks[23], (DEPTH, CONV_WIDTH, 2 * D_FF)) * CONV_WIDTH ** -0.5
    conv_b = 0.01 * nrm(ks[24], (DEPTH, 2 * D_FF))
    w_down = nrm(ks[25], (DEPTH, D_FF, D_MODEL)) * D_FF ** -0.5
    norm3_g = gain(ks[26], (DEPTH, D_MODEL))
    w_pg = nrm(ks[27], (DEPTH, D_MODEL, D_MODEL)) * D_MODEL ** -0.5
    w_pe = nrm(ks[28], (DEPTH, PLE_DIM, D_MODEL)) * PLE_DIM ** -0.5
    final_g = gain(ks[29], (D_MODEL,))
    return {"x": x, "p": p, "norm1_g": norm1_g, "w_in": w_in,
            "ssm_lam_re": ssm_lam_re, "ssm_lam_im": ssm_lam_im, "ssm_log_dt": ssm_log_dt,
            "ssm_b_re": ssm_b_re, "ssm_b_im": ssm_b_im, "ssm_c_re": ssm_c_re, "ssm_c_im": ssm_c_im,
            "ssm_d": ssm_d, "ssm_w_glu": ssm_w_glu, "ssm_b_glu": ssm_b_glu,
            "ret_gn_g": ret_gn_g,
            "diff_lq1": diff_lq1, "diff_lk1": diff_lk1, "diff_lq2": diff_lq2, "diff_lk2": diff_lk2,
            "diff_subln_g": diff_subln_g, "w_out": w_out,
            "norm2_g": norm2_g, "w_up": w_up, "conv_w": conv_w, "conv_b": conv_b, "w_down": w_down,
            "norm3_g": norm3_g, "w_pg": w_pg, "w_pe": w_pe, "final_g": final_g}


def reference(x, p, norm1_g, w_in, ssm_lam_re, ssm_lam_im, ssm_log_dt, ssm_b_re, ssm_b_im,
              ssm_c_re, ssm_c_im, ssm_d, ssm_w_glu, ssm_b_glu, ret_gn_g,
              diff_lq1, diff_lk1, diff_lq2, diff_lk2, diff_subln_g, w_out,
              norm2_g, w_up, conv_w, conv_b, w_down, norm3_g, w_pg, w_pe, final_g):
    h = x
    for l in range(DEPTH):
        lambda_init = 0.8 - 0.6 * math.exp(-0.3 * l)
        hn = rmsnorm(h, norm1_g[l])
        proj = hn @ w_in[l]
        s_u, r_q, r_k, r_v, r_g, d_q, d_k, d_v = jnp.split(proj, PROJ_SPLITS, axis=-1)
        s_out = s5_mixer(s_u, ssm_lam_re[l], ssm_lam_im[l], ssm_log_dt[l], ssm_b_re[l], ssm_b_im[l],
                         ssm_c_re[l], ssm_c_im[l], ssm_d[l], ssm_w_glu[l], ssm_b_glu[l])
        r_out = retention(r_q, r_k, r_v, r_g, ret_gn_g[l])
        d_out = diff_attention(d_q, d_k, d_v, diff_lq1[l], diff_lk1[l], diff_lq2[l], diff_lk2[l],
                               diff_subln_g[l], lambda_init)
        h = h + jnp.concatenate([s_out, r_out, d_out], axis=-1) @ w_out[l]
        h = h + conv_gated_mlp(rmsnorm(h, norm2_g[l]), w_up[l], conv_w[l], conv_b[l], w_down[l])
        gate = jax.nn.sigmoid(rmsnorm(h, norm3_g[l]) @ w_pg[l])
        h = h + gate * (p[l] @ w_pe[l])
    return rmsnorm(h, final_g)
```

```python
import math
from contextlib import ExitStack
import numpy as np
import concourse.bass as bass
import concourse.mybir as mybir
from concourse.bass_utils import run_bass_kernel_spmd

F32 = mybir.dt.float32
BF16 = mybir.dt.bfloat16
I32 = mybir.dt.int32
ALU = mybir.AluOpType
AF = mybir.ActivationFunctionType
AX = mybir.AxisListType

NL, D, SEQ = 4, 1024, 2048
BT, NBLK = 512, 4
PROJ = 2944
DFF = 2816
NCT = 22
EPS = 1e-6
RET_LOG_GAMMA = np.log1p(-(2.0 ** (-5.0 - np.arange(6)))).astype(np.float32)
ALIBI_SLOPES = (2.0 ** (-8.0 * (np.arange(6) + 1) / 6)).astype(np.float32)
TWO_PI = 2.0 * math.pi


class Buf:
    __slots__ = ("w", "r")

    def __init__(self):
        self.w = None
        self.r = {}


class V:
    __slots__ = ("ap", "bufs")

    def __init__(self, ap, bufs):
        self.ap = ap
        self.bufs = bufs


class Ctx:
    def __init__(self, nc, es, n_dma_sems=20):
        self.nc = nc
        self.es = es
        self.engs = {"pe": nc.tensor, "act": nc.scalar, "dve": nc.vector, "pool": nc.gpsimd, "sp": nc.sync}
        self.epoch = 0
        self.sems = {}
        self.cnt = {}
        self.seen = {k: {} for k in self.engs}
        self.semobj = {}
        self.dma_pool = {"sp": [], "pool": []}
        for i in range(n_dma_sems):
            s = es.enter_context(nc.semaphore("dsem%d" % i))
            self.semobj[("dma", i)] = s
            self.dma_pool["sp" if i < n_dma_sems - 6 else "pool"].append([("dma", i), 0])
        self.dma_next = {"sp": 0, "pool": 0}
        self.pending = {k: [] for k in self.engs}
        self.log = {k: [] for k in self.engs}

    def simulate(self):
        val = {}
        ptr = {k: 0 for k in self.engs}
        prog = True
        while prog:
            prog = False
            for e, lg in self.log.items():
                while ptr[e] < len(lg):
                    kind, key, v = lg[ptr[e]]
                    if kind == "wait":
                        if val.get(key, 0) < v:
                            break
                    else:
                        val[key] = val.get(key, 0) + v
                    ptr[e] += 1
                    prog = True
        stuck = {e: (ptr[e], len(lg), lg[ptr[e]], val.get(lg[ptr[e]][1], 0)) for e, lg in self.log.items() if ptr[e] < len(lg)}
        return stuck, {e: len(lg) for e, lg in self.log.items()}

    def _sem(self, e):
        key = (e, self.epoch)
        if key not in self.semobj:
            self.semobj[key] = self.es.enter_context(self.nc.semaphore("s_%s_%d" % key))
            self.cnt[key] = 0
        return key

    def new_epoch(self):
        self.epoch += 1

    def _wait(self, e, deps):
        seen = self.seen[e]
        for key, val in deps:
            if e == "pe" and key[0] == "pe":
                continue
            if seen.get(key, 0) >= val:
                continue
            self.engs[e].wait_ge(self.semobj[key], val)
            self.log[e].append(("wait", key, val))
            seen[key] = val

    def _deps(self, r, w):
        deps = []
        for v in r:
            for b in v.bufs:
                if b.w is not None:
                    deps.append(b.w)
        for v in w:
            for b in v.bufs:
                if b.w is not None:
                    deps.append(b.w)
                deps.extend(b.r.items())
        return deps

    def _mark(self, tok, r, w):
        key, val = tok
        for v in r:
            for b in v.bufs:
                if b.r.get(key, 0) < val:
                    b.r[key] = val
        for v in w:
            for b in v.bufs:
                b.w = tok
                b.r = {}

    def op(self, e, fn, r=(), w=(), signal=True, selfwait=False):
        self._wait(e, self._deps(r, w))
        if selfwait:
            key0 = self._sem(e)
            if self.cnt[key0] > 0:
                self.engs[e].wait_ge(self.semobj[key0], self.cnt[key0])
                self.log[e].append(("wait", key0, self.cnt[key0]))
        inst = fn(self.engs[e])
        key = self._sem(e)
        if signal:
            self.cnt[key] += 1
            inst.then_inc(self.semobj[key], 1)
            self.log[e].append(("inc", key, 1))
            tok = (key, self.cnt[key])
        else:
            tok = (key, self.cnt[key] + 1)
        self._mark(tok, r, w)
        return inst

    def dma(self, e, out, in_, r=(), w=()):
        pool = self.dma_pool[e]
        slot = pool[self.dma_next[e]]
        self.dma_next[e] = (self.dma_next[e] + 1) % len(pool)
        key, val = slot
        deps = self._deps(r, w)
        if val > 0:
            deps.append((key, val))
        self._wait(e, deps)
        inst = self.engs[e].dma_start(out=out, in_=in_)
        slot[1] = val + 16
        inst.then_inc(self.semobj[key], 16)
        self.log[e].append(("inc", key, 16))
        self._mark((key, val + 16), r, w)
        return inst

    def wait_all(self, e, views):
        deps = []
        for v in views:
            for b in v.bufs:
                if b.w is not None:
                    deps.append(b.w)
        self._wait(e, deps)


def build_program(cfg):
    nl = cfg.get("layers", NL)
    nblk = cfg.get("blocks", NBLK)
    on = lambda k: cfg.get(k, True)
    dbg = cfg.get("dbg", False)
    nc = bass.Bass("TRN2", target_bir_lowering=False)
    es = ExitStack()
    dram_in = {}

    def din(name, shape, dt=F32):
        dram_in[name] = nc.dram_tensor(name, list(shape), dt, kind="ExternalInput").ap()
        return dram_in[name]

    xT_d = din("xT", [128, 8, SEQ])
    pT_d = din("pT", [NL, 128, 2, SEQ])
    w_in_d = din("w_in", [nl, D, PROJ])
    w_out_d = din("w_out", [nl, D, D])
    w_up_d = din("w_up", [nl, D, 2 * DFF])
    w_down_d = din("w_down", [nl, DFF, D])
    w_pg_d = din("w_pg", [nl, D, D])
    w_pe_d = din("w_pe", [nl, 256, D])
    w_glu_d = din("w_glu", [NL, 128, 2, 256])
    gains_d = din("gains", [128, 3 * NL + 1, 8])
    cw_d = din("cw", [128, NL, 4, 2 * NCT])
    s5row_d = din("s5row", [NL, 128, 3, 1024])
    s5b_d = din("s5b", [NL, 128, 2, 2, 512])
    s5c_d = din("s5c", [NL, 128, 2, 8, 128])
    s5db_d = din("s5db", [128, NL, 2, 2])
    rows_d = din("rows", [128, NL, 2, 384])
    lqk_d = din("lqk", [128, NL, 4, 32])
    cmat_d = din("cmat", [128, 2, 128])
    rtab_d = din("rtab", [128, 6 * 128 + 6 + 384 + 2])
    rmask_d = din("rmask", [128, 2, 3, 128])
    aug_d = din("aug", [128, 2, 2, SEQ])
    yT_d = nc.dram_tensor("yT", [128, 8, SEQ], F32, kind="ExternalOutput").ap()
    dbg_d = {}
    if dbg:
        for l in range(nl):
            dbg_d["h%d" % l] = nc.dram_tensor("dbg_h%d" % l, [128, 8, SEQ], F32, kind="ExternalOutput").ap()
        dbg_d["mix"] = nc.dram_tensor("dbg_mix", [128, 8, SEQ], BF16, kind="ExternalOutput").ap()
    wb = {}
    for nm, shp in [("w_in", [nl, D, PROJ]), ("w_out", [nl, D, D]), ("w_up", [nl, D, 2 * DFF]),
                    ("w_down", [nl, DFF, D]), ("w_pg", [nl, D, D]), ("w_pe", [nl, 256, D])]:
        wb[nm] = nc.dram_tensor("wb_" + nm, shp, BF16, kind="Internal").ap()
    wsrc = {"w_in": w_in_d, "w_out": w_out_d, "w_up": w_up_d, "w_down": w_down_d, "w_pg": w_pg_d, "w_pe": w_pe_d}

    C = Ctx(nc, es)

    def sb(name, shape, dt):
        t = es.enter_context(nc.sbuf_tensor("sb_" + name, list(shape), dt))
        return V(t[:], [Buf()])

    def sub(v, ap):
        return V(ap, v.bufs)

    h = sb("h", [128, 8, SEQ], F32)
    hblk = [[Buf() for _ in range(NBLK)]]
    hn = sb("hn", [128, 8, BT], BF16)
    mix = sb("mix", [128, 8, BT], BF16)
    gains = sb("gains", [128, 3 * NL + 1, 8], F32)
    cw = sb("cw", [128, 4, 2 * NCT], F32)
    s5db = sb("s5db", [128, NL, 2, 2], F32)
    rows = sb("rows", [128, 2, 384], F32)
    lqk = sb("lqk", [128, 4, 32], F32)
    cmatf = sb("cmatf", [128, 2, 128], F32)
    cmatb = sb("cmatb", [128, 2, 128], BF16)
    rtab = sb("rtab", [128, 6 * 128 + 6 + 384 + 2], F32)
    rmaskb = sb("rmaskb", [128, 2, 3, 128], BF16)
    augk = sb("augk", [128, 2, SEQ], BF16)
    augq = sb("augq", [128, 2, BT], BF16)
    onesb = sb("onesb", [128, 128], BF16)
    zerob = sb("zerob", [128, 260], BF16)
    halo = sb("halo", [128, 2 * NCT, 2], F32)
    winv = sb("winv", [128, 2, 1024], BF16)
    wtf = sb("wtf", [128, 2, 8, 128], BF16)
    wlast = sb("wlast", [128, 2, 8], F32)
    bbbd = sb("bbbd", [128, 2, 1024], BF16)
    ctb = sb("ctb", [128, 2, 8, 128], BF16)
    carry = [sb("carry%d" % i, [128, 2, 4], F32) for i in range(2)]
    wglu = sb("wglu", [128, 2, 256], BF16)
    rstate = sb("rstate", [128, 384], F32)
    rstate_b = sb("rstate_b", [128, 384], BF16)
    kTc = sb("kTc", [128, 4, SEQ], BF16)
    vaug = sb("vaug", [128, 16, 6, 65], BF16)
    lam = sb("lam", [128, 4], F32)
    subg = sb("subg", [128, 384], F32)
    NSLOT = 3
    SLOT_ELEMS = 4096
    slots = [sb("slot%d" % i, [128, SLOT_ELEMS], BF16) for i in range(NSLOT)]
    SCR_BYTES = 30 * 1024
    scr_t = es.enter_context(nc.sbuf_tensor("scr", [128, SCR_BYTES // 2], BF16))
    CELL = 1024
    scr_cells = [Buf() for _ in range(SCR_BYTES // CELL)]

    def carve(off, shape, dt):
        esz = 4 if dt in (F32, I32) else 2
        n = int(np.prod(shape[1:]))
        nb = n * esz
        assert off % 4 == 0 and off + nb <= SCR_BYTES, (off, nb)
        ap = scr_t[0:shape[0], off // 2:(off + nb) // 2]
        if esz == 4:
            ap = ap.bitcast(dt)
        if len(shape) == 3:
            ap = ap.rearrange("p (a b) -> p a b", a=shape[1])
        elif len(shape) == 4:
            ap = ap.rearrange("p (a b c) -> p a b c", a=shape[1], b=shape[2])
        return V(ap, scr_cells[off // CELL:(off + nb + CELL - 1) // CELL])

    ps = []
    for i in range(7):
        t = es.enter_context(nc.psum_tensor("ps%d" % i, [128, 512], F32))
        ps.append(V(t[:], [Buf()]))
    pst_t = es.enter_context(nc.psum_tensor("pst", [128, 1024], BF16))
    pst = V(pst_t[:], [Buf()])

    cast_sems = [es.enter_context(nc.semaphore("cast%d" % l)) for l in range(nl)]
    cast_tot = [0] * nl
    for l in range(nl):
        for nm in ["w_in", "w_out", "w_up", "w_down", "w_pg", "w_pe"]:
            K, N = wsrc[nm].shape[1], wsrc[nm].shape[2]
            for r0 in range(0, K, 512):
                r1 = min(K, r0 + 512)
                for c0 in range(0, N, 2048):
                    c1 = min(N, c0 + 2048)
                    nc.gpsimd.dma_start(out=wb[nm][l, r0:r1, c0:c1], in_=wsrc[nm][l, r0:r1, c0:c1]).then_inc(cast_sems[l], 16)
                    cast_tot[l] += 16

    def load(dst, src):
        C.dma("sp", dst.ap, src, w=[dst])

    load(gains, gains_d)
    load(s5db, s5db_d)
    load(cmatf, cmat_d)
    load(rtab, rtab_d)
    for kt in range(8):
        C.dma("sp", h.ap[:, kt, :], xT_d[:, kt, :], w=[h])
    C.op("dve", lambda e: e.tensor_copy(out=cmatb.ap, in_=cmatf.ap), r=[cmatf], w=[cmatb])
    C.op("dve", lambda e: e.memset(onesb.ap, 1.0), w=[onesb])
    C.op("dve", lambda e: e.memset(zerob.ap, 0.0), w=[zerob])
    C.op("dve", lambda e: e.memset(vaug.ap, 1.0), w=[vaug])
    identb = sub(cmatb, cmatb.ap[:, 0, :])
    trib = sub(cmatb, cmatb.ap[:, 1, :])
    identf = sub(cmatf, cmatf.ap[:, 0, :])
    for at_ in range(2):
        tmp = carve(at_ * 8192, [128, SEQ], F32)
        C.dma("sp", tmp.ap, aug_d[:, 0, at_, :], w=[tmp])
        C.op("act", lambda e, tmp=tmp, at_=at_: e.activation(out=augk.ap[:, at_, :], in_=tmp.ap, func=AF.Copy), r=[tmp], w=[augk])
    tmp = carve(16384, [128, 2, 3, 128], F32)
    C.dma("sp", tmp.ap, rmask_d, w=[tmp])
    C.op("dve", lambda e: e.tensor_copy(out=rmaskb.ap, in_=tmp.ap), r=[tmp], w=[rmaskb])
    o0 = 0
    ret_qtab = sub(rtab, rtab.ap[:, o0:o0 + 768].rearrange("p (a b) -> p a b", a=6)); o0 += 768
    ret_ktab = sub(rtab, rtab.ap[:, o0:o0 + 6]); o0 += 6
    ret_mask = rmaskb
    ret_cd = sub(rtab, rtab.ap[:, o0:o0 + 384]); o0 += 384
    tpos = sub(rtab, rtab.ap[:, o0:o0 + 1])
    tneg = sub(rtab, rtab.ap[:, o0 + 1:o0 + 2])

    slab_state = {"i": 0, "queue": [], "issued": 0}

    def slab_specs(l):
        sp = []
        if on("s5"):
            sp.append(("w_in", l, 8, 0, 256))
        if on("ret"):
            for c0 in (256, 640, 1024, 1408):
                sp.append(("w_in", l, 8, c0, c0 + 384))
        if on("diff"):
            for c0 in (1792, 2176, 2560):
                sp.append(("w_in", l, 8, c0, c0 + 384))
        for c0 in (0, 512):
            sp.append(("w_out", l, 8, c0, c0 + 512))
        if on("ffn"):
            for g in range(NCT // 2):
                sp.append(("w_up", l, 8, g, None))
            for dp in range(4):
                for hf in range(2):
                    sp.append(("w_down", l, 11, dp, hf))
        if on("ple"):
            for c0 in (0, 512):
                sp.append(("w_pg", l, 8, c0, c0 + 512))
                sp.append(("w_pe", l, 2, c0, c0 + 512))
        return sp

    all_specs = []
    for l in range(nl):
        for b in range(nblk):
            all_specs.extend(slab_specs(l))
    cast_waited = set()

    def issue_slab(idx):
        nm, l, kt, a, b = all_specs[idx]
        slot = slots[idx % NSLOT]
        if l not in cast_waited:
            nc.sync.wait_ge(cast_sems[l], cast_tot[l])
            cast_waited.add(l)
        if nm == "w_up":
            g = a
            dst = slot.ap[:, 0:8 * 512].rearrange("p (k m c) -> p k m c", k=8, m=2)
            for m in range(2):
                c0 = m * DFF + g * 256
                src = wb[nm][l, :, c0:c0 + 256].rearrange("(k p) n -> p k n", p=128)
                C.dma("sp", dst[:, :, m, :], src, w=[slot])
        elif nm == "w_down":
            dp, hf = a, b
            dst = slot.ap[:, 0:11 * 256].rearrange("p (k c) -> p k c", k=11)
            src = wb[nm][l, hf * 1408:(hf + 1) * 1408, dp * 256:(dp + 1) * 256].rearrange("(k p) n -> p k n", p=128)
            C.dma("sp", dst, src, w=[slot])
        else:
            n = b - a
            dst = slot.ap[:, 0:kt * n].rearrange("p (k c) -> p k c", k=kt)
            src = wb[nm][l, :, a:b].rearrange("(k p) n -> p k n", p=128)
            C.dma("sp", dst, src, w=[slot])

    def next_slab(expect):
        i = slab_state["i"]
        assert all_specs[i][0] == expect, (all_specs[i], expect)
        assert i < slab_state["issued"], "slab not issued (too many held)"
        slab_state["i"] = i + 1
        nm, l, kt, a, b = all_specs[i]
        slot = slots[i % NSLOT]
        if nm == "w_up":
            return sub(slot, slot.ap[:, 0:8 * 512].rearrange("p (k m c) -> p k m c", k=8, m=2))
        if nm == "w_down":
            return sub(slot, slot.ap[:, 0:11 * 256].rearrange("p (k c) -> p k c", k=11))
        n = b - a
        return sub(slot, slot.ap[:, 0:kt * n].rearrange("p (k c) -> p k c", k=kt))

    def slab_done():
        if slab_state["issued"] < len(all_specs):
            issue_slab(slab_state["issued"])
            slab_state["issued"] += 1

    for _ in range(min(NSLOT, len(all_specs))):
        issue_slab(slab_state["issued"])
        slab_state["issued"] += 1

    def mm(out_v, out_ap, lhsT_v, lhsT_ap, rhs_v, rhs_ap, start, stop, signal=None, selfwait=False):
        if signal is None:
            signal = True
        C.op("pe", lambda e: e.matmul(out_ap, lhsT_ap, rhs_ap, start=start, stop=stop),
             r=[lhsT_v, rhs_v], w=[out_v], signal=signal, selfwait=selfwait)

    def hsl(b):
        return slice(b * BT, (b + 1) * BT)

    def rmsnorm(b, gidx, dst, scrA, scrB):
        pss = ps[6]
        sq = [carve(scrA, [128, BT], BF16), carve(scrA + 1024, [128, BT], BF16)]
        for kt in range(8):
            s = sq[kt % 2]
            C.op("act", lambda e, s=s, kt=kt: e.activation(out=s.ap, in_=h.ap[:, kt, hsl(b)], func=AF.Square), r=[h], w=[s])
            mm(pss, pss.ap, onesb, onesb.ap, s, s.ap, kt == 0, kt == 7)
        rs = carve(scrB, [128, BT], F32)
        C.op("act", lambda e: e.activation(out=rs.ap, in_=pss.ap, func=AF.Ln, scale=1.0 / D, bias=EPS), r=[pss], w=[rs])
        C.op("act", lambda e: e.activation(out=rs.ap, in_=rs.ap, func=AF.Exp, scale=-0.5), r=[rs], w=[rs])
        for kt in range(8):
            C.op("dve", lambda e, kt=kt: e.scalar_tensor_tensor(out=dst.ap[:, kt, :], in0=h.ap[:, kt, hsl(b)],
                                                               scalar=gains.ap[:, gidx, kt:kt + 1], in1=rs.ap,
                                                               op0=ALU.mult, op1=ALU.mult), r=[h, gains, rs], w=[dst])

    def proj_fm(w, cols, evac, width=128):
        for j, c0 in enumerate(cols):
            p = ps[4 + (j % 2)]
            for kt in range(8):
                mm(p, p.ap[0:width, :], w, w.ap[:, kt, c0:c0 + width], hn, hn.ap[:, kt, :], kt == 0, kt == 7)
            evac(j, p)

    def proj_tm(w, ncols, evac):
        for c in range(4):
            p = ps[4 + (c % 2)]
            for kt in range(8):
                mm(p, p.ap[:, 0:ncols], hn, hn.ap[:, kt, c * 128:(c + 1) * 128], w, w.ap[:, kt, 0:ncols], kt == 0, kt == 7)
            evac(c, p)

    def h_add(b, dt, p):
        C.op("dve", lambda e: e.tensor_tensor(out=h.ap[:, dt, hsl(b)], in0=p.ap, in1=h.ap[:, dt, hsl(b)], op=ALU.add), r=[p, h], w=[h])

    def s5_setup(l):
        CTf = carve(0, [128, 2, 8, 128], F32)
        C.dma("sp", CTf.ap, s5c_d[l], w=[CTf])
        C.op("dve", lambda e: e.tensor_copy(out=ctb.ap[:, 0], in_=CTf.ap[:, 0]), r=[CTf], w=[ctb])
        C.op("dve", lambda e: e.tensor_scalar(out=ctb.ap[:, 1], in0=CTf.ap[:, 1], scalar1=-1.0, scalar2=None, op0=ALU.mult), r=[CTf], w=[ctb])
        WG = carve(8192, [128, 2, 256], F32)
        C.dma("sp", WG.ap, w_glu_d[l], w=[WG])
        C.op("dve", lambda e: e.tensor_copy(out=wglu.ap, in_=WG.ap), r=[WG], w=[wglu])
        for i in range(2):
            C.op("dve", lambda e, i=i: e.memset(carry[i].ap, 0.0), w=[carry[i]])
        for hf in range(2):
            s5_setup_half(l, hf)

    def s5_setup_half(l, hf):
        HS = slice(hf * 512, (hf + 1) * 512)
        T = [carve(i * 2048, [128, 512], F32) for i in range(9)]
        LR, LI, A, B, Cc, Dd, E, Fv, G = T
        TI = carve(9 * 2048, [128, 512], I32)
        MK = carve(10 * 2048, [128, 512], F32)
        C.dma("sp", LR.ap, s5row_d[l, :, 0, HS], w=[LR])
        C.dma("sp", LI.ap, s5row_d[l, :, 1, HS], w=[LI])
        C.dma("sp", A.ap, s5row_d[l, :, 2, HS], w=[A])

        def tt(eng, o, a, b_, op):
            C.op(eng, lambda e: e.tensor_tensor(out=o.ap, in0=a.ap, in1=b_.ap, op=op), r=[a, b_], w=[o])

        def act(o, a, func, **kw):
            C.op("act", lambda e: e.activation(out=o.ap, in_=a.ap, func=func, **kw), r=[a], w=[o])

        def sincos(phi, s_out, c_out):
            C.op("dve", lambda e: e.tensor_scalar(out=TI.ap, in0=phi.ap, scalar1=1.0 / TWO_PI, scalar2=None, op0=ALU.mult), r=[phi], w=[TI])
            C.op("dve", lambda e: e.tensor_copy(out=MK.ap, in_=TI.ap), r=[TI], w=[MK])
            C.op("dve", lambda e: e.scalar_tensor_tensor(out=s_out.ap, in0=MK.ap, scalar=-TWO_PI, in1=phi.ap, op0=ALU.mult, op1=ALU.add), r=[MK, phi], w=[s_out])
            C.op("dve", lambda e: e.tensor_scalar(out=MK.ap, in0=s_out.ap, scalar1=math.pi, scalar2=-TWO_PI, op0=ALU.is_gt, op1=ALU.mult), r=[s_out], w=[MK])
            tt("dve", s_out, s_out, MK, ALU.add)
            C.op("dve", lambda e: e.tensor_scalar(out=MK.ap, in0=s_out.ap, scalar1=-math.pi, scalar2=TWO_PI, op0=ALU.is_lt, op1=ALU.mult), r=[s_out], w=[MK])
            tt("dve", s_out, s_out, MK, ALU.add)
            C.op("dve", lambda e: e.tensor_scalar(out=MK.ap, in0=s_out.ap, scalar1=math.pi / 2, scalar2=-TWO_PI, op0=ALU.is_gt, op1=ALU.mult), r=[s_out], w=[MK])
            C.op("dve", lambda e: e.scalar_tensor_tensor(out=c_out.ap, in0=s_out.ap, scalar=math.pi / 2, in1=MK.ap, op0=ALU.add, op1=ALU.add), r=[s_out, MK], w=[c_out])
            act(s_out, s_out, AF.Sin)
            act(c_out, c_out, AF.Sin)

        act(A, A, AF.Exp)
        tt("dve", B, LR, A, ALU.mult)
        tt("dve", Cc, LI, A, ALU.mult)
        sincos(Cc, A, Dd)
        act(E, B, AF.Exp)
        tt("dve", Dd, E, Dd, ALU.mult)
        tt("dve", A, E, A, ALU.mult)
        C.op("dve", lambda e: e.tensor_scalar(out=Dd.ap, in0=Dd.ap, scalar1=-1.0, scalar2=None, op0=ALU.add), r=[Dd], w=[Dd])
        tt("dve", E, LR, LR, ALU.mult)
        tt("dve", Fv, LI, LI, ALU.mult)
        tt("dve", E, E, Fv, ALU.add)
        C.op("dve", lambda e: e.reciprocal(out=E.ap, in_=E.ap), r=[E], w=[E])
        tt("dve", Fv, Dd, LR, ALU.mult)
        tt("dve", G, A, LI, ALU.mult)
        tt("dve", Fv, Fv, G, ALU.add)
        tt("dve", Fv, Fv, E, ALU.mult)
        tt("dve", G, A, LR, ALU.mult)
        tt("dve", A, Dd, LI, ALU.mult)
        tt("dve", G, G, A, ALU.subtract)
        tt("dve", G, G, E, ALU.mult)
        BB = carve(0, [128, 2, 512], F32)
        C.dma("sp", BB.ap, s5b_d[l, :, :, hf, :], w=[BB])
        t1, t2 = A, Dd
        C.op("dve", lambda e: e.tensor_tensor(out=t1.ap, in0=Fv.ap, in1=BB.ap[:, 0, :], op=ALU.mult), r=[Fv, BB], w=[t1])
        C.op("dve", lambda e: e.tensor_tensor(out=t2.ap, in0=G.ap, in1=BB.ap[:, 1, :], op=ALU.mult), r=[G, BB], w=[t2])
        C.op("dve", lambda e: e.tensor_tensor(out=bbbd.ap[:, hf, 0:512], in0=t1.ap, in1=t2.ap, op=ALU.subtract), r=[t1, t2], w=[bbbd])
        C.op("dve", lambda e: e.tensor_tensor(out=t1.ap, in0=Fv.ap, in1=BB.ap[:, 1, :], op=ALU.mult), r=[Fv, BB], w=[t1])
        C.op("dve", lambda e: e.tensor_tensor(out=t2.ap, in0=G.ap, in1=BB.ap[:, 0, :], op=ALU.mult), r=[G, BB], w=[t2])
        C.op("dve", lambda e: e.tensor_tensor(out=bbbd.ap[:, hf, 512:1024], in0=t1.ap, in1=t2.ap, op=ALU.add), r=[t1, t2], w=[bbbd])
        C.op("dve", lambda e: e.tensor_scalar(out=A.ap, in0=Cc.ap, scalar1=tpos.ap, scalar2=None, op0=ALU.mult), r=[Cc, rtab], w=[A])
        sincos(A, Dd, E)
        C.op("act", lambda e: e.activation(out=Fv.ap, in_=B.ap, func=AF.Exp, scale=tpos.ap), r=[B, rtab], w=[Fv])
        C.op("act", lambda e: e.activation(out=G.ap, in_=B.ap, func=AF.Exp, scale=tneg.ap), r=[B, rtab], w=[G])
        C.op("dve", lambda e: e.tensor_tensor(out=winv.ap[:, 0, HS], in0=G.ap, in1=E.ap, op=ALU.mult), r=[G, E], w=[winv])
        C.op("dve", lambda e: e.scalar_tensor_tensor(out=winv.ap[:, 1, HS], in0=G.ap, scalar=-1.0, in1=Dd.ap, op0=ALU.mult, op1=ALU.mult), r=[G, Dd], w=[winv])
        tt("dve", E, Fv, E, ALU.mult)
        tt("dve", Dd, Fv, Dd, ALU.mult)
        for ri, src in enumerate((E, Dd)):
            for jl in range(4):
                j = 4 * hf + jl
                p = ps[jl % 2]
                C.op("pe", lambda e, p=p, src=src, jl=jl: e.transpose(p.ap[:, 0:128], src.ap[:, jl * 128:(jl + 1) * 128], identf.ap), r=[src, cmatf], w=[p])
                C.op("act", lambda e, p=p, ri=ri, j=j: e.activation(out=wtf.ap[:, ri, j, :], in_=p.ap[:, 0:128], func=AF.Copy), r=[p], w=[wtf])
                C.op("dve", lambda e, p=p, ri=ri, j=j: e.tensor_copy(out=wlast.ap[:, ri, j:j + 1], in_=p.ap[:, 127:128]), r=[p], w=[wlast])

    def s5_block(l, b):
        w = next_slab("w_in")
        uT = carve(0, [128, 2, BT], BF16)
        Tm = [carve(2048 + k * 1024, [128, BT], BF16) for k in range(4)]
        Z = carve(6144, [128, 2, BT], BF16)
        SC = carve(8192, [128, 2, 4, 128], F32)
        U = [carve(12288 + k * 1024, [128, 4, 128], BF16) for k in range(4)]
        CT4 = [carve(16384 + k * 1024, [128, 4], F32) for k in range(4)]
        X = carve(20480, [128, 2, 4, 128], BF16)
        YV = carve(22528, [128, BT], F32)
        G1 = carve(24576, [128, BT], F32)
        ZB = carve(26624, [128, 2, BT], BF16)

        def ev(j, p):
            C.op("act", lambda e: e.activation(out=uT.ap[:, j, :], in_=p.ap, func=AF.Copy), r=[p], w=[uT])
        proj_fm(w, [0, 128], ev)
        slab_done()
        for c in range(4):
            cs = slice(c * 128, (c + 1) * 128)
            for i in range(2):
                mm(ps[0], ps[0].ap, uT, uT.ap[:, i, cs], bbbd, bbbd.ap[:, i, 0:512], True, True)
                mm(ps[1], ps[1].ap, uT, uT.ap[:, i, cs], bbbd, bbbd.ap[:, i, 512:1024], True, True)
                isl = slice(i * 512, (i + 1) * 512)
                for k, (pp, ri) in enumerate(((0, 0), (1, 1), (0, 1), (1, 0))):
                    C.op("dve", lambda e, k=k, pp=pp, ri=ri: e.tensor_tensor(out=Tm[k].ap, in0=ps[pp].ap, in1=winv.ap[:, ri, isl], op=ALU.mult),
                         r=[ps[pp], winv], w=[Tm[k]])
                C.op("pool", lambda e: e.tensor_tensor(out=Z.ap[:, 0, :], in0=Tm[0].ap, in1=Tm[1].ap, op=ALU.subtract), r=[Tm[0], Tm[1]], w=[Z])
                C.op("pool", lambda e: e.tensor_tensor(out=Z.ap[:, 1, :], in0=Tm[2].ap, in1=Tm[3].ap, op=ALU.add), r=[Tm[2], Tm[3]], w=[Z])
                for ri in range(2):
                    for jl in range(4):
                        mm(ps[2 + ri], ps[2 + ri].ap[:, jl * 128:(jl + 1) * 128], Z, Z.ap[:, ri, jl * 128:(jl + 1) * 128], trib, trib.ap, True, True)
                for ri in range(2):
                    C.op("dve", lambda e, ri=ri: e.tensor_tensor(out=SC.ap[:, ri], in0=ps[2 + ri].ap.rearrange("p (a b) -> p a b", a=4),
                                                                 in1=carry[i].ap[:, ri, :].unsqueeze(2).broadcast_to([128, 4, 128]), op=ALU.add),
                         r=[ps[2 + ri], carry[i]], w=[SC])
                wr = wtf.ap[:, 0, 4 * i:4 * i + 4, :]
                wi = wtf.ap[:, 1, 4 * i:4 * i + 4, :]
                C.op("dve", lambda e: e.tensor_tensor(out=U[0].ap, in0=SC.ap[:, 0], in1=wr, op=ALU.mult), r=[SC, wtf], w=[U[0]])
                C.op("pool", lambda e: e.tensor_tensor(out=U[1].ap, in0=SC.ap[:, 1], in1=wi, op=ALU.mult), r=[SC, wtf], w=[U[1]])
                C.op("dve", lambda e: e.tensor_tensor(out=U[2].ap, in0=SC.ap[:, 0], in1=wi, op=ALU.mult), r=[SC, wtf], w=[U[2]])
                C.op("pool", lambda e: e.tensor_tensor(out=U[3].ap, in0=SC.ap[:, 1], in1=wr, op=ALU.mult), r=[SC, wtf], w=[U[3]])
                C.op("pool", lambda e: e.tensor_tensor(out=X.ap[:, 0], in0=U[0].ap, in1=U[1].ap, op=ALU.subtract), r=[U[0], U[1]], w=[X])
                C.op("pool", lambda e: e.tensor_tensor(out=X.ap[:, 1], in0=U[2].ap, in1=U[3].ap, op=ALU.add), r=[U[2], U[3]], w=[X])
                for k, (ra, rb) in enumerate(((0, 0), (1, 1), (0, 1), (1, 0))):
                    C.op("dve", lambda e, k=k, ra=ra, rb=rb: e.tensor_tensor(out=CT4[k].ap, in0=SC.ap[:, ra, :, 127], in1=wlast.ap[:, rb, 4 * i:4 * i + 4], op=ALU.mult), r=[SC, wlast], w=[CT4[k]])
                C.op("dve", lambda e: e.tensor_tensor(out=carry[i].ap[:, 0, :], in0=CT4[0].ap, in1=CT4[1].ap, op=ALU.subtract), r=[CT4[0], CT4[1]], w=[carry[i]])
                C.op("dve", lambda e: e.tensor_tensor(out=carry[i].ap[:, 1, :], in0=CT4[2].ap, in1=CT4[3].ap, op=ALU.add), r=[CT4[2], CT4[3]], w=[carry[i]])
                py = ps[4 + i]
                for jl in range(4):
                    for ri in range(2):
                        mm(py, py.ap[:, cs], ctb, ctb.ap[:, ri, 4 * i + jl, :], X, X.ap[:, ri, jl, :], jl == 0 and ri == 0, jl == 3 and ri == 1)
        for i in range(2):
            py = ps[4 + i]
            C.op("dve", lambda e: e.scalar_tensor_tensor(out=YV.ap, in0=uT.ap[:, i, :], scalar=s5db.ap[:, l, 0, i:i + 1], in1=py.ap, op0=ALU.mult, op1=ALU.add), r=[uT, s5db, py], w=[YV])
            C.op("act", lambda e: e.activation(out=G1.ap, in_=YV.ap, func=AF.Square), r=[YV], w=[G1])
            C.op("dve", lambda e: e.tensor_scalar(out=G1.ap, in0=G1.ap, scalar1=0.044715, scalar2=1.0, op0=ALU.mult, op1=ALU.add), r=[G1], w=[G1])
            C.op("dve", lambda e: e.tensor_tensor(out=G1.ap, in0=G1.ap, in1=YV.ap, op=ALU.mult), r=[G1, YV], w=[G1])
            C.op("act", lambda e: e.activation(out=G1.ap, in_=G1.ap, func=AF.Sigmoid, scale=2.0 * 0.7978845608028654), r=[G1], w=[G1])
            C.op("dve", lambda e: e.tensor_tensor(out=ZB.ap[:, i, :], in0=YV.ap, in1=G1.ap, op=ALU.mult), r=[YV, G1], w=[ZB])
        for io in range(2):
            pg = ps[4 + io]
            for i in range(2):
                mm(pg, pg.ap, wglu, wglu.ap[:, i, io * 128:(io + 1) * 128], ZB, ZB.ap[:, i, :], i == 0, i == 1)
            C.op("act", lambda e: e.activation(out=G1.ap, in_=pg.ap, func=AF.Sigmoid, bias=s5db.ap[:, l, 1, io:io + 1]), r=[pg, s5db], w=[G1])
            C.op("dve", lambda e: e.tensor_tensor(out=mix.ap[:, io, :], in0=ZB.ap[:, io, :], in1=G1.ap, op=ALU.mult), r=[ZB, G1], w=[mix])

    def ret_block(l, b):
        qd = carve(0, [128, 6, BT], BF16)
        kT = carve(6144, [128, 6, BT], BF16)
        kdec = carve(12288, [128, 4, 384], BF16)
        vt = carve(15360, [128, 4, 384], BF16)
        gg = carve(18432, [128, 4, 384], BF16)
        inn = carve(21504, [128, 2, 3, 128], BF16)
        OC = carve(23040, [128, 6, 64], F32)
        SQ = carve(24576, [128, 6, 64], F32)
        GT = carve(26112, [128, 384], F32)
        ROUT = carve(27648, [128, 384], BF16)
        M1 = carve(28672, [128, 6], F32)
        M2 = carve(29696, [128, 6], F32)
        gn = rows.ap[:, 0, :]
        wq = next_slab("w_in")

        def evq(j, p):
            C.op("dve", lambda e: e.tensor_tensor(out=qd.ap[0:64, j, :].rearrange("p (c t) -> p c t", c=4), in0=p.ap[0:64, :].rearrange("p (c t) -> p c t", c=4),
                                                  in1=ret_qtab.ap[0:64, j, :].unsqueeze(1).broadcast_to([64, 4, 128]), op=ALU.mult), r=[p, rtab], w=[qd])
        proj_fm(wq, [0, 64, 128, 192, 256, 320], evq, width=64)
        slab_done()
        wk = next_slab("w_in")

        def evk(j, p):
            C.op("act", lambda e: e.activation(out=kT.ap[0:64, j, :], in_=p.ap[0:64, :], func=AF.Copy), r=[p], w=[kT])
        proj_fm(wk, [0, 64, 128, 192, 256, 320], evk, width=64)

        def evkt(c, p):
            C.op("dve", lambda e: e.tensor_tensor(out=kdec.ap[:, c, :].rearrange("p (h d) -> p h d", h=6), in0=p.ap[:, 0:384].rearrange("p (h d) -> p h d", h=6),
                                                  in1=ret_ktab.ap.unsqueeze(2).broadcast_to([128, 6, 64]), op=ALU.mult), r=[p, rtab], w=[kdec])
        proj_tm(wk, 384, evkt)
        slab_done()
        wv = next_slab("w_in")

        def evv(c, p):
            C.op("act", lambda e: e.activation(out=vt.ap[:, c, :], in_=p.ap[:, 0:384], func=AF.Copy), r=[p], w=[vt])
        proj_tm(wv, 384, evv)
        slab_done()
        wg = next_slab("w_in")

        def evg(c, p):
            C.op("act", lambda e: e.activation(out=GT.ap, in_=p.ap[:, 0:384], func=AF.Silu), r=[p], w=[GT])
            C.op("dve", lambda e: e.tensor_tensor(out=gg.ap[:, c, :], in0=GT.ap, in1=gn, op=ALU.mult), r=[GT, rows], w=[gg])
        proj_tm(wg, 384, evg)
        slab_done()
        if b == 0:
            C.op("dve", lambda e: e.memset(rstate.ap, 0.0), w=[rstate])
            C.op("dve", lambda e: e.memset(rstate_b.ap, 0.0), w=[rstate_b])
        for c in range(4):
            cs = slice(c * 128, (c + 1) * 128)
            for hh in range(6):
                pp = ps[hh % 2]
                o = (hh // 2) * 128
                mm(pp, pp.ap[:, o:o + 128], kT, kT.ap[0:64, hh, cs], qd, qd.ap[0:64, hh, cs], True, True)
            for par in range(2):
                C.op("dve", lambda e, par=par: e.tensor_tensor(out=inn.ap[:, par], in0=ps[par].ap[:, 0:384].rearrange("p (a b) -> p a b", a=3), in1=ret_mask.ap[:, par], op=ALU.mult), r=[ps[par], rmaskb], w=[inn])
            po = ps[2]
            for hh in range(6):
                hs = slice(hh * 64, (hh + 1) * 64)
                mm(po, po.ap[:, hs], inn, inn.ap[:, hh % 2, hh // 2, :], vt, vt.ap[:, c, hs], True, False, signal=False)
                mm(po, po.ap[:, hs], qd, qd.ap[0:64, hh, cs], rstate_b, rstate_b.ap[0:64, hs], False, True, signal=(hh == 5))
            pS = ps[3]
            for hh in range(6):
                hs = slice(hh * 64, (hh + 1) * 64)
                mm(pS, pS.ap[0:64, hs], kdec, kdec.ap[:, c, hs], vt, vt.ap[:, c, hs], True, True, signal=(hh == 5))
            C.op("dve", lambda e: e.tensor_tensor(out=rstate.ap[0:64], in0=rstate.ap[0:64], in1=ret_cd.ap[0:64], op=ALU.mult), r=[rstate, rtab], w=[rstate])
            C.op("dve", lambda e: e.tensor_tensor(out=rstate.ap[0:64], in0=pS.ap[0:64, 0:384], in1=rstate.ap[0:64], op=ALU.add), r=[pS, rstate], w=[rstate])
            C.op("dve", lambda e: e.tensor_copy(out=rstate_b.ap[0:64], in_=rstate.ap[0:64]), r=[rstate], w=[rstate_b])
            po3 = po.ap[:, 0:384].rearrange("p (h d) -> p h d", h=6)
            C.op("dve", lambda e: e.tensor_reduce(out=M1.ap, in_=po3, axis=AX.X, op=ALU.add), r=[po], w=[M1])
            C.op("dve", lambda e: e.tensor_scalar(out=M1.ap, in0=M1.ap, scalar1=-1.0 / 64, scalar2=None, op0=ALU.mult), r=[M1], w=[M1])
            C.op("dve", lambda e: e.tensor_tensor(out=OC.ap, in0=po3, in1=M1.ap.unsqueeze(2).broadcast_to([128, 6, 64]), op=ALU.add), r=[po, M1], w=[OC])
            C.op("act", lambda e: e.activation(out=SQ.ap, in_=OC.ap, func=AF.Square), r=[OC], w=[SQ])
            C.op("dve", lambda e: e.tensor_reduce(out=M2.ap, in_=SQ.ap, axis=AX.X, op=ALU.add), r=[SQ], w=[M2])
            C.op("dve", lambda e: e.tensor_scalar(out=M2.ap, in0=M2.ap, scalar1=1.0 / 64, scalar2=EPS, op0=ALU.mult, op1=ALU.add), r=[M2], w=[M2])
            C.op("act", lambda e: e.activation(out=M2.ap, in_=M2.ap, func=AF.Sqrt), r=[M2], w=[M2])
            C.op("dve", lambda e: e.reciprocal(out=M2.ap, in_=M2.ap), r=[M2], w=[M2])
            C.op("dve", lambda e: e.tensor_tensor(out=OC.ap, in0=OC.ap, in1=M2.ap.unsqueeze(2).broadcast_to([128, 6, 64]), op=ALU.mult), r=[OC, M2], w=[OC])
            C.op("dve", lambda e: e.tensor_tensor(out=ROUT.ap, in0=OC.ap.rearrange("p h d -> p (h d)"), in1=gg.ap[:, c, :], op=ALU.mult), r=[OC, gg], w=[ROUT])
            for j in range(3):
                C.op("pe", lambda e, j=j: e.transpose(pst.ap[:, j * 128:(j + 1) * 128], ROUT.ap[:, j * 128:(j + 1) * 128], identb.ap), r=[ROUT, cmatb], w=[pst], signal=(j == 2))
            C.op("act", lambda e: e.activation(out=mix.ap[:, 2:5, cs], in_=pst.ap[:, 0:384].rearrange("p (a b) -> p a b", a=3), func=AF.Copy), r=[pst], w=[mix])

    def diff_setup(l):
        lam_init = 0.8 - 0.6 * math.exp(-0.3 * l)
        T1 = carve(0, [128, 2, 32], F32)
        C.op("dve", lambda e: e.tensor_tensor(out=T1.ap[:, 0, :], in0=lqk.ap[:, 0, :], in1=lqk.ap[:, 1, :], op=ALU.mult), r=[lqk], w=[T1])
        C.op("dve", lambda e: e.tensor_tensor(out=T1.ap[:, 1, :], in0=lqk.ap[:, 2, :], in1=lqk.ap[:, 3, :], op=ALU.mult), r=[lqk], w=[T1])
        C.op("dve", lambda e: e.tensor_reduce(out=lam.ap[:, 0:2], in_=T1.ap, axis=AX.X, op=ALU.add), r=[T1], w=[lam])
        C.op("act", lambda e: e.activation(out=lam.ap[:, 0:2], in_=lam.ap[:, 0:2], func=AF.Exp), r=[lam], w=[lam])
        C.op("dve", lambda e: e.tensor_tensor(out=lam.ap[:, 2:3], in0=lam.ap[:, 0:1], in1=lam.ap[:, 1:2], op=ALU.subtract), r=[lam], w=[lam])
        C.op("dve", lambda e: e.tensor_scalar(out=lam.ap[:, 3:4], in0=lam.ap[:, 2:3], scalar1=lam_init, scalar2=-1.0, op0=ALU.add, op1=ALU.mult), r=[lam], w=[lam])
        C.op("dve", lambda e: e.tensor_scalar(out=subg.ap, in0=rows.ap[:, 1, :], scalar1=1.0 - lam_init, scalar2=None, op0=ALU.mult), r=[rows], w=[subg])

    def diff_block(l, b):
        qT = carve(0, [128, 4, BT], BF16)
        P = [carve(4096 + k * 2048, [128, 2, BT], BF16) for k in range(2)]
        dtok = carve(8192, [128, 4, 384], BF16)
        UU = carve(11264, [128, 4, 64], F32)
        TT = carve(12288, [128, 4, 64], F32)
        R0 = carve(13312, [128, 4], F32)
        R1 = carve(14336, [128, 4], F32)
        aqf = carve(15360, [128, 2, BT], F32)
        C.dma("sp", aqf.ap, aug_d[:, 1, :, hsl(b)], w=[aqf])
        C.op("act", lambda e: e.activation(out=augq.ap, in_=aqf.ap, func=AF.Copy), r=[aqf], w=[augq])
        wq = next_slab("w_in")

        def evq(j, p):
            C.op("act", lambda e: e.activation(out=qT.ap[0:96, j, :], in_=p.ap[0:96, :], func=AF.Copy, scale=32.0 ** -0.5), r=[p], w=[qT])
        proj_fm(wq, [0, 96, 192, 288], evq, width=96)
        slab_done()
        wk = next_slab("w_in")

        def evk(j, p):
            C.op("act", lambda e: e.activation(out=kTc.ap[0:96, j, hsl(b)], in_=p.ap[0:96, :], func=AF.Copy), r=[p], w=[kTc])
        proj_fm(wk, [0, 96, 192, 288], evk, width=96)
        slab_done()
        wv = next_slab("w_in")

        def evv(c, p):
            C.op("act", lambda e: e.activation(out=vaug.ap[:, 4 * b + c, :, 0:64], in_=p.ap[:, 0:384].rearrange("p (h d) -> p h d", h=6), func=AF.Copy), r=[p], w=[vaug])
        proj_tm(wv, 384, evv)
        slab_done()
        it = 0
        for hh in range(6):
            at, ab = hh // 3, 32 * (hh % 3)
            acc = [ps[2], ps[3]]
            for m in range(2):
                mm(acc[m], acc[m].ap[:, 0:260], zerob, zerob.ap[:, 0:128], zerob, zerob.ap[:, 0:260], True, False)
            nkt = 4 * b + 4
            for kt in range(nkt):
                j = kt - 4 * b
                q0 = max(j, 0) * 128
                ks = slice(kt * 128, (kt + 1) * 128)
                Pk = P[it % 2]
                pss = [ps[0], ps[1]] if it % 2 == 0 else [ps[4], ps[5]]
                it += 1
                for m in range(2):
                    mi = 2 * hh + m
                    tj, bp = mi // 3, 32 * (mi % 3)
                    mm(pss[m], pss[m].ap[:, q0:BT], kTc, kTc.ap[bp:bp + 32, tj, ks], qT, qT.ap[bp:bp + 32, tj, q0:BT], True, False, signal=True)
                    mm(pss[m], pss[m].ap[:, q0:BT], augk, augk.ap[ab:ab + 12, at, ks], augq, augq.ap[ab:ab + 12, at, q0:BT], False, True, selfwait=True)
                for m in range(2):
                    C.op("act", lambda e, m=m: e.activation(out=Pk.ap[:, m, q0:BT], in_=pss[m].ap[:, q0:BT], func=AF.Exp), r=[pss[m]], w=[Pk])
                if j >= 0:
                    C.op("pool", lambda e: e.affine_select(out=Pk.ap[:, :, q0:q0 + 128], in_=Pk.ap[:, :, q0:q0 + 128], pattern=[[0, 2], [1, 128]],
                                                           compare_op=ALU.is_ge, fill=0.0, base=0, channel_multiplier=-1), r=[Pk], w=[Pk])
                last = (kt == nkt - 1)
                for m in range(2):
                    for qt in range(max(j, 0), 4):
                        mm(acc[m], acc[m].ap[:, qt * 65:(qt + 1) * 65], Pk, Pk.ap[:, m, qt * 128:(qt + 1) * 128], vaug, vaug.ap[:, kt, hh, :], False, last and qt == 3,
                           signal=(qt == 3))
            a0 = acc[0].ap[:, 0:260].rearrange("p (q e) -> p q e", q=4)
            a1 = acc[1].ap[:, 0:260].rearrange("p (q e) -> p q e", q=4)
            C.op("dve", lambda e: e.reciprocal(out=R0.ap, in_=a0[:, :, 64]), r=[acc[0]], w=[R0])
            C.op("dve", lambda e: e.reciprocal(out=R1.ap, in_=a1[:, :, 64]), r=[acc[1]], w=[R1])
            C.op("dve", lambda e: e.tensor_scalar(out=R1.ap, in0=R1.ap, scalar1=lam.ap[:, 3:4], scalar2=None, op0=ALU.mult), r=[R1, lam], w=[R1])
            C.op("dve", lambda e: e.tensor_tensor(out=TT.ap, in0=a1[:, :, 0:64], in1=R1.ap.unsqueeze(2).broadcast_to([128, 4, 64]), op=ALU.mult), r=[acc[1], R1], w=[TT])
            C.op("dve", lambda e: e.tensor_tensor(out=UU.ap, in0=a0[:, :, 0:64], in1=R0.ap.unsqueeze(2).broadcast_to([128, 4, 64]), op=ALU.mult), r=[acc[0], R0], w=[UU])
            C.op("dve", lambda e: e.tensor_tensor(out=UU.ap, in0=UU.ap, in1=TT.ap, op=ALU.add), r=[UU, TT], w=[UU])
            C.op("act", lambda e: e.activation(out=TT.ap, in_=UU.ap, func=AF.Square), r=[UU], w=[TT])
            C.op("dve", lambda e: e.tensor_reduce(out=R0.ap, in_=TT.ap, axis=AX.X, op=ALU.add), r=[TT], w=[R0])
            C.op("dve", lambda e: e.tensor_scalar(out=R0.ap, in0=R0.ap, scalar1=1.0 / 64, scalar2=EPS, op0=ALU.mult, op1=ALU.add), r=[R0], w=[R0])
            C.op("act", lambda e: e.activation(out=R0.ap, in_=R0.ap, func=AF.Sqrt), r=[R0], w=[R0])
            C.op("dve", lambda e: e.reciprocal(out=R0.ap, in_=R0.ap), r=[R0], w=[R0])
            C.op("dve", lambda e: e.tensor_tensor(out=UU.ap, in0=UU.ap, in1=R0.ap.unsqueeze(2).broadcast_to([128, 4, 64]), op=ALU.mult), r=[UU, R0], w=[UU])
            C.op("dve", lambda e, hh=hh: e.tensor_tensor(out=dtok.ap[:, :, hh * 64:(hh + 1) * 64], in0=UU.ap,
                                                        in1=subg.ap[:, hh * 64:(hh + 1) * 64].unsqueeze(1).broadcast_to([128, 4, 64]), op=ALU.mult), r=[UU, subg], w=[dtok])
        for qt in range(4):
            for j in range(3):
                C.op("pe", lambda e, j=j, qt=qt: e.transpose(pst.ap[:, j * 128:(j + 1) * 128], dtok.ap[:, qt, j * 128:(j + 1) * 128], identb.ap), r=[dtok, cmatb], w=[pst], signal=(j == 2))
            C.op("act", lambda e, qt=qt: e.activation(out=mix.ap[:, 5:8, qt * 128:(qt + 1) * 128], in_=pst.ap[:, 0:384].rearrange("p (a b) -> p a b", a=3), func=AF.Copy), r=[pst], w=[mix])

    def ffn_block(l, b):
        rmsnorm(b, NL + l, hn, 22528, 24576)
        act = carve(0, [128, NCT, BT], BF16)
        c0b = [carve(22528 + k * 2048, [128, BT], F32) for k in range(3)] + [carve(28672, [128, BT], F32)]
        for g in range(NCT // 2):
            w = next_slab("w_up")
            for cl in range(2):
                ct = 2 * g + cl
                pa, pb = (ps[0], ps[1]) if ct % 2 == 0 else (ps[2], ps[3])
                cbuf = c0b[2 * (ct % 2):2 * (ct % 2) + 2]
                for m, p in enumerate((pa, pb)):
                    for kt in range(8):
                        mm(p, p.ap, w, w.ap[:, kt, m, cl * 128:(cl + 1) * 128], hn, hn.ap[:, kt, :], kt == 0, kt == 7)
                for m, p in enumerate((pa, pb)):
                    ci = m * NCT + ct
                    cc = cbuf[m]
                    C.op("act", lambda e, p=p, cc=cc, ci=ci: e.activation(out=cc.ap, in_=p.ap, func=AF.Identity, scale=cw.ap[:, 2, ci:ci + 1], bias=cw.ap[:, 3, ci:ci + 1]), r=[p, cw], w=[cc])
                    C.op("dve", lambda e, p=p, cc=cc, ci=ci: e.scalar_tensor_tensor(out=cc.ap[:, 1:BT], in0=p.ap[:, 0:BT - 1], scalar=cw.ap[:, 1, ci:ci + 1], in1=cc.ap[:, 1:BT], op0=ALU.mult, op1=ALU.add), r=[p, cw, cc], w=[cc])
                    C.op("dve", lambda e, p=p, cc=cc, ci=ci: e.scalar_tensor_tensor(out=cc.ap[:, 2:BT], in0=p.ap[:, 0:BT - 2], scalar=cw.ap[:, 0, ci:ci + 1], in1=cc.ap[:, 2:BT], op0=ALU.mult, op1=ALU.add), r=[p, cw, cc], w=[cc])
                    if b > 0:
                        C.op("dve", lambda e, cc=cc, ci=ci: e.scalar_tensor_tensor(out=cc.ap[:, 0:2], in0=halo.ap[:, ci, 0:2], scalar=cw.ap[:, 0, ci:ci + 1], in1=cc.ap[:, 0:2], op0=ALU.mult, op1=ALU.add), r=[halo, cw, cc], w=[cc])
                        C.op("dve", lambda e, cc=cc, ci=ci: e.scalar_tensor_tensor(out=cc.ap[:, 0:1], in0=halo.ap[:, ci, 1:2], scalar=cw.ap[:, 1, ci:ci + 1], in1=cc.ap[:, 0:1], op0=ALU.mult, op1=ALU.add), r=[halo, cw, cc], w=[cc])
                    if b < NBLK - 1:
                        C.op("dve", lambda e, p=p, ci=ci: e.tensor_copy(out=halo.ap[:, ci, :], in_=p.ap[:, BT - 2:BT]), r=[p], w=[halo])
                C.op("act", lambda e, cbuf=cbuf: e.activation(out=cbuf[0].ap, in_=cbuf[0].ap, func=AF.Silu), r=[cbuf[0]], w=[cbuf[0]])
                C.op("dve", lambda e, cbuf=cbuf, ct=ct: e.tensor_tensor(out=act.ap[:, ct, :], in0=cbuf[0].ap, in1=cbuf[1].ap, op=ALU.mult), r=[cbuf[0], cbuf[1]], w=[act])
            slab_done()
        for dp in range(4):
            pd = [ps[4], ps[5]]
            for hf in range(2):
                w = next_slab("w_down")
                for dl in range(2):
                    for k in range(11):
                        mm(pd[dl], pd[dl].ap, w, w.ap[:, k, dl * 128:(dl + 1) * 128], act, act.ap[:, hf * 11 + k, :], hf == 0 and k == 0, hf == 1 and k == 10,
                           signal=(k == 10))
                slab_done()
            for dl in range(2):
                h_add(b, 2 * dp + dl, pd[dl])

    def ple_block(l, b):
        rmsnorm(b, 2 * NL + l, hn, 0, 2048)
        pf = carve(4096, [128, 2, BT], F32)
        pb = carve(8192, [128, 2, BT], BF16)
        sg = [carve(10240 + k * 2048, [128, BT], F32) for k in range(2)]
        C.dma("sp", pf.ap, pT_d[l, :, :, hsl(b)], w=[pf])
        C.op("pool", lambda e: e.tensor_copy(out=pb.ap, in_=pf.ap), r=[pf], w=[pb])
        for half in range(2):
            w = next_slab("w_pg")
            wpe = next_slab("w_pe")
            for dl in range(4):
                dt = 4 * half + dl
                pg, pp = (ps[0], ps[1]) if dt % 2 == 0 else (ps[2], ps[3])
                c0 = dl * 128
                for kt in range(8):
                    mm(pg, pg.ap, w, w.ap[:, kt, c0:c0 + 128], hn, hn.ap[:, kt, :], kt == 0, kt == 7)
                for kt in range(2):
                    mm(pp, pp.ap, wpe, wpe.ap[:, kt, c0:c0 + 128], pb, pb.ap[:, kt, :], kt == 0, kt == 1)
                s_ = sg[dt % 2]
                C.op("act", lambda e, s_=s_, pg=pg: e.activation(out=s_.ap, in_=pg.ap, func=AF.Sigmoid), r=[pg], w=[s_])
                C.op("dve", lambda e, s_=s_, pp=pp: e.tensor_tensor(out=s_.ap, in0=pp.ap, in1=s_.ap, op=ALU.mult), r=[pp, s_], w=[s_])
                C.op("pool", lambda e, s_=s_, dt=dt: e.tensor_tensor(out=h.ap[:, dt, hsl(b)], in0=s_.ap, in1=h.ap[:, dt, hsl(b)], op=ALU.add), r=[s_, h], w=[h])
            slab_done()
            slab_done()

    for l in range(nl):
        if l > 0:
            C.new_epoch()
        C.dma("sp", cw.ap, cw_d[:, l], w=[cw])
        C.dma("sp", rows.ap, rows_d[:, l], w=[rows])
        C.dma("sp", lqk.ap, lqk_d[:, l], w=[lqk])
        if on("s5"):
            s5_setup(l)
        if on("diff"):
            diff_setup(l)
        for b in range(nblk):
            rmsnorm(b, l, hn, 26624, 28672)
            if on("s5"):
                s5_block(l, b)
            else:
                C.op("dve", lambda e: e.memset(mix.ap[:, 0:2, :], 0.0), w=[mix])
            if on("ret"):
                ret_block(l, b)
            else:
                C.op("dve", lambda e: e.memset(mix.ap[:, 2:5, :], 0.0), w=[mix])
            if on("diff"):
                diff_block(l, b)
            else:
                C.op("dve", lambda e: e.memset(mix.ap[:, 5:8, :], 0.0), w=[mix])
            if dbg and l == nl - 1:
                C.dma("sp", dbg_d["mix"][:, :, hsl(b)], mix.ap, r=[mix])
            for half in range(2):
                w = next_slab("w_out")
                for dl in range(4):
                    dt = 4 * half + dl
                    p = ps[4 + dt % 2]
                    c0 = dl * 128
                    for kt in range(8):
                        mm(p, p.ap, w, w.ap[:, kt, c0:c0 + 128], mix, mix.ap[:, kt, :], kt == 0, kt == 7)
                    h_add(b, dt, p)
                slab_done()
            if on("ffn"):
                ffn_block(l, b)
            if on("ple"):
                ple_block(l, b)
            if dbg:
                C.dma("sp", dbg_d["h%d" % l][:, :, hsl(b)], h.ap[:, :, hsl(b)], r=[h])
            if l == nl - 1:
                yo = carve(0, [128, 8, BT], F32)
                rmsnorm(b, 3 * NL, yo, 16384, 18432)
                C.dma("sp", yT_d[:, :, hsl(b)], yo.ap, r=[yo])
    deps = [(s_[0], s_[1]) for pl in C.dma_pool.values() for s_ in pl if s_[1] > 0]
    C._wait("sp", deps)
    stuck, counts = C.simulate()
    print("instr counts", counts, "stuck", stuck)
    assert not stuck, stuck
    es.close()
    return nc


def host_layouts(inp):
    f = np.float32
    L = NL
    shared = {}
    for nm in ["w_in", "w_out", "w_up", "w_down", "w_pg", "w_pe"]:
        shared[nm] = np.ascontiguousarray(inp[nm], dtype=f)
    shared["w_glu"] = np.ascontiguousarray(np.asarray(inp["ssm_w_glu"], f).reshape(L, 2, 128, 256).transpose(0, 2, 1, 3))
    gl = [np.asarray(inp["norm1_g"], f), np.asarray(inp["norm2_g"], f), np.asarray(inp["norm3_g"], f)]
    gains = np.concatenate([g.reshape(L, 8, 128) for g in gl] + [np.asarray(inp["final_g"], f).reshape(1, 8, 128)], 0)
    shared["gains"] = np.ascontiguousarray(gains.transpose(2, 0, 1))
    cwv = np.concatenate([np.asarray(inp["conv_w"], f), np.asarray(inp["conv_b"], f)[:, None, :]], 1)
    shared["cw"] = np.ascontiguousarray(cwv.reshape(L, 4, 2 * NCT, 128).transpose(3, 0, 1, 2))
    lre = np.asarray(inp["ssm_lam_re"], f).reshape(L, 1024)
    lim = np.asarray(inp["ssm_lam_im"], f).reshape(L, 1024)
    ldt = np.repeat(np.asarray(inp["ssm_log_dt"], f), 64, axis=1)
    row = np.stack([lre, lim, ldt], 1)
    shared["s5row"] = np.ascontiguousarray(np.broadcast_to(row[:, None], (L, 128, 3, 1024)))
    bre = np.asarray(inp["ssm_b_re"], f)
    bim = np.asarray(inp["ssm_b_im"], f)
    s5b = np.zeros((L, 128, 2, 2, 512), f)
    for ri, bb in enumerate((bre, bim)):
        for g in range(16):
            i, gl_ = g // 8, g % 8
            s5b[:, gl_ * 16:(gl_ + 1) * 16, ri, i, gl_ * 64:(gl_ + 1) * 64] = bb[:, g].transpose(0, 2, 1)
    shared["s5b"] = s5b
    cre = np.asarray(inp["ssm_c_re"], f)
    cim = np.asarray(inp["ssm_c_im"], f)
    s5c = np.zeros((L, 128, 2, 8, 128), f)
    for ri, cc in enumerate((cre, cim)):
        for g in range(16):
            j, g2, gl_ = g // 2, g % 2, g % 8
            s5c[:, g2 * 64:(g2 + 1) * 64, ri, j, gl_ * 16:(gl_ + 1) * 16] = cc[:, g].transpose(0, 2, 1)
    shared["s5c"] = s5c
    dsk = np.asarray(inp["ssm_d"], f).reshape(L, 2, 128)
    bgl = np.asarray(inp["ssm_b_glu"], f).reshape(L, 2, 128)
    shared["s5db"] = np.ascontiguousarray(np.stack([dsk, bgl], 1).transpose(3, 0, 1, 2))
    rws = np.stack([np.asarray(inp["ret_gn_g"], f), np.asarray(inp["diff_subln_g"], f)], 1)
    shared["rows"] = np.ascontiguousarray(np.broadcast_to(rws[None], (128, L, 2, 384)))
    lq = np.stack([np.asarray(inp[k], f) for k in ("diff_lq1", "diff_lk1", "diff_lq2", "diff_lk2")], 1)
    shared["lqk"] = np.ascontiguousarray(np.broadcast_to(lq[None], (128, L, 4, 32)))
    cm = np.zeros((128, 2, 128), f)
    cm[:, 0] = np.eye(128, dtype=f)
    cm[:, 1] = np.triu(np.ones((128, 128), f))
    shared["cmat"] = cm
    lg = RET_LOG_GAMMA.astype(np.float64)
    pos = np.arange(128, dtype=np.float64)
    qtab = np.zeros((128, 6, 128))
    for j in range(6):
        qtab[:, j] = np.exp((pos + 1.0) * lg[j])[None, :]
    ktab = np.exp((127.0 - pos)[:, None] * lg[None, :]) * 0.125
    mask = np.zeros((128, 2, 3, 128))
    for hh in range(6):
        mask[:, hh % 2, hh // 2, :] = np.where(pos[None, :] >= pos[:, None], np.exp(-(pos[:, None] + 1.0) * lg[hh]) * 0.125, 0.0)
    cd = np.zeros((128, 384))
    for j in range(6):
        cd[:, j * 64:(j + 1) * 64] = np.exp(128.0 * lg[j])
    tp = np.stack([pos + 1.0, -(pos + 1.0)], 1)
    shared["rtab"] = np.concatenate([qtab.reshape(128, -1), ktab, cd, tp], 1).astype(f)
    shared["rmask"] = mask.astype(f)
    import ml_dtypes
    bf = ml_dtypes.bfloat16
    aug = np.zeros((128, 2, 2, SEQ), f)
    tpos_ = np.arange(SEQ)
    pa = (tpos_ // 128 * 128).astype(f)
    pb_ = (tpos_ % 128).astype(f)
    for hh in range(6):
        m = np.float64(ALIBI_SLOPES[hh])
        m1 = np.float64(np.float32(m).astype(bf)); m2 = np.float64(np.float32(m - m1).astype(bf)); m3 = np.float64(np.float32(m - m1 - m2).astype(bf))
        at, ab = hh // 3, 32 * (hh % 3)
        krows = [pa, pb_, pa, pb_, pa, pb_] + [np.full(SEQ, v, f) for v in (m1, m1, m2, m2, m3, m3)]
        qrows = [np.full(SEQ, v, f) for v in (m1, m1, m2, m2, m3, m3)] + [-pa, -pb_, -pa, -pb_, -pa, -pb_]
        for r in range(12):
            aug[ab + r, 0, at] = krows[r]
            aug[ab + r, 1, at] = qrows[r]
    shared["aug"] = aug
    return shared


def kernel(**inputs):
    cfg = inputs.pop("_cfg", {})
    x = np.asarray(inputs["x"], np.float32)
    p = np.asarray(inputs["p"], np.float32)
    shared = host_layouts(inputs)
    nl_ = cfg.get("layers", NL)
    for nm in ["w_in", "w_out", "w_up", "w_down", "w_pg", "w_pe"]:
        shared[nm] = shared[nm][:nl_]
    nc = build_program(cfg)
    in_maps = []
    for c in range(8):
        m = dict(shared)
        m["xT"] = np.ascontiguousarray(x[c].T.reshape(8, 128, SEQ).transpose(1, 0, 2))
        m["pT"] = np.ascontiguousarray(p[:, c].transpose(0, 2, 1).reshape(NL, 2, 128, SEQ).transpose(0, 2, 1, 3))
        in_maps.append(m)
    res = run_bass_kernel_spmd(nc, in_maps, core_ids=list(range(8)))
    out = np.empty((8, SEQ, D), np.float32)
    for c in range(8):
        yT = np.asarray(res.results[c]["yT"], np.float32)
        out[c] = yT.transpose(1, 0, 2).reshape(D, SEQ).T
    if cfg.get("dbg"):
        kernel.last = res.results
    return out
```

```python
import math
from contextlib import ExitStack
import numpy as np
import concourse.bass as bass
import concourse.mybir as mybir
from concourse.bass_utils import run_bass_kernel_spmd

F32 = mybir.dt.float32
BF16 = mybir.dt.bfloat16
I32 = mybir.dt.int32
ALU = mybir.AluOpType
AF = mybir.ActivationFunctionType
AX = mybir.AxisListType

NL, D, SEQ = 4, 1024, 2048
BT, NBLK = 512, 4
PROJ = 2944
DFF = 2816
NCT = 22
EPS = 1e-6
RET_LOG_GAMMA = np.log1p(-(2.0 ** (-5.0 - np.arange(6)))).astype(np.float32)
ALIBI_SLOPES = (2.0 ** (-8.0 * (np.arange(6) + 1) / 6)).astype(np.float32)
TWO_PI = 2.0 * math.pi


class Buf:
    __slots__ = ("w", "r")

    def __init__(self):
        self.w = None
        self.r = {}


class V:
    __slots__ = ("ap", "bufs")

    def __init__(self, ap, bufs):
        self.ap = ap
        self.bufs = bufs


class Ctx:
    def __init__(self, nc, es, n_dma_sems=20):
        self.nc = nc
        self.es = es
        self.engs = {"pe": nc.tensor, "act": nc.scalar, "dve": nc.vector, "pool": nc.gpsimd, "sp": nc.sync}
        self.epoch = 0
        self.sems = {}
        self.cnt = {}
        self.seen = {k: {} for k in self.engs}
        self.semobj = {}
        self.dma_pool = {"sp": [], "pool": []}
        for i in range(n_dma_sems):
            s = es.enter_context(nc.semaphore("dsem%d" % i))
            self.semobj[("dma", i)] = s
            self.dma_pool["sp" if i < n_dma_sems - 6 else "pool"].append([("dma", i), 0])
        self.dma_next = {"sp": 0, "pool": 0}
        self.pending = {k: [] for k in self.engs}
        self.log = {k: [] for k in self.engs}
        self.npe = 0
        self.marks = []

    def mark(self, name):
        self.marks.append((name, self.npe))

    def simulate(self):
        val = {}
        ptr = {k: 0 for k in self.engs}
        prog = True
        while prog:
            prog = False
            for e, lg in self.log.items():
                while ptr[e] < len(lg):
                    kind, key, v = lg[ptr[e]]
                    if kind == "wait":
                        if val.get(key, 0) < v:
                            break
                    else:
                        val[key] = val.get(key, 0) + v
                    ptr[e] += 1
                    prog = True
        stuck = {e: (ptr[e], len(lg), lg[ptr[e]], val.get(lg[ptr[e]][1], 0)) for e, lg in self.log.items() if ptr[e] < len(lg)}
        return stuck, {e: len(lg) for e, lg in self.log.items()}

    def _sem(self, e):
        key = (e, self.epoch)
        if key not in self.semobj:
            self.semobj[key] = self.es.enter_context(self.nc.semaphore("s_%s_%d" % key))
            self.cnt[key] = 0
        return key

    def new_epoch(self):
        self.epoch += 1

    def _wait(self, e, deps):
        seen = self.seen[e]
        for key, val in deps:
            if e == "pe" and key[0] == "pe":
                continue
            if seen.get(key, 0) >= val:
                continue
            self.engs[e].wait_ge(self.semobj[key], val)
            self.log[e].append(("wait", key, val))
            seen[key] = val

    def _deps(self, r, w):
        deps = []
        for v in r:
            for b in v.bufs:
                if b.w is not None:
                    deps.append(b.w)
        for v in w:
            for b in v.bufs:
                if b.w is not None:
                    deps.append(b.w)
                deps.extend(b.r.items())
        return deps

    def _mark(self, tok, r, w):
        key, val = tok
        for v in r:
            for b in v.bufs:
                if b.r.get(key, 0) < val:
                    b.r[key] = val
        for v in w:
            for b in v.bufs:
                b.w = tok
                b.r = {}

    def op(self, e, fn, r=(), w=(), signal=True, selfwait=False):
        self._wait(e, self._deps(r, w))
        if selfwait:
            key0 = self._sem(e)
            if self.cnt[key0] > 0:
                self.engs[e].wait_ge(self.semobj[key0], self.cnt[key0])
                self.log[e].append(("wait", key0, self.cnt[key0]))
        inst = fn(self.engs[e])
        if e == "pe":
            self.npe += 1
        key = self._sem(e)
        if signal:
            self.cnt[key] += 1
            inst.then_inc(self.semobj[key], 1)
            self.log[e].append(("inc", key, 1))
            tok = (key, self.cnt[key])
        else:
            tok = (key, self.cnt[key] + 1)
        self._mark(tok, r, w)
        return inst

    def dma(self, e, out, in_, r=(), w=()):
        pool = self.dma_pool[e]
        slot = pool[self.dma_next[e]]
        self.dma_next[e] = (self.dma_next[e] + 1) % len(pool)
        key, val = slot
        deps = self._deps(r, w)
        if val > 0:
            deps.append((key, val))
        self._wait(e, deps)
        inst = self.engs[e].dma_start(out=out, in_=in_)
        slot[1] = val + 16
        inst.then_inc(self.semobj[key], 16)
        self.log[e].append(("inc", key, 16))
        self._mark((key, val + 16), r, w)
        return inst

    def wait_all(self, e, views):
        deps = []
        for v in views:
            for b in v.bufs:
                if b.w is not None:
                    deps.append(b.w)
        self._wait(e, deps)


def build_program(cfg):
    nl = cfg.get("layers", NL)
    nblk = cfg.get("blocks", NBLK)
    on = lambda k: cfg.get(k, True)
    dbg = cfg.get("dbg", False)
    nc = bass.Bass("TRN2", target_bir_lowering=False)
    es = ExitStack()
    dram_in = {}

    def din(name, shape, dt=F32):
        dram_in[name] = nc.dram_tensor(name, list(shape), dt, kind="ExternalInput").ap()
        return dram_in[name]

    xT_d = din("xT", [128, 8, SEQ])
    pT_d = din("pT", [NL, 128, 2, SEQ])
    w_in_d = din("w_in", [nl, D, PROJ])
    w_out_d = din("w_out", [nl, D, D])
    w_up_d = din("w_up", [nl, D, 2 * DFF])
    w_down_d = din("w_down", [nl, DFF, D])
    w_pg_d = din("w_pg", [nl, D, D])
    w_pe_d = din("w_pe", [nl, 256, D])
    w_glu_d = din("w_glu", [NL, 128, 2, 256])
    gains_d = din("gains", [128, 3 * NL + 1, 8])
    cw_d = din("cw", [128, NL, 4, 2 * NCT])
    s5row_d = din("s5row", [NL, 128, 3, 1024])
    s5b_d = din("s5b", [NL, 128, 2, 2, 512])
    s5c_d = din("s5c", [NL, 128, 2, 8, 128])
    s5db_d = din("s5db", [128, NL, 2, 2])
    rows_d = din("rows", [128, NL, 2, 384])
    lqk_d = din("lqk", [128, NL, 4, 32])
    cmat_d = din("cmat", [128, 2, 128])
    rtab_d = din("rtab", [128, 6 * 128 + 6 + 384 + 2])
    rmask_d = din("rmask", [128, 2, 3, 128])
    abias_d = din("abias", [128, 6, 20])
    yT_d = nc.dram_tensor("yT", [128, 8, SEQ], F32, kind="ExternalOutput").ap()
    dbg_d = {}
    if dbg:
        for l in range(nl):
            dbg_d["h%d" % l] = nc.dram_tensor("dbg_h%d" % l, [128, 8, SEQ], F32, kind="ExternalOutput").ap()
        dbg_d["mix"] = nc.dram_tensor("dbg_mix", [128, 8, SEQ], BF16, kind="ExternalOutput").ap()
    wb = {}
    for nm, shp in [("w_in", [nl, D, PROJ]), ("w_out", [nl, D, D]), ("w_up", [nl, D, 2 * DFF]),
                    ("w_down", [nl, DFF, D]), ("w_pg", [nl, D, D]), ("w_pe", [nl, 256, D])]:
        wb[nm] = nc.dram_tensor("wb_" + nm, shp, BF16, kind="Internal").ap()
    wsrc = {"w_in": w_in_d, "w_out": w_out_d, "w_up": w_up_d, "w_down": w_down_d, "w_pg": w_pg_d, "w_pe": w_pe_d}

    C = Ctx(nc, es)

    def sb(name, shape, dt):
        t = es.enter_context(nc.sbuf_tensor("sb_" + name, list(shape), dt))
        return V(t[:], [Buf()])

    def sub(v, ap):
        return V(ap, v.bufs)

    h = sb("h", [128, 8, SEQ], F32)
    hblk = [[Buf() for _ in range(NBLK)]]
    hn = sb("hn", [128, 8, BT], BF16)
    mix = sb("mix", [128, 8, BT], BF16)
    gains = sb("gains", [128, 3 * NL + 1, 8], F32)
    cw = sb("cw", [128, 4, 2 * NCT], F32)
    s5db = sb("s5db", [128, NL, 2, 2], F32)
    rows = sb("rows", [128, 2, 384], F32)
    lqk = sb("lqk", [128, 4, 32], F32)
    cmatf = sb("cmatf", [128, 2, 128], F32)
    cmatb = sb("cmatb", [128, 2, 128], BF16)
    rtab = sb("rtab", [128, 6 * 128 + 6 + 384 + 2], F32)
    rmaskb = sb("rmaskb", [128, 2, 3, 128], BF16)
    abias = sb("abias", [128, 6, 20], F32)
    onesb = sb("onesb", [128, 128], BF16)
    zerob = sb("zerob", [128, 260], BF16)
    halo = sb("halo", [128, 2 * NCT, 2], F32)
    winv = sb("winv", [128, 2, 1024], BF16)
    wtf = sb("wtf", [128, 2, 8, 128], BF16)
    wlast = sb("wlast", [128, 2, 8], F32)
    bbbd = sb("bbbd", [128, 2, 1024], BF16)
    ctb = sb("ctb", [128, 2, 8, 128], BF16)
    carry = [sb("carry%d" % i, [128, 2, 4], F32) for i in range(2)]
    wglu = sb("wglu", [128, 2, 256], BF16)
    rstate = sb("rstate", [128, 384], F32)
    rstate_b = sb("rstate_b", [128, 384], BF16)
    kTc = sb("kTc", [128, 4, SEQ], BF16)
    vaug = sb("vaug", [128, 16, 6, 65], BF16)
    lam = sb("lam", [128, 4], F32)
    subg = sb("subg", [128, 384], F32)
    NSLOT = 3
    SLOT_ELEMS = 4096
    slots = [sb("slot%d" % i, [128, SLOT_ELEMS], BF16) for i in range(NSLOT)]
    SCR_BYTES = 30 * 1024
    scr_t = es.enter_context(nc.sbuf_tensor("scr", [128, SCR_BYTES // 2], BF16))
    CELL = 1024
    scr_cells = [Buf() for _ in range(SCR_BYTES // CELL)]

    def carve(off, shape, dt):
        esz = 4 if dt in (F32, I32) else 2
        n = int(np.prod(shape[1:]))
        nb = n * esz
        assert off % 4 == 0 and off + nb <= SCR_BYTES, (off, nb)
        ap = scr_t[0:shape[0], off // 2:(off + nb) // 2]
        if esz == 4:
            ap = ap.bitcast(dt)
        if len(shape) == 3:
            ap = ap.rearrange("p (a b) -> p a b", a=shape[1])
        elif len(shape) == 4:
            ap = ap.rearrange("p (a b c) -> p a b c", a=shape[1], b=shape[2])
        return V(ap, scr_cells[off // CELL:(off + nb + CELL - 1) // CELL])

    ps = []
    for i in range(7):
        t = es.enter_context(nc.psum_tensor("ps%d" % i, [128, 512], F32))
        ps.append(V(t[:], [Buf()]))
    pst_t = es.enter_context(nc.psum_tensor("pst", [128, 1024], BF16))
    pst = V(pst_t[:], [Buf()])

    cast_sems = [es.enter_context(nc.semaphore("cast%d" % l)) for l in range(nl)]
    cast_tot = [0] * nl
    for l in range(nl):
        for nm in ["w_in", "w_out", "w_up", "w_down", "w_pg", "w_pe"]:
            K, N = wsrc[nm].shape[1], wsrc[nm].shape[2]
            for r0 in range(0, K, 512):
                r1 = min(K, r0 + 512)
                for c0 in range(0, N, 2048):
                    c1 = min(N, c0 + 2048)
                    nc.gpsimd.dma_start(out=wb[nm][l, r0:r1, c0:c1], in_=wsrc[nm][l, r0:r1, c0:c1]).then_inc(cast_sems[l], 16)
                    cast_tot[l] += 16

    def load(dst, src):
        C.dma("sp", dst.ap, src, w=[dst])

    load(gains, gains_d)
    load(s5db, s5db_d)
    load(cmatf, cmat_d)
    load(rtab, rtab_d)
    for kt in range(8):
        C.dma("sp", h.ap[:, kt, :], xT_d[:, kt, :], w=[h])
    C.op("dve", lambda e: e.tensor_copy(out=cmatb.ap, in_=cmatf.ap), r=[cmatf], w=[cmatb])
    C.op("dve", lambda e: e.memset(onesb.ap, 1.0), w=[onesb])
    C.op("dve", lambda e: e.memset(zerob.ap, 0.0), w=[zerob])
    C.op("dve", lambda e: e.memset(vaug.ap, 1.0), w=[vaug])
    identb = sub(cmatb, cmatb.ap[:, 0, :])
    trib = sub(cmatb, cmatb.ap[:, 1, :])
    identf = sub(cmatf, cmatf.ap[:, 0, :])
    load(abias, abias_d)
    tmp = carve(16384, [128, 2, 3, 128], F32)
    C.dma("sp", tmp.ap, rmask_d, w=[tmp])
    C.op("dve", lambda e: e.tensor_copy(out=rmaskb.ap, in_=tmp.ap), r=[tmp], w=[rmaskb])
    o0 = 0
    ret_qtab = sub(rtab, rtab.ap[:, o0:o0 + 768].rearrange("p (a b) -> p a b", a=6)); o0 += 768
    ret_ktab = sub(rtab, rtab.ap[:, o0:o0 + 6]); o0 += 6
    ret_mask = rmaskb
    ret_cd = sub(rtab, rtab.ap[:, o0:o0 + 384]); o0 += 384
    tpos = sub(rtab, rtab.ap[:, o0:o0 + 1])
    tneg = sub(rtab, rtab.ap[:, o0 + 1:o0 + 2])

    slab_state = {"i": 0, "queue": [], "issued": 0}

    def slab_specs(l):
        sp = []
        if on("s5"):
            sp.append(("w_in", l, 8, 0, 256))
        if on("ret"):
            for c0 in (256, 640, 1024, 1408):
                sp.append(("w_in", l, 8, c0, c0 + 384))
        if on("diff"):
            for c0 in (1792, 2176, 2560):
                sp.append(("w_in", l, 8, c0, c0 + 384))
        for c0 in (0, 512):
            sp.append(("w_out", l, 8, c0, c0 + 512))
        if on("ffn"):
            for g in range(NCT // 2):
                sp.append(("w_up", l, 8, g, None))
            for dp in range(4):
                for hf in range(2):
                    sp.append(("w_down", l, 11, dp, hf))
        if on("ple"):
            for c0 in (0, 512):
                sp.append(("w_pg", l, 8, c0, c0 + 512))
                sp.append(("w_pe", l, 2, c0, c0 + 512))
        return sp

    all_specs = []
    for l in range(nl):
        for b in range(nblk):
            all_specs.extend(slab_specs(l))
    cast_waited = set()

    def issue_slab(idx):
        nm, l, kt, a, b = all_specs[idx]
        slot = slots[idx % NSLOT]
        if l not in cast_waited:
            nc.sync.wait_ge(cast_sems[l], cast_tot[l])
            cast_waited.add(l)
        if nm == "w_up":
            g = a
            dst = slot.ap[:, 0:8 * 512].rearrange("p (k m c) -> p k m c", k=8, m=2)
            for m in range(2):
                c0 = m * DFF + g * 256
                src = wb[nm][l, :, c0:c0 + 256].rearrange("(k p) n -> p k n", p=128)
                C.dma("sp", dst[:, :, m, :], src, w=[slot])
        elif nm == "w_down":
            dp, hf = a, b
            dst = slot.ap[:, 0:11 * 256].rearrange("p (k c) -> p k c", k=11)
            src = wb[nm][l, hf * 1408:(hf + 1) * 1408, dp * 256:(dp + 1) * 256].rearrange("(k p) n -> p k n", p=128)
            C.dma("sp", dst, src, w=[slot])
        else:
            n = b - a
            dst = slot.ap[:, 0:kt * n].rearrange("p (k c) -> p k c", k=kt)
            src = wb[nm][l, :, a:b].rearrange("(k p) n -> p k n", p=128)
            C.dma("sp", dst, src, w=[slot])

    def next_slab(expect):
        i = slab_state["i"]
        assert all_specs[i][0] == expect, (all_specs[i], expect)
        assert i < slab_state["issued"], "slab not issued (too many held)"
        slab_state["i"] = i + 1
        nm, l, kt, a, b = all_specs[i]
        slot = slots[i % NSLOT]
        if nm == "w_up":
            return sub(slot, slot.ap[:, 0:8 * 512].rearrange("p (k m c) -> p k m c", k=8, m=2))
        if nm == "w_down":
            return sub(slot, slot.ap[:, 0:11 * 256].rearrange("p (k c) -> p k c", k=11))
        n = b - a
        return sub(slot, slot.ap[:, 0:kt * n].rearrange("p (k c) -> p k c", k=kt))

    def slab_done():
        if slab_state["issued"] < len(all_specs):
            issue_slab(slab_state["issued"])
            slab_state["issued"] += 1

    for _ in range(min(NSLOT, len(all_specs))):
        issue_slab(slab_state["issued"])
        slab_state["issued"] += 1

    def mm(out_v, out_ap, lhsT_v, lhsT_ap, rhs_v, rhs_ap, start, stop, signal=None, selfwait=False):
        if signal is None:
            signal = True
        C.op("pe", lambda e: e.matmul(out_ap, lhsT_ap, rhs_ap, start=start, stop=stop),
             r=[lhsT_v, rhs_v], w=[out_v], signal=signal, selfwait=selfwait)

    def hsl(b):
        return slice(b * BT, (b + 1) * BT)

    def rmsnorm(b, gidx, dst, scrA, scrB):
        pss = ps[6]
        sq = [carve(scrA, [128, BT], BF16), carve(scrA + 1024, [128, BT], BF16)]
        for kt in range(8):
            s = sq[kt % 2]
            C.op("act", lambda e, s=s, kt=kt: e.activation(out=s.ap, in_=h.ap[:, kt, hsl(b)], func=AF.Square), r=[h], w=[s])
            mm(pss, pss.ap, onesb, onesb.ap, s, s.ap, kt == 0, kt == 7)
        rs = carve(scrB, [128, BT], F32)
        C.op("act", lambda e: e.activation(out=rs.ap, in_=pss.ap, func=AF.Ln, scale=1.0 / D, bias=EPS), r=[pss], w=[rs])
        C.op("act", lambda e: e.activation(out=rs.ap, in_=rs.ap, func=AF.Exp, scale=-0.5), r=[rs], w=[rs])
        for kt in range(8):
            C.op("dve", lambda e, kt=kt: e.scalar_tensor_tensor(out=dst.ap[:, kt, :], in0=h.ap[:, kt, hsl(b)],
                                                               scalar=gains.ap[:, gidx, kt:kt + 1], in1=rs.ap,
                                                               op0=ALU.mult, op1=ALU.mult), r=[h, gains, rs], w=[dst])

    def proj_fm(w, cols, evac, width=128):
        for j, c0 in enumerate(cols):
            p = ps[4 + (j % 2)]
            for kt in range(8):
                mm(p, p.ap[0:width, :], w, w.ap[:, kt, c0:c0 + width], hn, hn.ap[:, kt, :], kt == 0, kt == 7)
            evac(j, p)

    def proj_tm(w, ncols, evac):
        for c in range(4):
            p = ps[4 + (c % 2)]
            for kt in range(8):
                mm(p, p.ap[:, 0:ncols], hn, hn.ap[:, kt, c * 128:(c + 1) * 128], w, w.ap[:, kt, 0:ncols], kt == 0, kt == 7)
            evac(c, p)

    def h_add(b, dt, p):
        C.op("dve", lambda e: e.tensor_tensor(out=h.ap[:, dt, hsl(b)], in0=p.ap, in1=h.ap[:, dt, hsl(b)], op=ALU.add), r=[p, h], w=[h])

    def s5_setup(l):
        CTf = carve(0, [128, 2, 8, 128], F32)
        C.dma("sp", CTf.ap, s5c_d[l], w=[CTf])
        C.op("dve", lambda e: e.tensor_copy(out=ctb.ap[:, 0], in_=CTf.ap[:, 0]), r=[CTf], w=[ctb])
        C.op("dve", lambda e: e.tensor_scalar(out=ctb.ap[:, 1], in0=CTf.ap[:, 1], scalar1=-1.0, scalar2=None, op0=ALU.mult), r=[CTf], w=[ctb])
        WG = carve(8192, [128, 2, 256], F32)
        C.dma("sp", WG.ap, w_glu_d[l], w=[WG])
        C.op("dve", lambda e: e.tensor_copy(out=wglu.ap, in_=WG.ap), r=[WG], w=[wglu])
        for i in range(2):
            C.op("dve", lambda e, i=i: e.memset(carry[i].ap, 0.0), w=[carry[i]])
        for hf in range(2):
            s5_setup_half(l, hf)

    def s5_setup_half(l, hf):
        HS = slice(hf * 512, (hf + 1) * 512)
        T = [carve(i * 2048, [128, 512], F32) for i in range(9)]
        LR, LI, A, B, Cc, Dd, E, Fv, G = T
        TI = carve(9 * 2048, [128, 512], I32)
        MK = carve(10 * 2048, [128, 512], F32)
        C.dma("sp", LR.ap, s5row_d[l, :, 0, HS], w=[LR])
        C.dma("sp", LI.ap, s5row_d[l, :, 1, HS], w=[LI])
        C.dma("sp", A.ap, s5row_d[l, :, 2, HS], w=[A])

        def tt(eng, o, a, b_, op):
            C.op(eng, lambda e: e.tensor_tensor(out=o.ap, in0=a.ap, in1=b_.ap, op=op), r=[a, b_], w=[o])

        def act(o, a, func, **kw):
            C.op("act", lambda e: e.activation(out=o.ap, in_=a.ap, func=func, **kw), r=[a], w=[o])

        def sincos(phi, s_out, c_out):
            C.op("dve", lambda e: e.tensor_scalar(out=TI.ap, in0=phi.ap, scalar1=1.0 / TWO_PI, scalar2=None, op0=ALU.mult), r=[phi], w=[TI])
            C.op("dve", lambda e: e.tensor_copy(out=MK.ap, in_=TI.ap), r=[TI], w=[MK])
            C.op("dve", lambda e: e.scalar_tensor_tensor(out=s_out.ap, in0=MK.ap, scalar=-TWO_PI, in1=phi.ap, op0=ALU.mult, op1=ALU.add), r=[MK, phi], w=[s_out])
            C.op("dve", lambda e: e.tensor_scalar(out=MK.ap, in0=s_out.ap, scalar1=math.pi, scalar2=-TWO_PI, op0=ALU.is_gt, op1=ALU.mult), r=[s_out], w=[MK])
            tt("dve", s_out, s_out, MK, ALU.add)
            C.op("dve", lambda e: e.tensor_scalar(out=MK.ap, in0=s_out.ap, scalar1=-math.pi, scalar2=TWO_PI, op0=ALU.is_lt, op1=ALU.mult), r=[s_out], w=[MK])
            tt("dve", s_out, s_out, MK, ALU.add)
            C.op("dve", lambda e: e.tensor_scalar(out=MK.ap, in0=s_out.ap, scalar1=math.pi / 2, scalar2=-TWO_PI, op0=ALU.is_gt, op1=ALU.mult), r=[s_out], w=[MK])
            C.op("dve", lambda e: e.scalar_tensor_tensor(out=c_out.ap, in0=s_out.ap, scalar=math.pi / 2, in1=MK.ap, op0=ALU.add, op1=ALU.add), r=[s_out, MK], w=[c_out])
            act(s_out, s_out, AF.Sin)
            act(c_out, c_out, AF.Sin)

        act(A, A, AF.Exp)
        tt("dve", B, LR, A, ALU.mult)
        tt("dve", Cc, LI, A, ALU.mult)
        sincos(Cc, A, Dd)
        act(E, B, AF.Exp)
        tt("dve", Dd, E, Dd, ALU.mult)
        tt("dve", A, E, A, ALU.mult)
        C.op("dve", lambda e: e.tensor_scalar(out=Dd.ap, in0=Dd.ap, scalar1=-1.0, scalar2=None, op0=ALU.add), r=[Dd], w=[Dd])
        tt("dve", E, LR, LR, ALU.mult)
        tt("dve", Fv, LI, LI, ALU.mult)
        tt("dve", E, E, Fv, ALU.add)
        C.op("dve", lambda e: e.reciprocal(out=E.ap, in_=E.ap), r=[E], w=[E])
        tt("dve", Fv, Dd, LR, ALU.mult)
        tt("dve", G, A, LI, ALU.mult)
        tt("dve", Fv, Fv, G, ALU.add)
        tt("dve", Fv, Fv, E, ALU.mult)
        tt("dve", G, A, LR, ALU.mult)
        tt("dve", A, Dd, LI, ALU.mult)
        tt("dve", G, G, A, ALU.subtract)
        tt("dve", G, G, E, ALU.mult)
        BB = carve(0, [128, 2, 512], F32)
        C.dma("sp", BB.ap, s5b_d[l, :, :, hf, :], w=[BB])
        t1, t2 = A, Dd
        C.op("dve", lambda e: e.tensor_tensor(out=t1.ap, in0=Fv.ap, in1=BB.ap[:, 0, :], op=ALU.mult), r=[Fv, BB], w=[t1])
        C.op("dve", lambda e: e.tensor_tensor(out=t2.ap, in0=G.ap, in1=BB.ap[:, 1, :], op=ALU.mult), r=[G, BB], w=[t2])
        C.op("dve", lambda e: e.tensor_tensor(out=bbbd.ap[:, hf, 0:512], in0=t1.ap, in1=t2.ap, op=ALU.subtract), r=[t1, t2], w=[bbbd])
        C.op("dve", lambda e: e.tensor_tensor(out=t1.ap, in0=Fv.ap, in1=BB.ap[:, 1, :], op=ALU.mult), r=[Fv, BB], w=[t1])
        C.op("dve", lambda e: e.tensor_tensor(out=t2.ap, in0=G.ap, in1=BB.ap[:, 0, :], op=ALU.mult), r=[G, BB], w=[t2])
        C.op("dve", lambda e: e.tensor_tensor(out=bbbd.ap[:, hf, 512:1024], in0=t1.ap, in1=t2.ap, op=ALU.add), r=[t1, t2], w=[bbbd])
        C.op("dve", lambda e: e.tensor_scalar(out=A.ap, in0=Cc.ap, scalar1=tpos.ap, scalar2=None, op0=ALU.mult), r=[Cc, rtab], w=[A])
        sincos(A, Dd, E)
        C.op("act", lambda e: e.activation(out=Fv.ap, in_=B.ap, func=AF.Exp, scale=tpos.ap), r=[B, rtab], w=[Fv])
        C.op("act", lambda e: e.activation(out=G.ap, in_=B.ap, func=AF.Exp, scale=tneg.ap), r=[B, rtab], w=[G])
        C.op("dve", lambda e: e.tensor_tensor(out=winv.ap[:, 0, HS], in0=G.ap, in1=E.ap, op=ALU.mult), r=[G, E], w=[winv])
        C.op("dve", lambda e: e.scalar_tensor_tensor(out=winv.ap[:, 1, HS], in0=G.ap, scalar=-1.0, in1=Dd.ap, op0=ALU.mult, op1=ALU.mult), r=[G, Dd], w=[winv])
        tt("dve", E, Fv, E, ALU.mult)
        tt("dve", Dd, Fv, Dd, ALU.mult)
        for ri, src in enumerate((E, Dd)):
            for jl in range(4):
                j = 4 * hf + jl
                p = ps[jl % 2]
                C.op("pe", lambda e, p=p, src=src, jl=jl: e.transpose(p.ap[:, 0:128], src.ap[:, jl * 128:(jl + 1) * 128], identf.ap), r=[src, cmatf], w=[p])
                C.op("act", lambda e, p=p, ri=ri, j=j: e.activation(out=wtf.ap[:, ri, j, :], in_=p.ap[:, 0:128], func=AF.Copy), r=[p], w=[wtf])
                C.op("dve", lambda e, p=p, ri=ri, j=j: e.tensor_copy(out=wlast.ap[:, ri, j:j + 1], in_=p.ap[:, 127:128]), r=[p], w=[wlast])

    def s5_block(l, b):
        w = next_slab("w_in")
        uT = carve(0, [128, 2, BT], BF16)
        Tm = [carve(2048 + k * 1024, [128, BT], BF16) for k in range(4)]
        Z = carve(6144, [128, 2, BT], BF16)
        SC = carve(8192, [128, 2, 4, 128], F32)
        U = [carve(12288 + k * 1024, [128, 4, 128], BF16) for k in range(4)]
        CT4 = [carve(16384 + k * 1024, [128, 4], F32) for k in range(4)]
        X = carve(20480, [128, 2, 4, 128], BF16)
        YV = carve(22528, [128, BT], F32)
        G1 = carve(24576, [128, BT], F32)
        ZB = carve(26624, [128, 2, BT], BF16)

        def ev(j, p):
            C.op("act", lambda e: e.activation(out=uT.ap[:, j, :], in_=p.ap, func=AF.Copy), r=[p], w=[uT])
        proj_fm(w, [0, 128], ev)
        slab_done()
        for c in range(4):
            cs = slice(c * 128, (c + 1) * 128)
            for i in range(2):
                mm(ps[0], ps[0].ap, uT, uT.ap[:, i, cs], bbbd, bbbd.ap[:, i, 0:512], True, True)
                mm(ps[1], ps[1].ap, uT, uT.ap[:, i, cs], bbbd, bbbd.ap[:, i, 512:1024], True, True)
                isl = slice(i * 512, (i + 1) * 512)
                for k, (pp, ri) in enumerate(((0, 0), (1, 1), (0, 1), (1, 0))):
                    C.op("dve", lambda e, k=k, pp=pp, ri=ri: e.tensor_tensor(out=Tm[k].ap, in0=ps[pp].ap, in1=winv.ap[:, ri, isl], op=ALU.mult),
                         r=[ps[pp], winv], w=[Tm[k]])
                C.op("pool", lambda e: e.tensor_tensor(out=Z.ap[:, 0, :], in0=Tm[0].ap, in1=Tm[1].ap, op=ALU.subtract), r=[Tm[0], Tm[1]], w=[Z])
                C.op("pool", lambda e: e.tensor_tensor(out=Z.ap[:, 1, :], in0=Tm[2].ap, in1=Tm[3].ap, op=ALU.add), r=[Tm[2], Tm[3]], w=[Z])
                for ri in range(2):
                    for jl in range(4):
                        mm(ps[2 + ri], ps[2 + ri].ap[:, jl * 128:(jl + 1) * 128], Z, Z.ap[:, ri, jl * 128:(jl + 1) * 128], trib, trib.ap, True, True)
                for ri in range(2):
                    C.op("dve", lambda e, ri=ri: e.tensor_tensor(out=SC.ap[:, ri], in0=ps[2 + ri].ap.rearrange("p (a b) -> p a b", a=4),
                                                                 in1=carry[i].ap[:, ri, :].unsqueeze(2).broadcast_to([128, 4, 128]), op=ALU.add),
                         r=[ps[2 + ri], carry[i]], w=[SC])
                wr = wtf.ap[:, 0, 4 * i:4 * i + 4, :]
                wi = wtf.ap[:, 1, 4 * i:4 * i + 4, :]
                C.op("dve", lambda e: e.tensor_tensor(out=U[0].ap, in0=SC.ap[:, 0], in1=wr, op=ALU.mult), r=[SC, wtf], w=[U[0]])
                C.op("pool", lambda e: e.tensor_tensor(out=U[1].ap, in0=SC.ap[:, 1], in1=wi, op=ALU.mult), r=[SC, wtf], w=[U[1]])
                C.op("dve", lambda e: e.tensor_tensor(out=U[2].ap, in0=SC.ap[:, 0], in1=wi, op=ALU.mult), r=[SC, wtf], w=[U[2]])
                C.op("pool", lambda e: e.tensor_tensor(out=U[3].ap, in0=SC.ap[:, 1], in1=wr, op=ALU.mult), r=[SC, wtf], w=[U[3]])
                C.op("pool", lambda e: e.tensor_tensor(out=X.ap[:, 0], in0=U[0].ap, in1=U[1].ap, op=ALU.subtract), r=[U[0], U[1]], w=[X])
                C.op("pool", lambda e: e.tensor_tensor(out=X.ap[:, 1], in0=U[2].ap, in1=U[3].ap, op=ALU.add), r=[U[2], U[3]], w=[X])
                for k, (ra, rb) in enumerate(((0, 0), (1, 1), (0, 1), (1, 0))):
                    C.op("dve", lambda e, k=k, ra=ra, rb=rb: e.tensor_tensor(out=CT4[k].ap, in0=SC.ap[:, ra, :, 127], in1=wlast.ap[:, rb, 4 * i:4 * i + 4], op=ALU.mult), r=[SC, wlast], w=[CT4[k]])
                C.op("dve", lambda e: e.tensor_tensor(out=carry[i].ap[:, 0, :], in0=CT4[0].ap, in1=CT4[1].ap, op=ALU.subtract), r=[CT4[0], CT4[1]], w=[carry[i]])
                C.op("dve", lambda e: e.tensor_tensor(out=carry[i].ap[:, 1, :], in0=CT4[2].ap, in1=CT4[3].ap, op=ALU.add), r=[CT4[2], CT4[3]], w=[carry[i]])
                py = ps[4 + i]
                for jl in range(4):
                    for ri in range(2):
                        mm(py, py.ap[:, cs], ctb, ctb.ap[:, ri, 4 * i + jl, :], X, X.ap[:, ri, jl, :], jl == 0 and ri == 0, jl == 3 and ri == 1)
        for i in range(2):
            py = ps[4 + i]
            C.op("dve", lambda e: e.scalar_tensor_tensor(out=YV.ap, in0=uT.ap[:, i, :], scalar=s5db.ap[:, l, 0, i:i + 1], in1=py.ap, op0=ALU.mult, op1=ALU.add), r=[uT, s5db, py], w=[YV])
            C.op("act", lambda e: e.activation(out=G1.ap, in_=YV.ap, func=AF.Square), r=[YV], w=[G1])
            C.op("dve", lambda e: e.tensor_scalar(out=G1.ap, in0=G1.ap, scalar1=0.044715, scalar2=1.0, op0=ALU.mult, op1=ALU.add), r=[G1], w=[G1])
            C.op("dve", lambda e: e.tensor_tensor(out=G1.ap, in0=G1.ap, in1=YV.ap, op=ALU.mult), r=[G1, YV], w=[G1])
            C.op("act", lambda e: e.activation(out=G1.ap, in_=G1.ap, func=AF.Sigmoid, scale=2.0 * 0.7978845608028654), r=[G1], w=[G1])
            C.op("dve", lambda e: e.tensor_tensor(out=ZB.ap[:, i, :], in0=YV.ap, in1=G1.ap, op=ALU.mult), r=[YV, G1], w=[ZB])
        for io in range(2):
            pg = ps[4 + io]
            for i in range(2):
                mm(pg, pg.ap, wglu, wglu.ap[:, i, io * 128:(io + 1) * 128], ZB, ZB.ap[:, i, :], i == 0, i == 1)
            C.op("act", lambda e: e.activation(out=G1.ap, in_=pg.ap, func=AF.Sigmoid, bias=s5db.ap[:, l, 1, io:io + 1]), r=[pg, s5db], w=[G1])
            C.op("dve", lambda e: e.tensor_tensor(out=mix.ap[:, io, :], in0=ZB.ap[:, io, :], in1=G1.ap, op=ALU.mult), r=[ZB, G1], w=[mix])

    def ret_block(l, b):
        qd = carve(0, [128, 6, BT], BF16)
        kT = carve(6144, [128, 6, BT], BF16)
        kdec = carve(12288, [128, 4, 384], BF16)
        vt = carve(15360, [128, 4, 384], BF16)
        gg = carve(18432, [128, 4, 384], BF16)
        inn = carve(21504, [128, 2, 3, 128], BF16)
        OC = carve(23040, [128, 6, 64], F32)
        SQ = carve(24576, [128, 6, 64], F32)
        GT = carve(26112, [128, 384], F32)
        ROUT = carve(27648, [128, 384], BF16)
        M1 = carve(28672, [128, 6], F32)
        M2 = carve(29696, [128, 6], F32)
        gn = rows.ap[:, 0, :]
        wq = next_slab("w_in")

        def evq(j, p):
            C.op("dve", lambda e: e.tensor_tensor(out=qd.ap[0:64, j, :].rearrange("p (c t) -> p c t", c=4), in0=p.ap[0:64, :].rearrange("p (c t) -> p c t", c=4),
                                                  in1=ret_qtab.ap[0:64, j, :].unsqueeze(1).broadcast_to([64, 4, 128]), op=ALU.mult), r=[p, rtab], w=[qd])
        proj_fm(wq, [0, 64, 128, 192, 256, 320], evq, width=64)
        slab_done()
        wk = next_slab("w_in")

        def evk(j, p):
            C.op("act", lambda e: e.activation(out=kT.ap[0:64, j, :], in_=p.ap[0:64, :], func=AF.Copy), r=[p], w=[kT])
        proj_fm(wk, [0, 64, 128, 192, 256, 320], evk, width=64)

        def evkt(c, p):
            C.op("dve", lambda e: e.tensor_tensor(out=kdec.ap[:, c, :].rearrange("p (h d) -> p h d", h=6), in0=p.ap[:, 0:384].rearrange("p (h d) -> p h d", h=6),
                                                  in1=ret_ktab.ap.unsqueeze(2).broadcast_to([128, 6, 64]), op=ALU.mult), r=[p, rtab], w=[kdec])
        proj_tm(wk, 384, evkt)
        slab_done()
        wv = next_slab("w_in")

        def evv(c, p):
            C.op("act", lambda e: e.activation(out=vt.ap[:, c, :], in_=p.ap[:, 0:384], func=AF.Copy), r=[p], w=[vt])
        proj_tm(wv, 384, evv)
        slab_done()
        wg = next_slab("w_in")

        def evg(c, p):
            C.op("act", lambda e: e.activation(out=GT.ap, in_=p.ap[:, 0:384], func=AF.Silu), r=[p], w=[GT])
            C.op("dve", lambda e: e.tensor_tensor(out=gg.ap[:, c, :], in0=GT.ap, in1=gn, op=ALU.mult), r=[GT, rows], w=[gg])
        proj_tm(wg, 384, evg)
        slab_done()
        if b == 0:
            C.op("dve", lambda e: e.memset(rstate.ap, 0.0), w=[rstate])
            C.op("dve", lambda e: e.memset(rstate_b.ap, 0.0), w=[rstate_b])
        for c in range(4):
            cs = slice(c * 128, (c + 1) * 128)
            for hh in range(6):
                pp = ps[hh % 2]
                o = (hh // 2) * 128
                mm(pp, pp.ap[:, o:o + 128], kT, kT.ap[0:64, hh, cs], qd, qd.ap[0:64, hh, cs], True, True)
            for par in range(2):
                C.op("dve", lambda e, par=par: e.tensor_tensor(out=inn.ap[:, par], in0=ps[par].ap[:, 0:384].rearrange("p (a b) -> p a b", a=3), in1=ret_mask.ap[:, par], op=ALU.mult), r=[ps[par], rmaskb], w=[inn])
            po = ps[2]
            for hh in range(6):
                hs = slice(hh * 64, (hh + 1) * 64)
                mm(po, po.ap[:, hs], inn, inn.ap[:, hh % 2, hh // 2, :], vt, vt.ap[:, c, hs], True, False, signal=False)
                mm(po, po.ap[:, hs], qd, qd.ap[0:64, hh, cs], rstate_b, rstate_b.ap[0:64, hs], False, True, signal=(hh == 5))
            pS = ps[3]
            for hh in range(6):
                hs = slice(hh * 64, (hh + 1) * 64)
                mm(pS, pS.ap[0:64, hs], kdec, kdec.ap[:, c, hs], vt, vt.ap[:, c, hs], True, True, signal=(hh == 5))
            C.op("dve", lambda e: e.tensor_tensor(out=rstate.ap[0:64], in0=rstate.ap[0:64], in1=ret_cd.ap[0:64], op=ALU.mult), r=[rstate, rtab], w=[rstate])
            C.op("dve", lambda e: e.tensor_tensor(out=rstate.ap[0:64], in0=pS.ap[0:64, 0:384], in1=rstate.ap[0:64], op=ALU.add), r=[pS, rstate], w=[rstate])
            C.op("dve", lambda e: e.tensor_copy(out=rstate_b.ap[0:64], in_=rstate.ap[0:64]), r=[rstate], w=[rstate_b])
            po3 = po.ap[:, 0:384].rearrange("p (h d) -> p h d", h=6)
            C.op("dve", lambda e: e.tensor_reduce(out=M1.ap, in_=po3, axis=AX.X, op=ALU.add), r=[po], w=[M1])
            C.op("dve", lambda e: e.tensor_scalar(out=M1.ap, in0=M1.ap, scalar1=-1.0 / 64, scalar2=None, op0=ALU.mult), r=[M1], w=[M1])
            C.op("dve", lambda e: e.tensor_tensor(out=OC.ap, in0=po3, in1=M1.ap.unsqueeze(2).broadcast_to([128, 6, 64]), op=ALU.add), r=[po, M1], w=[OC])
            C.op("act", lambda e: e.activation(out=SQ.ap, in_=OC.ap, func=AF.Square), r=[OC], w=[SQ])
            C.op("dve", lambda e: e.tensor_reduce(out=M2.ap, in_=SQ.ap, axis=AX.X, op=ALU.add), r=[SQ], w=[M2])
            C.op("dve", lambda e: e.tensor_scalar(out=M2.ap, in0=M2.ap, scalar1=1.0 / 64, scalar2=EPS, op0=ALU.mult, op1=ALU.add), r=[M2], w=[M2])
            C.op("act", lambda e: e.activation(out=M2.ap, in_=M2.ap, func=AF.Sqrt), r=[M2], w=[M2])
            C.op("dve", lambda e: e.reciprocal(out=M2.ap, in_=M2.ap), r=[M2], w=[M2])
            C.op("dve", lambda e: e.tensor_tensor(out=OC.ap, in0=OC.ap, in1=M2.ap.unsqueeze(2).broadcast_to([128, 6, 64]), op=ALU.mult), r=[OC, M2], w=[OC])
            C.op("dve", lambda e: e.tensor_tensor(out=ROUT.ap, in0=OC.ap.rearrange("p h d -> p (h d)"), in1=gg.ap[:, c, :], op=ALU.mult), r=[OC, gg], w=[ROUT])
            for j in range(3):
                C.op("pe", lambda e, j=j: e.transpose(pst.ap[:, j * 128:(j + 1) * 128], ROUT.ap[:, j * 128:(j + 1) * 128], identb.ap), r=[ROUT, cmatb], w=[pst], signal=(j == 2))
            C.op("act", lambda e: e.activation(out=mix.ap[:, 2:5, cs], in_=pst.ap[:, 0:384].rearrange("p (a b) -> p a b", a=3), func=AF.Copy), r=[pst], w=[mix])

    def diff_setup(l):
        lam_init = 0.8 - 0.6 * math.exp(-0.3 * l)
        T1 = carve(0, [128, 2, 32], F32)
        C.op("dve", lambda e: e.tensor_tensor(out=T1.ap[:, 0, :], in0=lqk.ap[:, 0, :], in1=lqk.ap[:, 1, :], op=ALU.mult), r=[lqk], w=[T1])
        C.op("dve", lambda e: e.tensor_tensor(out=T1.ap[:, 1, :], in0=lqk.ap[:, 2, :], in1=lqk.ap[:, 3, :], op=ALU.mult), r=[lqk], w=[T1])
        C.op("dve", lambda e: e.tensor_reduce(out=lam.ap[:, 0:2], in_=T1.ap, axis=AX.X, op=ALU.add), r=[T1], w=[lam])
        C.op("act", lambda e: e.activation(out=lam.ap[:, 0:2], in_=lam.ap[:, 0:2], func=AF.Exp), r=[lam], w=[lam])
        C.op("dve", lambda e: e.tensor_tensor(out=lam.ap[:, 2:3], in0=lam.ap[:, 0:1], in1=lam.ap[:, 1:2], op=ALU.subtract), r=[lam], w=[lam])
        C.op("dve", lambda e: e.tensor_scalar(out=lam.ap[:, 3:4], in0=lam.ap[:, 2:3], scalar1=lam_init, scalar2=-1.0, op0=ALU.add, op1=ALU.mult), r=[lam], w=[lam])
        C.op("dve", lambda e: e.tensor_scalar(out=subg.ap, in0=rows.ap[:, 1, :], scalar1=1.0 - lam_init, scalar2=None, op0=ALU.mult), r=[rows], w=[subg])

    def diff_block(l, b):
        qT = carve(0, [128, 4, BT], BF16)
        P = [carve(4096 + k * 2048, [128, 2, BT], BF16) for k in range(2)]
        dtok = carve(8192, [128, 4, 384], BF16)
        UU = carve(11264, [128, 4, 64], F32)
        TT = carve(12288, [128, 4, 64], F32)
        R0 = carve(13312, [128, 4], F32)
        R1 = carve(14336, [128, 4], F32)
        wq = next_slab("w_in")

        def evq(j, p):
            C.op("act", lambda e: e.activation(out=qT.ap[0:96, j, :], in_=p.ap[0:96, :], func=AF.Copy, scale=32.0 ** -0.5), r=[p], w=[qT])
        proj_fm(wq, [0, 96, 192, 288], evq, width=96)
        slab_done()
        wk = next_slab("w_in")

        def evk(j, p):
            C.op("act", lambda e: e.activation(out=kTc.ap[0:96, j, hsl(b)], in_=p.ap[0:96, :], func=AF.Copy), r=[p], w=[kTc])
        proj_fm(wk, [0, 96, 192, 288], evk, width=96)
        slab_done()
        wv = next_slab("w_in")

        def evv(c, p):
            C.op("act", lambda e: e.activation(out=vaug.ap[:, 4 * b + c, :, 0:64], in_=p.ap[:, 0:384].rearrange("p (h d) -> p h d", h=6), func=AF.Copy), r=[p], w=[vaug])
        proj_tm(wv, 384, evv)
        slab_done()
        it = 0
        for hh in range(6):
            W = (128, 256, 512, 512, 512, 512)[hh]
            acc = [ps[2], ps[3]]
            for m in range(2):
                mm(acc[m], acc[m].ap[:, 0:260], zerob, zerob.ap[:, 0:128], zerob, zerob.ap[:, 0:260], True, False)
            nkt = 4 * b + 4
            for kt in range(nkt):
                j = kt - 4 * b
                q0 = max(j, 0) * 128
                ks = slice(kt * 128, (kt + 1) * 128)
                Pk = P[it % 2]
                pss = [ps[0], ps[1]] if it % 2 == 0 else [ps[4], ps[5]]
                it += 1
                for m in range(2):
                    mi = 2 * hh + m
                    tj, bp = mi // 3, 32 * (mi % 3)
                    mm(pss[m], pss[m].ap[:, q0:BT], kTc, kTc.ap[bp:bp + 32, tj, ks], qT, qT.ap[bp:bp + 32, tj, q0:BT], True, True)
                for m in range(2):
                    for s0 in range(0, BT, W):
                        lo = max(s0, q0)
                        if lo >= s0 + W:
                            continue
                        oi = kt - 4 * b - s0 // 128 + 16
                        C.op("act", lambda e, m=m, lo=lo, s0=s0, oi=oi: e.activation(out=Pk.ap[:, m, lo:s0 + W], in_=pss[m].ap[:, lo:s0 + W], func=AF.Exp,
                                                                                     bias=abias.ap[:, hh, oi:oi + 1]), r=[pss[m], abias], w=[Pk])
                if j >= 0:
                    C.op("pool", lambda e: e.affine_select(out=Pk.ap[:, :, q0:q0 + 128], in_=Pk.ap[:, :, q0:q0 + 128], pattern=[[0, 2], [1, 128]],
                                                           compare_op=ALU.is_ge, fill=0.0, base=0, channel_multiplier=-1), r=[Pk], w=[Pk])
                last = (kt == nkt - 1)
                for m in range(2):
                    for qt in range(max(j, 0), 4):
                        mm(acc[m], acc[m].ap[:, qt * 65:(qt + 1) * 65], Pk, Pk.ap[:, m, qt * 128:(qt + 1) * 128], vaug, vaug.ap[:, kt, hh, :], False, last and qt == 3,
                           signal=(qt == 3))
            a0 = acc[0].ap[:, 0:260].rearrange("p (q e) -> p q e", q=4)
            a1 = acc[1].ap[:, 0:260].rearrange("p (q e) -> p q e", q=4)
            C.op("dve", lambda e: e.reciprocal(out=R0.ap, in_=a0[:, :, 64]), r=[acc[0]], w=[R0])
            C.op("dve", lambda e: e.reciprocal(out=R1.ap, in_=a1[:, :, 64]), r=[acc[1]], w=[R1])
            C.op("dve", lambda e: e.tensor_scalar(out=R1.ap, in0=R1.ap, scalar1=lam.ap[:, 3:4], scalar2=None, op0=ALU.mult), r=[R1, lam], w=[R1])
            C.op("dve", lambda e: e.tensor_tensor(out=TT.ap, in0=a1[:, :, 0:64], in1=R1.ap.unsqueeze(2).broadcast_to([128, 4, 64]), op=ALU.mult), r=[acc[1], R1], w=[TT])
            C.op("dve", lambda e: e.tensor_tensor(out=UU.ap, in0=a0[:, :, 0:64], in1=R0.ap.unsqueeze(2).broadcast_to([128, 4, 64]), op=ALU.mult), r=[acc[0], R0], w=[UU])
            C.op("dve", lambda e: e.tensor_tensor(out=UU.ap, in0=UU.ap, in1=TT.ap, op=ALU.add), r=[UU, TT], w=[UU])
            C.op("act", lambda e: e.activation(out=TT.ap, in_=UU.ap, func=AF.Square), r=[UU], w=[TT])
            C.op("dve", lambda e: e.tensor_reduce(out=R0.ap, in_=TT.ap, axis=AX.X, op=ALU.add), r=[TT], w=[R0])
            C.op("dve", lambda e: e.tensor_scalar(out=R0.ap, in0=R0.ap, scalar1=1.0 / 64, scalar2=EPS, op0=ALU.mult, op1=ALU.add), r=[R0], w=[R0])
            C.op("act", lambda e: e.activation(out=R0.ap, in_=R0.ap, func=AF.Sqrt), r=[R0], w=[R0])
            C.op("dve", lambda e: e.reciprocal(out=R0.ap, in_=R0.ap), r=[R0], w=[R0])
            C.op("dve", lambda e: e.tensor_tensor(out=UU.ap, in0=UU.ap, in1=R0.ap.unsqueeze(2).broadcast_to([128, 4, 64]), op=ALU.mult), r=[UU, R0], w=[UU])
            C.op("dve", lambda e, hh=hh: e.tensor_tensor(out=dtok.ap[:, :, hh * 64:(hh + 1) * 64], in0=UU.ap,
                                                        in1=subg.ap[:, hh * 64:(hh + 1) * 64].unsqueeze(1).broadcast_to([128, 4, 64]), op=ALU.mult), r=[UU, subg], w=[dtok])
        for qt in range(4):
            for j in range(3):
                C.op("pe", lambda e, j=j, qt=qt: e.transpose(pst.ap[:, j * 128:(j + 1) * 128], dtok.ap[:, qt, j * 128:(j + 1) * 128], identb.ap), r=[dtok, cmatb], w=[pst], signal=(j == 2))
            C.op("act", lambda e, qt=qt: e.activation(out=mix.ap[:, 5:8, qt * 128:(qt + 1) * 128], in_=pst.ap[:, 0:384].rearrange("p (a b) -> p a b", a=3), func=AF.Copy), r=[pst], w=[mix])

    def ffn_block(l, b):
        rmsnorm(b, NL + l, hn, 22528, 24576)
        act = carve(0, [128, NCT, BT], BF16)
        c0b = [carve(22528 + k * 2048, [128, BT], F32) for k in range(3)] + [carve(28672, [128, BT], F32)]
        for g in range(NCT // 2):
            w = next_slab("w_up")
            for cl in range(2):
                ct = 2 * g + cl
                pa, pb = (ps[0], ps[1]) if ct % 2 == 0 else (ps[2], ps[3])
                cbuf = c0b[2 * (ct % 2):2 * (ct % 2) + 2]
                for m, p in enumerate((pa, pb)):
                    for kt in range(8):
                        mm(p, p.ap, w, w.ap[:, kt, m, cl * 128:(cl + 1) * 128], hn, hn.ap[:, kt, :], kt == 0, kt == 7)
                for m, p in enumerate((pa, pb)):
                    ci = m * NCT + ct
                    cc = cbuf[m]
                    C.op("act", lambda e, p=p, cc=cc, ci=ci: e.activation(out=cc.ap, in_=p.ap, func=AF.Identity, scale=cw.ap[:, 2, ci:ci + 1], bias=cw.ap[:, 3, ci:ci + 1]), r=[p, cw], w=[cc])
                    C.op("dve", lambda e, p=p, cc=cc, ci=ci: e.scalar_tensor_tensor(out=cc.ap[:, 1:BT], in0=p.ap[:, 0:BT - 1], scalar=cw.ap[:, 1, ci:ci + 1], in1=cc.ap[:, 1:BT], op0=ALU.mult, op1=ALU.add), r=[p, cw, cc], w=[cc])
                    C.op("dve", lambda e, p=p, cc=cc, ci=ci: e.scalar_tensor_tensor(out=cc.ap[:, 2:BT], in0=p.ap[:, 0:BT - 2], scalar=cw.ap[:, 0, ci:ci + 1], in1=cc.ap[:, 2:BT], op0=ALU.mult, op1=ALU.add), r=[p, cw, cc], w=[cc])
                    if b > 0:
                        C.op("dve", lambda e, cc=cc, ci=ci: e.scalar_tensor_tensor(out=cc.ap[:, 0:2], in0=halo.ap[:, ci, 0:2], scalar=cw.ap[:, 0, ci:ci + 1], in1=cc.ap[:, 0:2], op0=ALU.mult, op1=ALU.add), r=[halo, cw, cc], w=[cc])
                        C.op("dve", lambda e, cc=cc, ci=ci: e.scalar_tensor_tensor(out=cc.ap[:, 0:1], in0=halo.ap[:, ci, 1:2], scalar=cw.ap[:, 1, ci:ci + 1], in1=cc.ap[:, 0:1], op0=ALU.mult, op1=ALU.add), r=[halo, cw, cc], w=[cc])
                    if b < NBLK - 1:
                        C.op("dve", lambda e, p=p, ci=ci: e.tensor_copy(out=halo.ap[:, ci, :], in_=p.ap[:, BT - 2:BT]), r=[p], w=[halo])
                C.op("act", lambda e, cbuf=cbuf: e.activation(out=cbuf[0].ap, in_=cbuf[0].ap, func=AF.Silu), r=[cbuf[0]], w=[cbuf[0]])
                C.op("dve", lambda e, cbuf=cbuf, ct=ct: e.tensor_tensor(out=act.ap[:, ct, :], in0=cbuf[0].ap, in1=cbuf[1].ap, op=ALU.mult), r=[cbuf[0], cbuf[1]], w=[act])
            slab_done()
        for dp in range(4):
            pd = [ps[4], ps[5]]
            for hf in range(2):
                w = next_slab("w_down")
                for dl in range(2):
                    for k in range(11):
                        mm(pd[dl], pd[dl].ap, w, w.ap[:, k, dl * 128:(dl + 1) * 128], act, act.ap[:, hf * 11 + k, :], hf == 0 and k == 0, hf == 1 and k == 10,
                           signal=(k == 10))
                slab_done()
            for dl in range(2):
                h_add(b, 2 * dp + dl, pd[dl])

    def ple_block(l, b):
        rmsnorm(b, 2 * NL + l, hn, 0, 2048)
        pf = carve(4096, [128, 2, BT], F32)
        pb = carve(8192, [128, 2, BT], BF16)
        sg = [carve(10240 + k * 2048, [128, BT], F32) for k in range(2)]
        C.dma("sp", pf.ap, pT_d[l, :, :, hsl(b)], w=[pf])
        C.op("pool", lambda e: e.tensor_copy(out=pb.ap, in_=pf.ap), r=[pf], w=[pb])
        for half in range(2):
            w = next_slab("w_pg")
            wpe = next_slab("w_pe")
            for dl in range(4):
                dt = 4 * half + dl
                pg, pp = (ps[0], ps[1]) if dt % 2 == 0 else (ps[2], ps[3])
                c0 = dl * 128
                for kt in range(8):
                    mm(pg, pg.ap, w, w.ap[:, kt, c0:c0 + 128], hn, hn.ap[:, kt, :], kt == 0, kt == 7)
                for kt in range(2):
                    mm(pp, pp.ap, wpe, wpe.ap[:, kt, c0:c0 + 128], pb, pb.ap[:, kt, :], kt == 0, kt == 1)
                s_ = sg[dt % 2]
                C.op("act", lambda e, s_=s_, pg=pg: e.activation(out=s_.ap, in_=pg.ap, func=AF.Sigmoid), r=[pg], w=[s_])
                C.op("dve", lambda e, s_=s_, pp=pp: e.tensor_tensor(out=s_.ap, in0=pp.ap, in1=s_.ap, op=ALU.mult), r=[pp, s_], w=[s_])
                C.op("pool", lambda e, s_=s_, dt=dt: e.tensor_tensor(out=h.ap[:, dt, hsl(b)], in0=s_.ap, in1=h.ap[:, dt, hsl(b)], op=ALU.add), r=[s_, h], w=[h])
            slab_done()
            slab_done()

    for l in range(nl):
        if l > 0:
            C.new_epoch()
        C.dma("sp", cw.ap, cw_d[:, l], w=[cw])
        C.dma("sp", rows.ap, rows_d[:, l], w=[rows])
        C.dma("sp", lqk.ap, lqk_d[:, l], w=[lqk])
        C.mark("setup")
        if on("s5"):
            s5_setup(l)
        if on("diff"):
            diff_setup(l)
        for b in range(nblk):
            C.mark("norm1")
            rmsnorm(b, l, hn, 26624, 28672)
            C.mark("s5")
            if on("s5"):
                s5_block(l, b)
            else:
                C.op("dve", lambda e: e.memset(mix.ap[:, 0:2, :], 0.0), w=[mix])
            C.mark("ret")
            if on("ret"):
                ret_block(l, b)
            else:
                C.op("dve", lambda e: e.memset(mix.ap[:, 2:5, :], 0.0), w=[mix])
            C.mark("diff")
            if on("diff"):
                diff_block(l, b)
            else:
                C.op("dve", lambda e: e.memset(mix.ap[:, 5:8, :], 0.0), w=[mix])
            if dbg and l == nl - 1:
                C.dma("sp", dbg_d["mix"][:, :, hsl(b)], mix.ap, r=[mix])
            C.mark("wout")
            for half in range(2):
                w = next_slab("w_out")
                for dl in range(4):
                    dt = 4 * half + dl
                    p = ps[4 + dt % 2]
                    c0 = dl * 128
                    for kt in range(8):
                        mm(p, p.ap, w, w.ap[:, kt, c0:c0 + 128], mix, mix.ap[:, kt, :], kt == 0, kt == 7)
                    h_add(b, dt, p)
                slab_done()
            C.mark("ffn")
            if on("ffn"):
                ffn_block(l, b)
            C.mark("ple")
            if on("ple"):
                ple_block(l, b)
            C.mark("end")
            if dbg:
                C.dma("sp", dbg_d["h%d" % l][:, :, hsl(b)], h.ap[:, :, hsl(b)], r=[h])
            if l == nl - 1:
                yo = carve(0, [128, 8, BT], F32)
                rmsnorm(b, 3 * NL, yo, 16384, 18432)
                C.dma("sp", yT_d[:, :, hsl(b)], yo.ap, r=[yo])
    deps = [(s_[0], s_[1]) for pl in C.dma_pool.values() for s_ in pl if s_[1] > 0]
    C._wait("sp", deps)
    build_program.marks = C.marks
    stuck, counts = C.simulate()
    print("instr counts", counts, "stuck", stuck)
    assert not stuck, stuck
    es.close()
    return nc


def host_layouts(inp):
    f = np.float32
    L = NL
    shared = {}
    for nm in ["w_in", "w_out", "w_up", "w_down", "w_pg", "w_pe"]:
        shared[nm] = np.ascontiguousarray(inp[nm], dtype=f)
    shared["w_glu"] = np.ascontiguousarray(np.asarray(inp["ssm_w_glu"], f).reshape(L, 2, 128, 256).transpose(0, 2, 1, 3))
    gl = [np.asarray(inp["norm1_g"], f), np.asarray(inp["norm2_g"], f), np.asarray(inp["norm3_g"], f)]
    gains = np.concatenate([g.reshape(L, 8, 128) for g in gl] + [np.asarray(inp["final_g"], f).reshape(1, 8, 128)], 0)
    shared["gains"] = np.ascontiguousarray(gains.transpose(2, 0, 1))
    cwv = np.concatenate([np.asarray(inp["conv_w"], f), np.asarray(inp["conv_b"], f)[:, None, :]], 1)
    shared["cw"] = np.ascontiguousarray(cwv.reshape(L, 4, 2 * NCT, 128).transpose(3, 0, 1, 2))
    lre = np.asarray(inp["ssm_lam_re"], f).reshape(L, 1024)
    lim = np.asarray(inp["ssm_lam_im"], f).reshape(L, 1024)
    ldt = np.repeat(np.asarray(inp["ssm_log_dt"], f), 64, axis=1)
    row = np.stack([lre, lim, ldt], 1)
    shared["s5row"] = np.ascontiguousarray(np.broadcast_to(row[:, None], (L, 128, 3, 1024)))
    bre = np.asarray(inp["ssm_b_re"], f)
    bim = np.asarray(inp["ssm_b_im"], f)
    s5b = np.zeros((L, 128, 2, 2, 512), f)
    for ri, bb in enumerate((bre, bim)):
        for g in range(16):
            i, gl_ = g // 8, g % 8
            s5b[:, gl_ * 16:(gl_ + 1) * 16, ri, i, gl_ * 64:(gl_ + 1) * 64] = bb[:, g].transpose(0, 2, 1)
    shared["s5b"] = s5b
    cre = np.asarray(inp["ssm_c_re"], f)
    cim = np.asarray(inp["ssm_c_im"], f)
    s5c = np.zeros((L, 128, 2, 8, 128), f)
    for ri, cc in enumerate((cre, cim)):
        for g in range(16):
            j, g2, gl_ = g // 2, g % 2, g % 8
            s5c[:, g2 * 64:(g2 + 1) * 64, ri, j, gl_ * 16:(gl_ + 1) * 16] = cc[:, g].transpose(0, 2, 1)
    shared["s5c"] = s5c
    dsk = np.asarray(inp["ssm_d"], f).reshape(L, 2, 128)
    bgl = np.asarray(inp["ssm_b_glu"], f).reshape(L, 2, 128)
    shared["s5db"] = np.ascontiguousarray(np.stack([dsk, bgl], 1).transpose(3, 0, 1, 2))
    rws = np.stack([np.asarray(inp["ret_gn_g"], f), np.asarray(inp["diff_subln_g"], f)], 1)
    shared["rows"] = np.ascontiguousarray(np.broadcast_to(rws[None], (128, L, 2, 384)))
    lq = np.stack([np.asarray(inp[k], f) for k in ("diff_lq1", "diff_lk1", "diff_lq2", "diff_lk2")], 1)
    shared["lqk"] = np.ascontiguousarray(np.broadcast_to(lq[None], (128, L, 4, 32)))
    cm = np.zeros((128, 2, 128), f)
    cm[:, 0] = np.eye(128, dtype=f)
    cm[:, 1] = np.triu(np.ones((128, 128), f))
    shared["cmat"] = cm
    lg = RET_LOG_GAMMA.astype(np.float64)
    pos = np.arange(128, dtype=np.float64)
    qtab = np.zeros((128, 6, 128))
    for j in range(6):
        qtab[:, j] = np.exp((pos + 1.0) * lg[j])[None, :]
    ktab = np.exp((127.0 - pos)[:, None] * lg[None, :]) * 0.125
    mask = np.zeros((128, 2, 3, 128))
    for hh in range(6):
        mask[:, hh % 2, hh // 2, :] = np.where(pos[None, :] >= pos[:, None], np.exp(-(pos[:, None] + 1.0) * lg[hh]) * 0.125, 0.0)
    cd = np.zeros((128, 384))
    for j in range(6):
        cd[:, j * 64:(j + 1) * 64] = np.exp(128.0 * lg[j])
    tp = np.stack([pos + 1.0, -(pos + 1.0)], 1)
    shared["rtab"] = np.concatenate([qtab.reshape(128, -1), ktab, cd, tp], 1).astype(f)
    shared["rmask"] = mask.astype(f)
    ab = np.zeros((128, 6, 20), np.float64)
    for hh in range(6):
        for o in range(20):
            ab[:, hh, o] = np.float64(ALIBI_SLOPES[hh]) * (128.0 * (o - 16) + np.arange(128))
    shared["abias"] = ab.astype(f)
    return shared


def kernel(**inputs):
    cfg = inputs.pop("_cfg", {})
    x = np.asarray(inputs["x"], np.float32)
    p = np.asarray(inputs["p"], np.float32)
    shared = host_layouts(inputs)
    nl_ = cfg.get("layers", NL)
    for nm in ["w_in", "w_out", "w_up", "w_down", "w_pg", "w_pe"]:
        shared[nm] = shared[nm][:nl_]
    nc = build_program(cfg)
    in_maps = []
    for c in range(8):
        m = dict(shared)
        m["xT"] = np.ascontiguousarray(x[c].T.reshape(8, 128, SEQ).transpose(1, 0, 2))
        m["pT"] = np.ascontiguousarray(p[:, c].transpose(0, 2, 1).reshape(NL, 2, 128, SEQ).transpose(0, 2, 1, 3))
        in_maps.append(m)
    res = run_bass_kernel_spmd(nc, in_maps, core_ids=list(range(8)))
    out = np.empty((8, SEQ, D), np.float32)
    for c in range(8):
        yT = np.asarray(res.results[c]["yT"], np.float32)
        out[c] = yT.transpose(1, 0, 2).reshape(D, SEQ).T
    if cfg.get("dbg"):
        kernel.last = res.results
    return out
```

```python
import math
from contextlib import ExitStack
import numpy as np
import concourse.bass as bass
import concourse.mybir as mybir
from concourse.bass_utils import run_bass_kernel_spmd

F32 = mybir.dt.float32
BF16 = mybir.dt.bfloat16
I32 = mybir.dt.int32
ALU = mybir.AluOpType
AF = mybir.ActivationFunctionType
AX = mybir.AxisListType

NL, D, SEQ = 4, 1024, 2048
BT, NBLK = 512, 4
PROJ = 2944
DFF = 2816
NCT = 22
EPS = 1e-6
RET_LOG_GAMMA = np.log1p(-(2.0 ** (-5.0 - np.arange(6)))).astype(np.float32)
ALIBI_SLOPES = (2.0 ** (-8.0 * (np.arange(6) + 1) / 6)).astype(np.float32)
TWO_PI = 2.0 * math.pi


class Buf:
    __slots__ = ("w", "r")

    def __init__(self):
        self.w = None
        self.r = {}


class V:
    __slots__ = ("ap", "bufs")

    def __init__(self, ap, bufs):
        self.ap = ap
        self.bufs = bufs


class Ctx:
    def __init__(self, nc, es, n_dma_sems=20):
        self.nc = nc
        self.es = es
        self.engs = {"pe": nc.tensor, "act": nc.scalar, "dve": nc.vector, "pool": nc.gpsimd, "sp": nc.sync}
        self.epoch = 0
        self.sems = {}
        self.cnt = {}
        self.seen = {k: {} for k in self.engs}
        self.semobj = {}
        self.dma_pool = {"sp": [], "pool": []}
        for i in range(n_dma_sems):
            s = es.enter_context(nc.semaphore("dsem%d" % i))
            self.semobj[("dma", i)] = s
            self.dma_pool["sp" if i < n_dma_sems - 6 else "pool"].append([("dma", i), 0])
        self.dma_next = {"sp": 0, "pool": 0}
        self.pending = {k: [] for k in self.engs}
        self.log = {k: [] for k in self.engs}
        self.npe = 0
        self.marks = []

    def mark(self, name):
        self.marks.append((name, self.npe))

    def simulate(self):
        val = {}
        ptr = {k: 0 for k in self.engs}
        prog = True
        while prog:
            prog = False
            for e, lg in self.log.items():
                while ptr[e] < len(lg):
                    kind, key, v = lg[ptr[e]]
                    if kind == "wait":
                        if val.get(key, 0) < v:
                            break
                    else:
                        val[key] = val.get(key, 0) + v
                    ptr[e] += 1
                    prog = True
        stuck = {e: (ptr[e], len(lg), lg[ptr[e]], val.get(lg[ptr[e]][1], 0)) for e, lg in self.log.items() if ptr[e] < len(lg)}
        return stuck, {e: len(lg) for e, lg in self.log.items()}

    def _sem(self, e):
        key = (e, self.epoch)
        if key not in self.semobj:
            self.semobj[key] = self.es.enter_context(self.nc.semaphore("s_%s_%d" % key))
            self.cnt[key] = 0
        return key

    def new_epoch(self):
        self.epoch += 1

    def _wait(self, e, deps):
        seen = self.seen[e]
        for key, val in deps:
            if e == "pe" and key[0] == "pe":
                continue
            if seen.get(key, 0) >= val:
                continue
            self.engs[e].wait_ge(self.semobj[key], val)
            self.log[e].append(("wait", key, val))
            seen[key] = val

    def _deps(self, r, w):
        deps = []
        for v in r:
            for b in v.bufs:
                if b.w is not None:
                    deps.append(b.w)
        for v in w:
            for b in v.bufs:
                if b.w is not None:
                    deps.append(b.w)
                deps.extend(b.r.items())
        return deps

    def _mark(self, tok, r, w):
        key, val = tok
        for v in r:
            for b in v.bufs:
                if b.r.get(key, 0) < val:
                    b.r[key] = val
        for v in w:
            for b in v.bufs:
                b.w = tok
                b.r = {}

    def op(self, e, fn, r=(), w=(), signal=True, selfwait=False):
        self._wait(e, self._deps(r, w))
        if selfwait:
            key0 = self._sem(e)
            if self.cnt[key0] > 0:
                self.engs[e].wait_ge(self.semobj[key0], self.cnt[key0])
                self.log[e].append(("wait", key0, self.cnt[key0]))
        inst = fn(self.engs[e])
        if e == "pe":
            self.npe += 1
        key = self._sem(e)
        if signal:
            self.cnt[key] += 1
            inst.then_inc(self.semobj[key], 1)
            self.log[e].append(("inc", key, 1))
            tok = (key, self.cnt[key])
        else:
            tok = (key, self.cnt[key] + 1)
        self._mark(tok, r, w)
        return inst

    def dma(self, e, out, in_, r=(), w=()):
        pool = self.dma_pool[e]
        slot = pool[self.dma_next[e]]
        self.dma_next[e] = (self.dma_next[e] + 1) % len(pool)
        key, val = slot
        deps = self._deps(r, w)
        if val > 0:
            deps.append((key, val))
        self._wait(e, deps)
        inst = self.engs[e].dma_start(out=out, in_=in_)
        slot[1] = val + 16
        inst.then_inc(self.semobj[key], 16)
        self.log[e].append(("inc", key, 16))
        self._mark((key, val + 16), r, w)
        return inst

    def wait_all(self, e, views):
        deps = []
        for v in views:
            for b in v.bufs:
                if b.w is not None:
                    deps.append(b.w)
        self._wait(e, deps)


def build_program(cfg):
    nl = cfg.get("layers", NL)
    nblk = cfg.get("blocks", NBLK)
    on = lambda k: cfg.get(k, True)
    dbg = cfg.get("dbg", False)
    nc = bass.Bass("TRN2", target_bir_lowering=False)
    es = ExitStack()
    dram_in = {}

    def din(name, shape, dt=F32):
        dram_in[name] = nc.dram_tensor(name, list(shape), dt, kind="ExternalInput").ap()
        return dram_in[name]

    xT_d = din("xT", [128, 8, SEQ])
    pT_d = din("pT", [NL, 128, 2, SEQ])
    w_in_d = din("w_in", [nl, D, PROJ])
    w_out_d = din("w_out", [nl, D, D])
    w_up_d = din("w_up", [nl, D, 2 * DFF])
    w_down_d = din("w_down", [nl, DFF, D])
    w_pg_d = din("w_pg", [nl, D, D])
    w_pe_d = din("w_pe", [nl, 256, D])
    w_glu_d = din("w_glu", [NL, 128, 2, 256])
    gains_d = din("gains", [128, 3 * NL + 1, 8])
    cw_d = din("cw", [128, NL, 4, 2 * NCT])
    s5row_d = din("s5row", [NL, 128, 3, 1024])
    s5b_d = din("s5b", [NL, 128, 2, 2, 512])
    s5c_d = din("s5c", [NL, 128, 2, 8, 128])
    s5db_d = din("s5db", [128, NL, 2, 2])
    rows_d = din("rows", [128, NL, 2, 384])
    lqk_d = din("lqk", [128, NL, 4, 32])
    cmat_d = din("cmat", [128, 2, 128])
    rtab_d = din("rtab", [128, 6 * 128 + 6 + 384 + 2])
    rmask_d = din("rmask", [128, 2, 3, 128])
    abias_d = din("abias", [128, 6, 20])
    yT_d = nc.dram_tensor("yT", [128, 8, SEQ], F32, kind="ExternalOutput").ap()
    dbg_d = {}
    if dbg:
        for l in range(nl):
            dbg_d["h%d" % l] = nc.dram_tensor("dbg_h%d" % l, [128, 8, SEQ], F32, kind="ExternalOutput").ap()
        dbg_d["mix"] = nc.dram_tensor("dbg_mix", [128, 8, SEQ], BF16, kind="ExternalOutput").ap()
    wb = {}
    for nm, shp in [("w_in", [nl, D, PROJ]), ("w_out", [nl, D, D]), ("w_up", [nl, D, 2 * DFF]),
                    ("w_down", [nl, DFF, D]), ("w_pg", [nl, D, D]), ("w_pe", [nl, 256, D])]:
        wb[nm] = nc.dram_tensor("wb_" + nm, shp, BF16, kind="Internal").ap()
    wsrc = {"w_in": w_in_d, "w_out": w_out_d, "w_up": w_up_d, "w_down": w_down_d, "w_pg": w_pg_d, "w_pe": w_pe_d}

    C = Ctx(nc, es)

    def sb(name, shape, dt):
        t = es.enter_context(nc.sbuf_tensor("sb_" + name, list(shape), dt))
        return V(t[:], [Buf()])

    def sub(v, ap):
        return V(ap, v.bufs)

    h = sb("h", [128, 8, SEQ], F32)
    hblk = [[Buf() for _ in range(NBLK)]]
    hn = sb("hn", [128, 8, BT], BF16)
    mix = sb("mix", [128, 8, BT], BF16)
    gains = sb("gains", [128, 3 * NL + 1, 8], F32)
    cw = sb("cw", [128, 4, 2 * NCT], F32)
    s5db = sb("s5db", [128, NL, 2, 2], F32)
    rows = sb("rows", [128, 2, 384], F32)
    lqk = sb("lqk", [128, 4, 32], F32)
    cmatf = sb("cmatf", [128, 2, 128], F32)
    cmatb = sb("cmatb", [128, 2, 128], BF16)
    rtab = sb("rtab", [128, 6 * 128 + 6 + 384 + 2], F32)
    rmaskb = sb("rmaskb", [128, 2, 3, 128], BF16)
    abias = sb("abias", [128, 6, 20], F32)
    onesb = sb("onesb", [128, 128], BF16)
    zerob = sb("zerob", [128, 260], BF16)
    halo = sb("halo", [128, 2 * NCT, 2], F32)
    winv = sb("winv", [128, 2, 1024], BF16)
    wtf = sb("wtf", [128, 2, 8, 128], BF16)
    wlast = sb("wlast", [128, 2, 8], F32)
    bbbd = sb("bbbd", [128, 2, 1024], BF16)
    ctb = sb("ctb", [128, 2, 8, 128], BF16)
    carry = [sb("carry%d" % i, [128, 2, 4], F32) for i in range(2)]
    wglu = sb("wglu", [128, 2, 256], BF16)
    rstate = sb("rstate", [128, 384], F32)
    rstate_b = sb("rstate_b", [128, 384], BF16)
    kTc = sb("kTc", [128, 4, SEQ], BF16)
    vaug = sb("vaug", [128, 16, 6, 65], BF16)
    lam = sb("lam", [128, 4], F32)
    subg = sb("subg", [128, 384], F32)
    NSLOT = 3
    SLOT_ELEMS = 4096
    slots = [sb("slot%d" % i, [128, SLOT_ELEMS], BF16) for i in range(NSLOT)]
    SCR_BYTES = 30 * 1024
    scr_t = es.enter_context(nc.sbuf_tensor("scr", [128, SCR_BYTES // 2], BF16))
    CELL = 1024
    scr_cells = [Buf() for _ in range(SCR_BYTES // CELL)]

    def carve(off, shape, dt):
        esz = 4 if dt in (F32, I32) else 2
        n = int(np.prod(shape[1:]))
        nb = n * esz
        assert off % 4 == 0 and off + nb <= SCR_BYTES, (off, nb)
        ap = scr_t[0:shape[0], off // 2:(off + nb) // 2]
        if esz == 4:
            ap = ap.bitcast(dt)
        if len(shape) == 3:
            ap = ap.rearrange("p (a b) -> p a b", a=shape[1])
        elif len(shape) == 4:
            ap = ap.rearrange("p (a b c) -> p a b c", a=shape[1], b=shape[2])
        return V(ap, scr_cells[off // CELL:(off + nb + CELL - 1) // CELL])

    ps = []
    for i in range(7):
        t = es.enter_context(nc.psum_tensor("ps%d" % i, [128, 512], F32))
        ps.append(V(t[:], [Buf()]))
    pst_t = es.enter_context(nc.psum_tensor("pst", [128, 1024], BF16))
    pst = V(pst_t[:], [Buf()])

    cast_sems = [es.enter_context(nc.semaphore("cast%d" % l)) for l in range(nl)]
    cast_tot = [0] * nl
    for l in range(nl):
        for nm in ["w_in", "w_out", "w_up", "w_down", "w_pg", "w_pe"]:
            K, N = wsrc[nm].shape[1], wsrc[nm].shape[2]
            for r0 in range(0, K, 512):
                r1 = min(K, r0 + 512)
                for c0 in range(0, N, 2048):
                    c1 = min(N, c0 + 2048)
                    nc.gpsimd.dma_start(out=wb[nm][l, r0:r1, c0:c1], in_=wsrc[nm][l, r0:r1, c0:c1]).then_inc(cast_sems[l], 16)
                    cast_tot[l] += 16

    def load(dst, src):
        C.dma("sp", dst.ap, src, w=[dst])

    load(gains, gains_d)
    load(s5db, s5db_d)
    load(cmatf, cmat_d)
    load(rtab, rtab_d)
    for kt in range(8):
        C.dma("sp", h.ap[:, kt, :], xT_d[:, kt, :], w=[h])
    C.op("dve", lambda e: e.tensor_copy(out=cmatb.ap, in_=cmatf.ap), r=[cmatf], w=[cmatb])
    C.op("dve", lambda e: e.memset(onesb.ap, 1.0), w=[onesb])
    C.op("dve", lambda e: e.memset(zerob.ap, 0.0), w=[zerob])
    C.op("dve", lambda e: e.memset(vaug.ap, 1.0), w=[vaug])
    identb = sub(cmatb, cmatb.ap[:, 0, :])
    trib = sub(cmatb, cmatb.ap[:, 1, :])
    identf = sub(cmatf, cmatf.ap[:, 0, :])
    load(abias, abias_d)
    tmp = carve(16384, [128, 2, 3, 128], F32)
    C.dma("sp", tmp.ap, rmask_d, w=[tmp])
    C.op("dve", lambda e: e.tensor_copy(out=rmaskb.ap, in_=tmp.ap), r=[tmp], w=[rmaskb])
    o0 = 0
    ret_qtab = sub(rtab, rtab.ap[:, o0:o0 + 768].rearrange("p (a b) -> p a b", a=6)); o0 += 768
    ret_ktab = sub(rtab, rtab.ap[:, o0:o0 + 6]); o0 += 6
    ret_mask = rmaskb
    ret_cd = sub(rtab, rtab.ap[:, o0:o0 + 384]); o0 += 384
    tpos = sub(rtab, rtab.ap[:, o0:o0 + 1])
    tneg = sub(rtab, rtab.ap[:, o0 + 1:o0 + 2])

    slab_state = {"i": 0, "queue": [], "issued": 0}

    def slab_specs(l):
        sp = []
        if on("s5"):
            sp.append(("w_in", l, 8, 0, 256))
        if on("ret"):
            for c0 in (256, 640, 1024, 1408):
                sp.append(("w_in", l, 8, c0, c0 + 384))
        if on("diff"):
            for c0 in (1792, 2176, 2560):
                sp.append(("w_in", l, 8, c0, c0 + 384))
        for c0 in (0, 512):
            sp.append(("w_out", l, 8, c0, c0 + 512))
        if on("ffn"):
            for g in range(NCT // 2):
                sp.append(("w_up", l, 8, g, None))
            for dp in range(4):
                for hf in range(2):
                    sp.append(("w_down", l, 11, dp, hf))
        if on("ple"):
            for c0 in (0, 512):
                sp.append(("w_pg", l, 8, c0, c0 + 512))
                sp.append(("w_pe", l, 2, c0, c0 + 512))
        return sp

    all_specs = []
    for l in range(nl):
        for b in range(nblk):
            all_specs.extend(slab_specs(l))
    cast_waited = set()

    def issue_slab(idx):
        nm, l, kt, a, b = all_specs[idx]
        slot = slots[idx % NSLOT]
        if l not in cast_waited:
            nc.sync.wait_ge(cast_sems[l], cast_tot[l])
            cast_waited.add(l)
        if nm == "w_up":
            g = a
            dst = slot.ap[:, 0:8 * 512].rearrange("p (k m c) -> p k m c", k=8, m=2)
            for m in range(2):
                c0 = m * DFF + g * 256
                src = wb[nm][l, :, c0:c0 + 256].rearrange("(k p) n -> p k n", p=128)
                C.dma("sp", dst[:, :, m, :], src, w=[slot])
        elif nm == "w_down":
            dp, hf = a, b
            dst = slot.ap[:, 0:11 * 256].rearrange("p (k c) -> p k c", k=11)
            src = wb[nm][l, hf * 1408:(hf + 1) * 1408, dp * 256:(dp + 1) * 256].rearrange("(k p) n -> p k n", p=128)
            C.dma("sp", dst, src, w=[slot])
        else:
            n = b - a
            dst = slot.ap[:, 0:kt * n].rearrange("p (k c) -> p k c", k=kt)
            src = wb[nm][l, :, a:b].rearrange("(k p) n -> p k n", p=128)
            C.dma("sp", dst, src, w=[slot])

    def next_slab(expect):
        i = slab_state["i"]
        assert all_specs[i][0] == expect, (all_specs[i], expect)
        assert i < slab_state["issued"], "slab not issued (too many held)"
        slab_state["i"] = i + 1
        nm, l, kt, a, b = all_specs[i]
        slot = slots[i % NSLOT]
        if nm == "w_up":
            return sub(slot, slot.ap[:, 0:8 * 512].rearrange("p (k m c) -> p k m c", k=8, m=2))
        if nm == "w_down":
            return sub(slot, slot.ap[:, 0:11 * 256].rearrange("p (k c) -> p k c", k=11))
        n = b - a
        return sub(slot, slot.ap[:, 0:kt * n].rearrange("p (k c) -> p k c", k=kt))

    def slab_done():
        if slab_state["issued"] < len(all_specs):
            issue_slab(slab_state["issued"])
            slab_state["issued"] += 1

    for _ in range(min(NSLOT, len(all_specs))):
        issue_slab(slab_state["issued"])
        slab_state["issued"] += 1

    def mm(out_v, out_ap, lhsT_v, lhsT_ap, rhs_v, rhs_ap, start, stop, signal=None, selfwait=False):
        if signal is None:
            signal = True
        C.op("pe", lambda e: e.matmul(out_ap, lhsT_ap, rhs_ap, start=start, stop=stop),
             r=[lhsT_v, rhs_v], w=[out_v], signal=signal, selfwait=selfwait)

    def hsl(b):
        return slice(b * BT, (b + 1) * BT)

    def rmsnorm(b, gidx, dst, scrA, scrB):
        pss = ps[6]
        sq = [carve(scrA, [128, BT], BF16), carve(scrA + 1024, [128, BT], BF16)]
        for kt in range(8):
            s = sq[kt % 2]
            C.op("act", lambda e, s=s, kt=kt: e.activation(out=s.ap, in_=h.ap[:, kt, hsl(b)], func=AF.Square), r=[h], w=[s])
            mm(pss, pss.ap, onesb, onesb.ap, s, s.ap, kt == 0, kt == 7)
        rs = carve(scrB, [128, BT], F32)
        C.op("act", lambda e: e.activation(out=rs.ap, in_=pss.ap, func=AF.Ln, scale=1.0 / D, bias=EPS), r=[pss], w=[rs])
        C.op("act", lambda e: e.activation(out=rs.ap, in_=rs.ap, func=AF.Exp, scale=-0.5), r=[rs], w=[rs])
        for kt in range(8):
            C.op("dve", lambda e, kt=kt: e.scalar_tensor_tensor(out=dst.ap[:, kt, :], in0=h.ap[:, kt, hsl(b)],
                                                               scalar=gains.ap[:, gidx, kt:kt + 1], in1=rs.ap,
                                                               op0=ALU.mult, op1=ALU.mult), r=[h, gains, rs], w=[dst])

    def proj_fm(w, cols, evac, width=128):
        for j, c0 in enumerate(cols):
            p = ps[4 + (j % 2)]
            for kt in range(8):
                mm(p, p.ap[0:width, :], w, w.ap[:, kt, c0:c0 + width], hn, hn.ap[:, kt, :], kt == 0, kt == 7)
            evac(j, p)

    def proj_tm(w, ncols, evac):
        for c in range(4):
            p = ps[4 + (c % 2)]
            for kt in range(8):
                mm(p, p.ap[:, 0:ncols], hn, hn.ap[:, kt, c * 128:(c + 1) * 128], w, w.ap[:, kt, 0:ncols], kt == 0, kt == 7)
            evac(c, p)

    def h_add(b, dt, p):
        C.op("dve", lambda e: e.tensor_tensor(out=h.ap[:, dt, hsl(b)], in0=p.ap, in1=h.ap[:, dt, hsl(b)], op=ALU.add), r=[p, h], w=[h])

    def s5_setup(l):
        CTf = carve(0, [128, 2, 8, 128], F32)
        C.dma("sp", CTf.ap, s5c_d[l], w=[CTf])
        C.op("dve", lambda e: e.tensor_copy(out=ctb.ap[:, 0], in_=CTf.ap[:, 0]), r=[CTf], w=[ctb])
        C.op("dve", lambda e: e.tensor_scalar(out=ctb.ap[:, 1], in0=CTf.ap[:, 1], scalar1=-1.0, scalar2=None, op0=ALU.mult), r=[CTf], w=[ctb])
        WG = carve(8192, [128, 2, 256], F32)
        C.dma("sp", WG.ap, w_glu_d[l], w=[WG])
        C.op("dve", lambda e: e.tensor_copy(out=wglu.ap, in_=WG.ap), r=[WG], w=[wglu])
        for i in range(2):
            C.op("dve", lambda e, i=i: e.memset(carry[i].ap, 0.0), w=[carry[i]])
        for hf in range(2):
            s5_setup_half(l, hf)

    def s5_setup_half(l, hf):
        HS = slice(hf * 512, (hf + 1) * 512)
        T = [carve(i * 2048, [128, 512], F32) for i in range(9)]
        LR, LI, A, B, Cc, Dd, E, Fv, G = T
        TI = carve(9 * 2048, [128, 512], I32)
        MK = carve(10 * 2048, [128, 512], F32)
        C.dma("sp", LR.ap, s5row_d[l, :, 0, HS], w=[LR])
        C.dma("sp", LI.ap, s5row_d[l, :, 1, HS], w=[LI])
        C.dma("sp", A.ap, s5row_d[l, :, 2, HS], w=[A])

        def tt(eng, o, a, b_, op):
            C.op(eng, lambda e: e.tensor_tensor(out=o.ap, in0=a.ap, in1=b_.ap, op=op), r=[a, b_], w=[o])

        def act(o, a, func, **kw):
            C.op("act", lambda e: e.activation(out=o.ap, in_=a.ap, func=func, **kw), r=[a], w=[o])

        def sincos(phi, s_out, c_out):
            C.op("dve", lambda e: e.tensor_scalar(out=TI.ap, in0=phi.ap, scalar1=1.0 / TWO_PI, scalar2=None, op0=ALU.mult), r=[phi], w=[TI])
            C.op("dve", lambda e: e.tensor_copy(out=MK.ap, in_=TI.ap), r=[TI], w=[MK])
            C.op("dve", lambda e: e.scalar_tensor_tensor(out=s_out.ap, in0=MK.ap, scalar=-TWO_PI, in1=phi.ap, op0=ALU.mult, op1=ALU.add), r=[MK, phi], w=[s_out])
            C.op("dve", lambda e: e.tensor_scalar(out=MK.ap, in0=s_out.ap, scalar1=math.pi, scalar2=-TWO_PI, op0=ALU.is_gt, op1=ALU.mult), r=[s_out], w=[MK])
            tt("dve", s_out, s_out, MK, ALU.add)
            C.op("dve", lambda e: e.tensor_scalar(out=MK.ap, in0=s_out.ap, scalar1=-math.pi, scalar2=TWO_PI, op0=ALU.is_lt, op1=ALU.mult), r=[s_out], w=[MK])
            tt("dve", s_out, s_out, MK, ALU.add)
            C.op("dve", lambda e: e.tensor_scalar(out=MK.ap, in0=s_out.ap, scalar1=math.pi / 2, scalar2=-TWO_PI, op0=ALU.is_gt, op1=ALU.mult), r=[s_out], w=[MK])
            C.op("dve", lambda e: e.scalar_tensor_tensor(out=c_out.ap, in0=s_out.ap, scalar=math.pi / 2, in1=MK.ap, op0=ALU.add, op1=ALU.add), r=[s_out, MK], w=[c_out])
            act(s_out, s_out, AF.Sin)
            act(c_out, c_out, AF.Sin)

        act(A, A, AF.Exp)
        tt("dve", B, LR, A, ALU.mult)
        tt("dve", Cc, LI, A, ALU.mult)
        sincos(Cc, A, Dd)
        act(E, B, AF.Exp)
        tt("dve", Dd, E, Dd, ALU.mult)
        tt("dve", A, E, A, ALU.mult)
        C.op("dve", lambda e: e.tensor_scalar(out=Dd.ap, in0=Dd.ap, scalar1=-1.0, scalar2=None, op0=ALU.add), r=[Dd], w=[Dd])
        tt("dve", E, LR, LR, ALU.mult)
        tt("dve", Fv, LI, LI, ALU.mult)
        tt("dve", E, E, Fv, ALU.add)
        C.op("dve", lambda e: e.reciprocal(out=E.ap, in_=E.ap), r=[E], w=[E])
        tt("dve", Fv, Dd, LR, ALU.mult)
        tt("dve", G, A, LI, ALU.mult)
        tt("dve", Fv, Fv, G, ALU.add)
        tt("dve", Fv, Fv, E, ALU.mult)
        tt("dve", G, A, LR, ALU.mult)
        tt("dve", A, Dd, LI, ALU.mult)
        tt("dve", G, G, A, ALU.subtract)
        tt("dve", G, G, E, ALU.mult)
        BB = carve(0, [128, 2, 512], F32)
        C.dma("sp", BB.ap, s5b_d[l, :, :, hf, :], w=[BB])
        t1, t2 = A, Dd
        C.op("dve", lambda e: e.tensor_tensor(out=t1.ap, in0=Fv.ap, in1=BB.ap[:, 0, :], op=ALU.mult), r=[Fv, BB], w=[t1])
        C.op("dve", lambda e: e.tensor_tensor(out=t2.ap, in0=G.ap, in1=BB.ap[:, 1, :], op=ALU.mult), r=[G, BB], w=[t2])
        C.op("dve", lambda e: e.tensor_tensor(out=bbbd.ap[:, hf, 0:512], in0=t1.ap, in1=t2.ap, op=ALU.subtract), r=[t1, t2], w=[bbbd])
        C.op("dve", lambda e: e.tensor_tensor(out=t1.ap, in0=Fv.ap, in1=BB.ap[:, 1, :], op=ALU.mult), r=[Fv, BB], w=[t1])
        C.op("dve", lambda e: e.tensor_tensor(out=t2.ap, in0=G.ap, in1=BB.ap[:, 0, :], op=ALU.mult), r=[G, BB], w=[t2])
        C.op("dve", lambda e: e.tensor_tensor(out=bbbd.ap[:, hf, 512:1024], in0=t1.ap, in1=t2.ap, op=ALU.add), r=[t1, t2], w=[bbbd])
        C.op("dve", lambda e: e.tensor_scalar(out=A.ap, in0=Cc.ap, scalar1=tpos.ap, scalar2=None, op0=ALU.mult), r=[Cc, rtab], w=[A])
        sincos(A, Dd, E)
        C.op("act", lambda e: e.activation(out=Fv.ap, in_=B.ap, func=AF.Exp, scale=tpos.ap), r=[B, rtab], w=[Fv])
        C.op("act", lambda e: e.activation(out=G.ap, in_=B.ap, func=AF.Exp, scale=tneg.ap), r=[B, rtab], w=[G])
        C.op("dve", lambda e: e.tensor_tensor(out=winv.ap[:, 0, HS], in0=G.ap, in1=E.ap, op=ALU.mult), r=[G, E], w=[winv])
        C.op("dve", lambda e: e.scalar_tensor_tensor(out=winv.ap[:, 1, HS], in0=G.ap, scalar=-1.0, in1=Dd.ap, op0=ALU.mult, op1=ALU.mult), r=[G, Dd], w=[winv])
        tt("dve", E, Fv, E, ALU.mult)
        tt("dve", Dd, Fv, Dd, ALU.mult)
        for ri, src in enumerate((E, Dd)):
            for jl in range(4):
                j = 4 * hf + jl
                p = ps[jl % 2]
                C.op("pe", lambda e, p=p, src=src, jl=jl: e.transpose(p.ap[:, 0:128], src.ap[:, jl * 128:(jl + 1) * 128], identf.ap), r=[src, cmatf], w=[p])
                C.op("act", lambda e, p=p, ri=ri, j=j: e.activation(out=wtf.ap[:, ri, j, :], in_=p.ap[:, 0:128], func=AF.Copy), r=[p], w=[wtf])
                C.op("dve", lambda e, p=p, ri=ri, j=j: e.tensor_copy(out=wlast.ap[:, ri, j:j + 1], in_=p.ap[:, 127:128]), r=[p], w=[wlast])

    def s5_block(l, b):
        w = next_slab("w_in")
        uT = carve(0, [128, 2, BT], BF16)
        Tm = [carve(2048 + k * 1024, [128, BT], BF16) for k in range(4)]
        Z = carve(6144, [128, 2, BT], BF16)
        SC = carve(8192, [128, 2, 4, 128], F32)
        U = [carve(12288 + k * 1024, [128, 4, 128], BF16) for k in range(4)]
        CT4 = [carve(16384 + k * 1024, [128, 4], F32) for k in range(4)]
        X = carve(20480, [128, 2, 4, 128], BF16)
        YV = carve(22528, [128, BT], F32)
        G1 = carve(24576, [128, BT], F32)
        ZB = carve(26624, [128, 2, BT], BF16)

        def ev(j, p):
            C.op("act", lambda e: e.activation(out=uT.ap[:, j, :], in_=p.ap, func=AF.Copy), r=[p], w=[uT])
        proj_fm(w, [0, 128], ev)
        slab_done()
        for c in range(4):
            cs = slice(c * 128, (c + 1) * 128)
            for i in range(2):
                mm(ps[0], ps[0].ap, uT, uT.ap[:, i, cs], bbbd, bbbd.ap[:, i, 0:512], True, True)
                mm(ps[1], ps[1].ap, uT, uT.ap[:, i, cs], bbbd, bbbd.ap[:, i, 512:1024], True, True)
                isl = slice(i * 512, (i + 1) * 512)
                for k, (pp, ri) in enumerate(((0, 0), (1, 1), (0, 1), (1, 0))):
                    C.op("dve", lambda e, k=k, pp=pp, ri=ri: e.tensor_tensor(out=Tm[k].ap, in0=ps[pp].ap, in1=winv.ap[:, ri, isl], op=ALU.mult),
                         r=[ps[pp], winv], w=[Tm[k]])
                C.op("pool", lambda e: e.tensor_tensor(out=Z.ap[:, 0, :], in0=Tm[0].ap, in1=Tm[1].ap, op=ALU.subtract), r=[Tm[0], Tm[1]], w=[Z])
                C.op("pool", lambda e: e.tensor_tensor(out=Z.ap[:, 1, :], in0=Tm[2].ap, in1=Tm[3].ap, op=ALU.add), r=[Tm[2], Tm[3]], w=[Z])
                for ri in range(2):
                    for jl in range(4):
                        mm(ps[2 + ri], ps[2 + ri].ap[:, jl * 128:(jl + 1) * 128], Z, Z.ap[:, ri, jl * 128:(jl + 1) * 128], trib, trib.ap, True, True)
                for ri in range(2):
                    C.op("dve", lambda e, ri=ri: e.tensor_tensor(out=SC.ap[:, ri], in0=ps[2 + ri].ap.rearrange("p (a b) -> p a b", a=4),
                                                                 in1=carry[i].ap[:, ri, :].unsqueeze(2).broadcast_to([128, 4, 128]), op=ALU.add),
                         r=[ps[2 + ri], carry[i]], w=[SC])
                wr = wtf.ap[:, 0, 4 * i:4 * i + 4, :]
                wi = wtf.ap[:, 1, 4 * i:4 * i + 4, :]
                C.op("dve", lambda e: e.tensor_tensor(out=U[0].ap, in0=SC.ap[:, 0], in1=wr, op=ALU.mult), r=[SC, wtf], w=[U[0]])
                C.op("pool", lambda e: e.tensor_tensor(out=U[1].ap, in0=SC.ap[:, 1], in1=wi, op=ALU.mult), r=[SC, wtf], w=[U[1]])
                C.op("dve", lambda e: e.tensor_tensor(out=U[2].ap, in0=SC.ap[:, 0], in1=wi, op=ALU.mult), r=[SC, wtf], w=[U[2]])
                C.op("pool", lambda e: e.tensor_tensor(out=U[3].ap, in0=SC.ap[:, 1], in1=wr, op=ALU.mult), r=[SC, wtf], w=[U[3]])
                C.op("pool", lambda e: e.tensor_tensor(out=X.ap[:, 0], in0=U[0].ap, in1=U[1].ap, op=ALU.subtract), r=[U[0], U[1]], w=[X])
                C.op("pool", lambda e: e.tensor_tensor(out=X.ap[:, 1], in0=U[2].ap, in1=U[3].ap, op=ALU.add), r=[U[2], U[3]], w=[X])
                for k, (ra, rb) in enumerate(((0, 0), (1, 1), (0, 1), (1, 0))):
                    C.op("dve", lambda e, k=k, ra=ra, rb=rb: e.tensor_tensor(out=CT4[k].ap, in0=SC.ap[:, ra, :, 127], in1=wlast.ap[:, rb, 4 * i:4 * i + 4], op=ALU.mult), r=[SC, wlast], w=[CT4[k]])
                C.op("dve", lambda e: e.tensor_tensor(out=carry[i].ap[:, 0, :], in0=CT4[0].ap, in1=CT4[1].ap, op=ALU.subtract), r=[CT4[0], CT4[1]], w=[carry[i]])
                C.op("dve", lambda e: e.tensor_tensor(out=carry[i].ap[:, 1, :], in0=CT4[2].ap, in1=CT4[3].ap, op=ALU.add), r=[CT4[2], CT4[3]], w=[carry[i]])
                py = ps[4 + i]
                for jl in range(4):
                    for ri in range(2):
                        mm(py, py.ap[:, cs], ctb, ctb.ap[:, ri, 4 * i + jl, :], X, X.ap[:, ri, jl, :], jl == 0 and ri == 0, jl == 3 and ri == 1)
        for i in range(2):
            py = ps[4 + i]
            C.op("dve", lambda e: e.scalar_tensor_tensor(out=YV.ap, in0=uT.ap[:, i, :], scalar=s5db.ap[:, l, 0, i:i + 1], in1=py.ap, op0=ALU.mult, op1=ALU.add), r=[uT, s5db, py], w=[YV])
            C.op("act", lambda e: e.activation(out=G1.ap, in_=YV.ap, func=AF.Square), r=[YV], w=[G1])
            C.op("dve", lambda e: e.tensor_scalar(out=G1.ap, in0=G1.ap, scalar1=0.044715, scalar2=1.0, op0=ALU.mult, op1=ALU.add), r=[G1], w=[G1])
            C.op("dve", lambda e: e.tensor_tensor(out=G1.ap, in0=G1.ap, in1=YV.ap, op=ALU.mult), r=[G1, YV], w=[G1])
            C.op("act", lambda e: e.activation(out=G1.ap, in_=G1.ap, func=AF.Sigmoid, scale=2.0 * 0.7978845608028654), r=[G1], w=[G1])
            C.op("dve", lambda e: e.tensor_tensor(out=ZB.ap[:, i, :], in0=YV.ap, in1=G1.ap, op=ALU.mult), r=[YV, G1], w=[ZB])
        for io in range(2):
            pg = ps[4 + io]
            for i in range(2):
                mm(pg, pg.ap, wglu, wglu.ap[:, i, io * 128:(io + 1) * 128], ZB, ZB.ap[:, i, :], i == 0, i == 1)
            C.op("act", lambda e: e.activation(out=G1.ap, in_=pg.ap, func=AF.Sigmoid, bias=s5db.ap[:, l, 1, io:io + 1]), r=[pg, s5db], w=[G1])
            C.op("dve", lambda e: e.tensor_tensor(out=mix.ap[:, io, :], in0=ZB.ap[:, io, :], in1=G1.ap, op=ALU.mult), r=[ZB, G1], w=[mix])

    def ret_block(l, b):
        qd = carve(0, [128, 6, BT], BF16)
        kT = carve(6144, [128, 6, BT], BF16)
        kdec = carve(12288, [128, 4, 384], BF16)
        vt = carve(15360, [128, 4, 384], BF16)
        gg = carve(18432, [128, 4, 384], BF16)
        inn = carve(21504, [128, 2, 3, 128], BF16)
        OC = carve(23040, [128, 6, 64], F32)
        SQ = carve(24576, [128, 6, 64], F32)
        GT = carve(26112, [128, 384], F32)
        ROUT = carve(27648, [128, 384], BF16)
        M1 = carve(28672, [128, 6], F32)
        M2 = carve(29696, [128, 6], F32)
        gn = rows.ap[:, 0, :]
        wq = next_slab("w_in")

        def evq(j, p):
            C.op("dve", lambda e: e.tensor_tensor(out=qd.ap[0:64, j, :].rearrange("p (c t) -> p c t", c=4), in0=p.ap[0:64, :].rearrange("p (c t) -> p c t", c=4),
                                                  in1=ret_qtab.ap[0:64, j, :].unsqueeze(1).broadcast_to([64, 4, 128]), op=ALU.mult), r=[p, rtab], w=[qd])
        proj_fm(wq, [0, 64, 128, 192, 256, 320], evq, width=64)
        slab_done()
        wk = next_slab("w_in")

        def evk(j, p):
            C.op("act", lambda e: e.activation(out=kT.ap[0:64, j, :], in_=p.ap[0:64, :], func=AF.Copy), r=[p], w=[kT])
        proj_fm(wk, [0, 64, 128, 192, 256, 320], evk, width=64)

        def evkt(c, p):
            C.op("dve", lambda e: e.tensor_tensor(out=kdec.ap[:, c, :].rearrange("p (h d) -> p h d", h=6), in0=p.ap[:, 0:384].rearrange("p (h d) -> p h d", h=6),
                                                  in1=ret_ktab.ap.unsqueeze(2).broadcast_to([128, 6, 64]), op=ALU.mult), r=[p, rtab], w=[kdec])
        proj_tm(wk, 384, evkt)
        slab_done()
        wv = next_slab("w_in")

        def evv(c, p):
            C.op("act", lambda e: e.activation(out=vt.ap[:, c, :], in_=p.ap[:, 0:384], func=AF.Copy), r=[p], w=[vt])
        proj_tm(wv, 384, evv)
        slab_done()
        wg = next_slab("w_in")

        def evg(c, p):
            C.op("act", lambda e: e.activation(out=GT.ap, in_=p.ap[:, 0:384], func=AF.Silu), r=[p], w=[GT])
            C.op("dve", lambda e: e.tensor_tensor(out=gg.ap[:, c, :], in0=GT.ap, in1=gn, op=ALU.mult), r=[GT, rows], w=[gg])
        proj_tm(wg, 384, evg)
        slab_done()
        if b == 0:
            C.op("dve", lambda e: e.memset(rstate.ap, 0.0), w=[rstate])
            C.op("dve", lambda e: e.memset(rstate_b.ap, 0.0), w=[rstate_b])
        def epilogue(c, po):
            cs = slice(c * 128, (c + 1) * 128)
            po3 = po.ap[:, 0:384].rearrange("p (h d) -> p h d", h=6)
            C.op("dve", lambda e: e.tensor_reduce(out=M1.ap, in_=po3, axis=AX.X, op=ALU.add), r=[po], w=[M1])
            C.op("dve", lambda e: e.tensor_scalar(out=M1.ap, in0=M1.ap, scalar1=-1.0 / 64, scalar2=None, op0=ALU.mult), r=[M1], w=[M1])
            C.op("dve", lambda e: e.tensor_tensor(out=OC.ap, in0=po3, in1=M1.ap.unsqueeze(2).broadcast_to([128, 6, 64]), op=ALU.add), r=[po, M1], w=[OC])
            C.op("act", lambda e: e.activation(out=SQ.ap, in_=OC.ap, func=AF.Square), r=[OC], w=[SQ])
            C.op("dve", lambda e: e.tensor_reduce(out=M2.ap, in_=SQ.ap, axis=AX.X, op=ALU.add), r=[SQ], w=[M2])
            C.op("dve", lambda e: e.tensor_scalar(out=M2.ap, in0=M2.ap, scalar1=1.0 / 64, scalar2=EPS, op0=ALU.mult, op1=ALU.add), r=[M2], w=[M2])
            C.op("act", lambda e: e.activation(out=M2.ap, in_=M2.ap, func=AF.Sqrt), r=[M2], w=[M2])
            C.op("dve", lambda e: e.reciprocal(out=M2.ap, in_=M2.ap), r=[M2], w=[M2])
            C.op("dve", lambda e: e.tensor_tensor(out=OC.ap, in0=OC.ap, in1=M2.ap.unsqueeze(2).broadcast_to([128, 6, 64]), op=ALU.mult), r=[OC, M2], w=[OC])
            C.op("dve", lambda e: e.tensor_tensor(out=ROUT.ap, in0=OC.ap.rearrange("p h d -> p (h d)"), in1=gg.ap[:, c, :], op=ALU.mult), r=[OC, gg], w=[ROUT])
            for j in range(3):
                C.op("pe", lambda e, j=j: e.transpose(pst.ap[:, j * 128:(j + 1) * 128], ROUT.ap[:, j * 128:(j + 1) * 128], identb.ap), r=[ROUT, cmatb], w=[pst], signal=(j == 2))
            C.op("act", lambda e: e.activation(out=mix.ap[:, 2:5, cs], in_=pst.ap[:, 0:384].rearrange("p (a b) -> p a b", a=3), func=AF.Copy), r=[pst], w=[mix])


        pend = []
        for c in range(4):
            cs = slice(c * 128, (c + 1) * 128)
            for hh in range(6):
                pp = ps[hh % 2]
                o = (hh // 2) * 128
                mm(pp, pp.ap[:, o:o + 128], kT, kT.ap[0:64, hh, cs], qd, qd.ap[0:64, hh, cs], True, True)
            for par in range(2):
                C.op("dve", lambda e, par=par: e.tensor_tensor(out=inn.ap[:, par], in0=ps[par].ap[:, 0:384].rearrange("p (a b) -> p a b", a=3), in1=ret_mask.ap[:, par], op=ALU.mult), r=[ps[par], rmaskb], w=[inn])
            po = ps[2] if c % 2 == 0 else ps[6]
            for hh in range(6):
                hs = slice(hh * 64, (hh + 1) * 64)
                mm(po, po.ap[:, hs], inn, inn.ap[:, hh % 2, hh // 2, :], vt, vt.ap[:, c, hs], True, False, signal=False)
                mm(po, po.ap[:, hs], qd, qd.ap[0:64, hh, cs], rstate_b, rstate_b.ap[0:64, hs], False, True, signal=(hh == 5))
            pS = ps[3]
            for hh in range(6):
                hs = slice(hh * 64, (hh + 1) * 64)
                mm(pS, pS.ap[0:64, hs], kdec, kdec.ap[:, c, hs], vt, vt.ap[:, c, hs], True, True, signal=(hh == 5))
            C.op("dve", lambda e: e.tensor_tensor(out=rstate.ap[0:64], in0=rstate.ap[0:64], in1=ret_cd.ap[0:64], op=ALU.mult), r=[rstate, rtab], w=[rstate])
            C.op("dve", lambda e: e.tensor_tensor(out=rstate.ap[0:64], in0=pS.ap[0:64, 0:384], in1=rstate.ap[0:64], op=ALU.add), r=[pS, rstate], w=[rstate])
            C.op("dve", lambda e: e.tensor_copy(out=rstate_b.ap[0:64], in_=rstate.ap[0:64]), r=[rstate], w=[rstate_b])
            if pend:
                pend.pop()()
            pend.append(lambda c=c, po=po: epilogue(c, po))
        pend.pop()()

    def diff_setup(l):
        lam_init = 0.8 - 0.6 * math.exp(-0.3 * l)
        T1 = carve(0, [128, 2, 32], F32)
        C.op("dve", lambda e: e.tensor_tensor(out=T1.ap[:, 0, :], in0=lqk.ap[:, 0, :], in1=lqk.ap[:, 1, :], op=ALU.mult), r=[lqk], w=[T1])
        C.op("dve", lambda e: e.tensor_tensor(out=T1.ap[:, 1, :], in0=lqk.ap[:, 2, :], in1=lqk.ap[:, 3, :], op=ALU.mult), r=[lqk], w=[T1])
        C.op("dve", lambda e: e.tensor_reduce(out=lam.ap[:, 0:2], in_=T1.ap, axis=AX.X, op=ALU.add), r=[T1], w=[lam])
        C.op("act", lambda e: e.activation(out=lam.ap[:, 0:2], in_=lam.ap[:, 0:2], func=AF.Exp), r=[lam], w=[lam])
        C.op("dve", lambda e: e.tensor_tensor(out=lam.ap[:, 2:3], in0=lam.ap[:, 0:1], in1=lam.ap[:, 1:2], op=ALU.subtract), r=[lam], w=[lam])
        C.op("dve", lambda e: e.tensor_scalar(out=lam.ap[:, 3:4], in0=lam.ap[:, 2:3], scalar1=lam_init, scalar2=-1.0, op0=ALU.add, op1=ALU.mult), r=[lam], w=[lam])
        C.op("dve", lambda e: e.tensor_scalar(out=subg.ap, in0=rows.ap[:, 1, :], scalar1=1.0 - lam_init, scalar2=None, op0=ALU.mult), r=[rows], w=[subg])

    def diff_block(l, b):
        qT = carve(0, [128, 4, BT], BF16)
        P = [carve(4096 + k * 2048, [128, 2, BT], BF16) for k in range(2)]
        dtok = carve(8192, [128, 4, 384], BF16)
        UU = carve(11264, [128, 4, 64], F32)
        TT = carve(12288, [128, 4, 64], F32)
        R0 = carve(13312, [128, 4], F32)
        R1 = carve(14336, [128, 4], F32)
        wq = next_slab("w_in")

        def evq(j, p):
            C.op("act", lambda e: e.activation(out=qT.ap[0:96, j, :], in_=p.ap[0:96, :], func=AF.Copy, scale=32.0 ** -0.5), r=[p], w=[qT])
        proj_fm(wq, [0, 96, 192, 288], evq, width=96)
        slab_done()
        wk = next_slab("w_in")

        def evk(j, p):
            C.op("act", lambda e: e.activation(out=kTc.ap[0:96, j, hsl(b)], in_=p.ap[0:96, :], func=AF.Copy), r=[p], w=[kTc])
        proj_fm(wk, [0, 96, 192, 288], evk, width=96)
        slab_done()
        wv = next_slab("w_in")

        def evv(c, p):
            C.op("act", lambda e: e.activation(out=vaug.ap[:, 4 * b + c, :, 0:64], in_=p.ap[:, 0:384].rearrange("p (h d) -> p h d", h=6), func=AF.Copy), r=[p], w=[vaug])
        proj_tm(wv, 384, evv)
        slab_done()
        nkt = 4 * b + 4
        its = [(hh, kt) for hh in range(6) for kt in range(nkt)]

        def scores(it):
            hh, kt = its[it]
            j = kt - 4 * b
            q0 = max(j, 0) * 128
            ks = slice(kt * 128, (kt + 1) * 128)
            pss = [ps[0], ps[1]] if it % 2 == 0 else [ps[4], ps[5]]
            for m in range(2):
                mi = 2 * hh + m
                tj, bp = mi // 3, 32 * (mi % 3)
                mm(pss[m], pss[m].ap[:, q0:BT], kTc, kTc.ap[bp:bp + 32, tj, ks], qT, qT.ap[bp:bp + 32, tj, q0:BT], True, True)

        acc = [ps[2], ps[3]]
        scores(0)
        for it, (hh, kt) in enumerate(its):
            W = (128, 256, 512, 512, 512, 512)[hh]
            j = kt - 4 * b
            q0 = max(j, 0) * 128
            Pk = P[it % 2]
            pss = [ps[0], ps[1]] if it % 2 == 0 else [ps[4], ps[5]]
            if it + 1 < len(its):
                scores(it + 1)
            if kt == 0:
                for m in range(2):
                    mm(acc[m], acc[m].ap[:, 0:260], zerob, zerob.ap[:, 0:128], zerob, zerob.ap[:, 0:260], True, False)
            for m in range(2):
                for s0 in range(0, BT, W):
                    lo = max(s0, q0)
                    if lo >= s0 + W:
                        continue
                    oi = kt - 4 * b - s0 // 128 + 16
                    C.op("act", lambda e, m=m, lo=lo, s0=s0, oi=oi, hh=hh, W=W, Pk=Pk, pss=pss: e.activation(out=Pk.ap[:, m, lo:s0 + W], in_=pss[m].ap[:, lo:s0 + W], func=AF.Exp,
                                                                                 bias=abias.ap[:, hh, oi:oi + 1]), r=[pss[m], abias], w=[Pk])
            if j >= 0:
                C.op("pool", lambda e, Pk=Pk, q0=q0: e.affine_select(out=Pk.ap[:, :, q0:q0 + 128], in_=Pk.ap[:, :, q0:q0 + 128], pattern=[[0, 2], [1, 128]],
                                                       compare_op=ALU.is_ge, fill=0.0, base=0, channel_multiplier=-1), r=[Pk], w=[Pk])
            last = (kt == nkt - 1)
            for m in range(2):
                for qt in range(max(j, 0), 4):
                    mm(acc[m], acc[m].ap[:, qt * 65:(qt + 1) * 65], Pk, Pk.ap[:, m, qt * 128:(qt + 1) * 128], vaug, vaug.ap[:, kt, hh, :], False, last and qt == 3,
                       signal=(qt == 3))
            if not last:
                continue
            a0 = acc[0].ap[:, 0:260].rearrange("p (q e) -> p q e", q=4)
            a1 = acc[1].ap[:, 0:260].rearrange("p (q e) -> p q e", q=4)
            C.op("dve", lambda e: e.reciprocal(out=R0.ap, in_=a0[:, :, 64]), r=[acc[0]], w=[R0])
            C.op("dve", lambda e: e.reciprocal(out=R1.ap, in_=a1[:, :, 64]), r=[acc[1]], w=[R1])
            C.op("dve", lambda e: e.tensor_scalar(out=R1.ap, in0=R1.ap, scalar1=lam.ap[:, 3:4], scalar2=None, op0=ALU.mult), r=[R1, lam], w=[R1])
            C.op("dve", lambda e: e.tensor_tensor(out=TT.ap, in0=a1[:, :, 0:64], in1=R1.ap.unsqueeze(2).broadcast_to([128, 4, 64]), op=ALU.mult), r=[acc[1], R1], w=[TT])
            C.op("dve", lambda e: e.tensor_tensor(out=UU.ap, in0=a0[:, :, 0:64], in1=R0.ap.unsqueeze(2).broadcast_to([128, 4, 64]), op=ALU.mult), r=[acc[0], R0], w=[UU])
            C.op("dve", lambda e: e.tensor_tensor(out=UU.ap, in0=UU.ap, in1=TT.ap, op=ALU.add), r=[UU, TT], w=[UU])
            C.op("act", lambda e: e.activation(out=TT.ap, in_=UU.ap, func=AF.Square), r=[UU], w=[TT])
            C.op("dve", lambda e: e.tensor_reduce(out=R0.ap, in_=TT.ap, axis=AX.X, op=ALU.add), r=[TT], w=[R0])
            C.op("dve", lambda e: e.tensor_scalar(out=R0.ap, in0=R0.ap, scalar1=1.0 / 64, scalar2=EPS, op0=ALU.mult, op1=ALU.add), r=[R0], w=[R0])
            C.op("act", lambda e: e.activation(out=R0.ap, in_=R0.ap, func=AF.Sqrt), r=[R0], w=[R0])
            C.op("dve", lambda e: e.reciprocal(out=R0.ap, in_=R0.ap), r=[R0], w=[R0])
            C.op("dve", lambda e: e.tensor_tensor(out=UU.ap, in0=UU.ap, in1=R0.ap.unsqueeze(2).broadcast_to([128, 4, 64]), op=ALU.mult), r=[UU, R0], w=[UU])
            C.op("dve", lambda e, hh=hh: e.tensor_tensor(out=dtok.ap[:, :, hh * 64:(hh + 1) * 64], in0=UU.ap,
                                                        in1=subg.ap[:, hh * 64:(hh + 1) * 64].unsqueeze(1).broadcast_to([128, 4, 64]), op=ALU.mult), r=[UU, subg], w=[dtok])
        for qt in range(4):
            for j in range(3):
                C.op("pe", lambda e, j=j, qt=qt: e.transpose(pst.ap[:, j * 128:(j + 1) * 128], dtok.ap[:, qt, j * 128:(j + 1) * 128], identb.ap), r=[dtok, cmatb], w=[pst], signal=(j == 2))
            C.op("act", lambda e, qt=qt: e.activation(out=mix.ap[:, 5:8, qt * 128:(qt + 1) * 128], in_=pst.ap[:, 0:384].rearrange("p (a b) -> p a b", a=3), func=AF.Copy), r=[pst], w=[mix])

    def ffn_block(l, b):
        rmsnorm(b, NL + l, hn, 22528, 24576)
        act = carve(0, [128, NCT, BT], BF16)
        c0b = [carve(22528 + k * 2048, [128, BT], F32) for k in range(3)] + [carve(28672, [128, BT], F32)]
        pend = []
        for g in range(NCT // 2):
            w = next_slab("w_up")
            for cl in range(2):
                ct = 2 * g + cl
                pa, pb = (ps[0], ps[1]) if ct % 2 == 0 else (ps[2], ps[3])
                cbuf = c0b[2 * (ct % 2):2 * (ct % 2) + 2]
                for m, p in enumerate((pa, pb)):
                    for kt in range(8):
                        mm(p, p.ap, w, w.ap[:, kt, m, cl * 128:(cl + 1) * 128], hn, hn.ap[:, kt, :], kt == 0, kt == 7)
                for m, p in enumerate((pa, pb)):
                    ci = m * NCT + ct
                    cc = cbuf[m]
                    C.op("act", lambda e, p=p, cc=cc, ci=ci: e.activation(out=cc.ap, in_=p.ap, func=AF.Identity, scale=cw.ap[:, 2, ci:ci + 1], bias=cw.ap[:, 3, ci:ci + 1]), r=[p, cw], w=[cc])
                    C.op("dve", lambda e, p=p, cc=cc, ci=ci: e.scalar_tensor_tensor(out=cc.ap[:, 1:BT], in0=p.ap[:, 0:BT - 1], scalar=cw.ap[:, 1, ci:ci + 1], in1=cc.ap[:, 1:BT], op0=ALU.mult, op1=ALU.add), r=[p, cw, cc], w=[cc])
                    C.op("dve", lambda e, p=p, cc=cc, ci=ci: e.scalar_tensor_tensor(out=cc.ap[:, 2:BT], in0=p.ap[:, 0:BT - 2], scalar=cw.ap[:, 0, ci:ci + 1], in1=cc.ap[:, 2:BT], op0=ALU.mult, op1=ALU.add), r=[p, cw, cc], w=[cc])
                    if b > 0:
                        C.op("dve", lambda e, cc=cc, ci=ci: e.scalar_tensor_tensor(out=cc.ap[:, 0:2], in0=halo.ap[:, ci, 0:2], scalar=cw.ap[:, 0, ci:ci + 1], in1=cc.ap[:, 0:2], op0=ALU.mult, op1=ALU.add), r=[halo, cw, cc], w=[cc])
                        C.op("dve", lambda e, cc=cc, ci=ci: e.scalar_tensor_tensor(out=cc.ap[:, 0:1], in0=halo.ap[:, ci, 1:2], scalar=cw.ap[:, 1, ci:ci + 1], in1=cc.ap[:, 0:1], op0=ALU.mult, op1=ALU.add), r=[halo, cw, cc], w=[cc])
                    if b < NBLK - 1:
                        C.op("dve", lambda e, p=p, ci=ci: e.tensor_copy(out=halo.ap[:, ci, :], in_=p.ap[:, BT - 2:BT]), r=[p], w=[halo])
                if pend:
                    pend.pop()()

                def fin(cbuf=cbuf, ct=ct):
                    C.op("act", lambda e: e.activation(out=cbuf[0].ap, in_=cbuf[0].ap, func=AF.Silu), r=[cbuf[0]], w=[cbuf[0]])
                    C.op("dve", lambda e: e.tensor_tensor(out=act.ap[:, ct, :], in0=cbuf[0].ap, in1=cbuf[1].ap, op=ALU.mult), r=[cbuf[0], cbuf[1]], w=[act])
                pend.append(fin)
            slab_done()
        pend.pop()()
        for dp in range(4):
            pd = [ps[4], ps[5]]
            for hf in range(2):
                w = next_slab("w_down")
                for dl in range(2):
                    for k in range(11):
                        mm(pd[dl], pd[dl].ap, w, w.ap[:, k, dl * 128:(dl + 1) * 128], act, act.ap[:, hf * 11 + k, :], hf == 0 and k == 0, hf == 1 and k == 10,
                           signal=(k == 10))
                slab_done()
            for dl in range(2):
                h_add(b, 2 * dp + dl, pd[dl])

    def ple_block(l, b):
        rmsnorm(b, 2 * NL + l, hn, 0, 2048)
        pf = carve(4096, [128, 2, BT], F32)
        pb = carve(8192, [128, 2, BT], BF16)
        sg = [carve(10240 + k * 2048, [128, BT], F32) for k in range(2)]
        C.dma("sp", pf.ap, pT_d[l, :, :, hsl(b)], w=[pf])
        C.op("pool", lambda e: e.tensor_copy(out=pb.ap, in_=pf.ap), r=[pf], w=[pb])
        for half in range(2):
            w = next_slab("w_pg")
            wpe = next_slab("w_pe")
            for dl in range(4):
                dt = 4 * half + dl
                pg, pp = (ps[0], ps[1]) if dt % 2 == 0 else (ps[2], ps[3])
                c0 = dl * 128
                for kt in range(8):
                    mm(pg, pg.ap, w, w.ap[:, kt, c0:c0 + 128], hn, hn.ap[:, kt, :], kt == 0, kt == 7)
                for kt in range(2):
                    mm(pp, pp.ap, wpe, wpe.ap[:, kt, c0:c0 + 128], pb, pb.ap[:, kt, :], kt == 0, kt == 1)
                s_ = sg[dt % 2]
                C.op("act", lambda e, s_=s_, pg=pg: e.activation(out=s_.ap, in_=pg.ap, func=AF.Sigmoid), r=[pg], w=[s_])
                C.op("dve", lambda e, s_=s_, pp=pp: e.tensor_tensor(out=s_.ap, in0=pp.ap, in1=s_.ap, op=ALU.mult), r=[pp, s_], w=[s_])
                C.op("pool", lambda e, s_=s_, dt=dt: e.tensor_tensor(out=h.ap[:, dt, hsl(b)], in0=s_.ap, in1=h.ap[:, dt, hsl(b)], op=ALU.add), r=[s_, h], w=[h])
            slab_done()
            slab_done()

    for l in range(nl):
        if l > 0:
            C.new_epoch()
        C.dma("sp", cw.ap, cw_d[:, l], w=[cw])
        C.dma("sp", rows.ap, rows_d[:, l], w=[rows])
        C.dma("sp", lqk.ap, lqk_d[:, l], w=[lqk])
        C.mark("setup")
        if on("s5"):
            s5_setup(l)
        if on("diff"):
            diff_setup(l)
        for b in range(nblk):
            C.mark("norm1")
            rmsnorm(b, l, hn, 26624, 28672)
            C.mark("s5")
            if on("s5"):
                s5_block(l, b)
            else:
                C.op("dve", lambda e: e.memset(mix.ap[:, 0:2, :], 0.0), w=[mix])
            C.mark("ret")
            if on("ret"):
                ret_block(l, b)
            else:
                C.op("dve", lambda e: e.memset(mix.ap[:, 2:5, :], 0.0), w=[mix])
            C.mark("diff")
            if on("diff"):
                diff_block(l, b)
            else:
                C.op("dve", lambda e: e.memset(mix.ap[:, 5:8, :], 0.0), w=[mix])
            if dbg and l == nl - 1:
                C.dma("sp", dbg_d["mix"][:, :, hsl(b)], mix.ap, r=[mix])
            C.mark("wout")
            for half in range(2):
                w = next_slab("w_out")
                for dl in range(4):
                    dt = 4 * half + dl
                    p = ps[4 + dt % 2]
                    c0 = dl * 128
                    for kt in range(8):
                        mm(p, p.ap, w, w.ap[:, kt, c0:c0 + 128], mix, mix.ap[:, kt, :], kt == 0, kt == 7)
                    h_add(b, dt, p)
                slab_done()
            C.mark("ffn")
            if on("ffn"):
                ffn_block(l, b)
            C.mark("ple")
            if on("ple"):
                ple_block(l, b)
            C.mark("end")
            if dbg:
                C.dma("sp", dbg_d["h%d" % l][:, :, hsl(b)], h.ap[:, :, hsl(b)], r=[h])
            if l == nl - 1:
                yo = carve(0, [128, 8, BT], F32)
                rmsnorm(b, 3 * NL, yo, 16384, 18432)
                C.dma("sp", yT_d[:, :, hsl(b)], yo.ap, r=[yo])
    deps = [(s_[0], s_[1]) for pl in C.dma_pool.values() for s_ in pl if s_[1] > 0]
    C._wait("sp", deps)
    build_program.marks = C.marks
    stuck, counts = C.simulate()
    print("instr counts", counts, "stuck", stuck)
    assert not stuck, stuck
    es.close()
    return nc


def host_layouts(inp):
    f = np.float32
    L = NL
    shared = {}
    for nm in ["w_in", "w_out", "w_up", "w_down", "w_pg", "w_pe"]:
        shared[nm] = np.ascontiguousarray(inp[nm], dtype=f)
    shared["w_glu"] = np.ascontiguousarray(np.asarray(inp["ssm_w_glu"], f).reshape(L, 2, 128, 256).transpose(0, 2, 1, 3))
    gl = [np.asarray(inp["norm1_g"], f), np.asarray(inp["norm2_g"], f), np.asarray(inp["norm3_g"], f)]
    gains = np.concatenate([g.reshape(L, 8, 128) for g in gl] + [np.asarray(inp["final_g"], f).reshape(1, 8, 128)], 0)
    shared["gains"] = np.ascontiguousarray(gains.transpose(2, 0, 1))
    cwv = np.concatenate([np.asarray(inp["conv_w"], f), np.asarray(inp["conv_b"], f)[:, None, :]], 1)
    shared["cw"] = np.ascontiguousarray(cwv.reshape(L, 4, 2 * NCT, 128).transpose(3, 0, 1, 2))
    lre = np.asarray(inp["ssm_lam_re"], f).reshape(L, 1024)
    lim = np.asarray(inp["ssm_lam_im"], f).reshape(L, 1024)
    ldt = np.repeat(np.asarray(inp["ssm_log_dt"], f), 64, axis=1)
    row = np.stack([lre, lim, ldt], 1)
    shared["s5row"] = np.ascontiguousarray(np.broadcast_to(row[:, None], (L, 128, 3, 1024)))
    bre = np.asarray(inp["ssm_b_re"], f)
    bim = np.asarray(inp["ssm_b_im"], f)
    s5b = np.zeros((L, 128, 2, 2, 512), f)
    for ri, bb in enumerate((bre, bim)):
        for g in range(16):
            i, gl_ = g // 8, g % 8
            s5b[:, gl_ * 16:(gl_ + 1) * 16, ri, i, gl_ * 64:(gl_ + 1) * 64] = bb[:, g].transpose(0, 2, 1)
    shared["s5b"] = s5b
    cre = np.asarray(inp["ssm_c_re"], f)
    cim = np.asarray(inp["ssm_c_im"], f)
    s5c = np.zeros((L, 128, 2, 8, 128), f)
    for ri, cc in enumerate((cre, cim)):
        for g in range(16):
            j, g2, gl_ = g // 2, g % 2, g % 8
            s5c[:, g2 * 64:(g2 + 1) * 64, ri, j, gl_ * 16:(gl_ + 1) * 16] = cc[:, g].transpose(0, 2, 1)
    shared["s5c"] = s5c
    dsk = np.asarray(inp["ssm_d"], f).reshape(L, 2, 128)
    bgl = np.asarray(inp["ssm_b_glu"], f).reshape(L, 2, 128)
    shared["s5db"] = np.ascontiguousarray(np.stack([dsk, bgl], 1).transpose(3, 0, 1, 2))
    rws = np.stack([np.asarray(inp["ret_gn_g"], f), np.asarray(inp["diff_subln_g"], f)], 1)
    shared["rows"] = np.ascontiguousarray(np.broadcast_to(rws[None], (128, L, 2, 384)))
    lq = np.stack([np.asarray(inp[k], f) for k in ("diff_lq1", "diff_lk1", "diff_lq2", "diff_lk2")], 1)
    shared["lqk"] = np.ascontiguousarray(np.broadcast_to(lq[None], (128, L, 4, 32)))
    cm = np.zeros((128, 2, 128), f)
    cm[:, 0] = np.eye(128, dtype=f)
    cm[:, 1] = np.triu(np.ones((128, 128), f))
    shared["cmat"] = cm
    lg = RET_LOG_GAMMA.astype(np.float64)
    pos = np.arange(128, dtype=np.float64)
    qtab = np.zeros((128, 6, 128))
    for j in range(6):
        qtab[:, j] = np.exp((pos + 1.0) * lg[j])[None, :]
    ktab = np.exp((127.0 - pos)[:, None] * lg[None, :]) * 0.125
    mask = np.zeros((128, 2, 3, 128))
    for hh in range(6):
        mask[:, hh % 2, hh // 2, :] = np.where(pos[None, :] >= pos[:, None], np.exp(-(pos[:, None] + 1.0) * lg[hh]) * 0.125, 0.0)
    cd = np.zeros((128, 384))
    for j in range(6):
        cd[:, j * 64:(j + 1) * 64] = np.exp(128.0 * lg[j])
    tp = np.stack([pos + 1.0, -(pos + 1.0)], 1)
    shared["rtab"] = np.concatenate([qtab.reshape(128, -1), ktab, cd, tp], 1).astype(f)
    shared["rmask"] = mask.astype(f)
    ab = np.zeros((128, 6, 20), np.float64)
    for hh in range(6):
        for o in range(20):
            ab[:, hh, o] = np.float64(ALIBI_SLOPES[hh]) * (128.0 * (o - 16) + np.arange(128))
    shared["abias"] = ab.astype(f)
    return shared


def kernel(**inputs):
    cfg = inputs.pop("_cfg", {})
    x = np.asarray(inputs["x"], np.float32)
    p = np.asarray(inputs["p"], np.float32)
    shared = host_layouts(inputs)
    nl_ = cfg.get("layers", NL)
    for nm in ["w_in", "w_out", "w_up", "w_down", "w_pg", "w_pe"]:
        shared[nm] = shared[nm][:nl_]
    nc = build_program(cfg)
    in_maps = []
    for c in range(8):
        m = dict(shared)
        m["xT"] = np.ascontiguousarray(x[c].T.reshape(8, 128, SEQ).transpose(1, 0, 2))
        m["pT"] = np.ascontiguousarray(p[:, c].transpose(0, 2, 1).reshape(NL, 2, 128, SEQ).transpose(0, 2, 1, 3))
        in_maps.append(m)
    res = run_bass_kernel_spmd(nc, in_maps, core_ids=list(range(8)))
    out = np.empty((8, SEQ, D), np.float32)
    for c in range(8):
        yT = np.asarray(res.results[c]["yT"], np.float32)
        out[c] = yT.transpose(1, 0, 2).reshape(D, SEQ).T
    if cfg.get("dbg"):
        kernel.last = res.results
    return out
```

```python
import math
from contextlib import ExitStack
import numpy as np
import concourse.bass as bass
import concourse.mybir as mybir
from concourse.bass_utils import run_bass_kernel_spmd

F32 = mybir.dt.float32
BF16 = mybir.dt.bfloat16
I32 = mybir.dt.int32
ALU = mybir.AluOpType
AF = mybir.ActivationFunctionType
AX = mybir.AxisListType

NL, D, SEQ = 4, 1024, 2048
BT, NBLK = 512, 4
PROJ = 2944
DFF = 2816
NCT = 22
EPS = 1e-6
RET_LOG_GAMMA = np.log1p(-(2.0 ** (-5.0 - np.arange(6)))).astype(np.float32)
ALIBI_SLOPES = (2.0 ** (-8.0 * (np.arange(6) + 1) / 6)).astype(np.float32)
TWO_PI = 2.0 * math.pi


class Buf:
    __slots__ = ("w", "r")

    def __init__(self):
        self.w = None
        self.r = {}


class V:
    __slots__ = ("ap", "bufs")

    def __init__(self, ap, bufs):
        self.ap = ap
        self.bufs = bufs


class Ctx:
    def __init__(self, nc, es, n_dma_sems=20):
        self.nc = nc
        self.es = es
        self.engs = {"pe": nc.tensor, "act": nc.scalar, "dve": nc.vector, "pool": nc.gpsimd, "sp": nc.sync}
        self.epoch = 0
        self.sems = {}
        self.cnt = {}
        self.seen = {k: {} for k in self.engs}
        self.semobj = {}
        self.dma_pool = {"sp": [], "pool": []}
        for i in range(n_dma_sems):
            s = es.enter_context(nc.semaphore("dsem%d" % i))
            self.semobj[("dma", i)] = s
            self.dma_pool["sp" if i < n_dma_sems - 6 else "pool"].append([("dma", i), 0])
        self.dma_next = {"sp": 0, "pool": 0}
        self.pending = {k: [] for k in self.engs}
        self.log = {k: [] for k in self.engs}
        self.npe = 0
        self.marks = []

    def mark(self, name):
        self.marks.append((name, self.npe))

    def simulate(self):
        val = {}
        ptr = {k: 0 for k in self.engs}
        prog = True
        while prog:
            prog = False
            for e, lg in self.log.items():
                while ptr[e] < len(lg):
                    kind, key, v = lg[ptr[e]]
                    if kind == "wait":
                        if val.get(key, 0) < v:
                            break
                    else:
                        val[key] = val.get(key, 0) + v
                    ptr[e] += 1
                    prog = True
        stuck = {e: (ptr[e], len(lg), lg[ptr[e]], val.get(lg[ptr[e]][1], 0)) for e, lg in self.log.items() if ptr[e] < len(lg)}
        return stuck, {e: len(lg) for e, lg in self.log.items()}

    def _sem(self, e):
        key = (e, self.epoch)
        if key not in self.semobj:
            self.semobj[key] = self.es.enter_context(self.nc.semaphore("s_%s_%d" % key))
            self.cnt[key] = 0
        return key

    def new_epoch(self):
        self.epoch += 1

    def _wait(self, e, deps):
        seen = self.seen[e]
        for key, val in deps:
            if e == "pe" and key[0] == "pe":
                continue
            if seen.get(key, 0) >= val:
                continue
            self.engs[e].wait_ge(self.semobj[key], val)
            self.log[e].append(("wait", key, val))
            seen[key] = val

    def _deps(self, r, w):
        deps = []
        for v in r:
            for b in v.bufs:
                if b.w is not None:
                    deps.append(b.w)
        for v in w:
            for b in v.bufs:
                if b.w is not None:
                    deps.append(b.w)
                deps.extend(b.r.items())
        return deps

    def _mark(self, tok, r, w):
        key, val = tok
        for v in r:
            for b in v.bufs:
                if b.r.get(key, 0) < val:
                    b.r[key] = val
        for v in w:
            for b in v.bufs:
                b.w = tok
                b.r = {}

    def op(self, e, fn, r=(), w=(), signal=True, selfwait=False):
        self._wait(e, self._deps(r, w))
        if selfwait:
            key0 = self._sem(e)
            if self.cnt[key0] > 0:
                self.engs[e].wait_ge(self.semobj[key0], self.cnt[key0])
                self.log[e].append(("wait", key0, self.cnt[key0]))
        inst = fn(self.engs[e])
        if e == "pe":
            self.npe += 1
        key = self._sem(e)
        if signal:
            self.cnt[key] += 1
            inst.then_inc(self.semobj[key], 1)
            self.log[e].append(("inc", key, 1))
            tok = (key, self.cnt[key])
        else:
            tok = (key, self.cnt[key] + 1)
        self._mark(tok, r, w)
        return inst

    def dma(self, e, out, in_, r=(), w=()):
        pool = self.dma_pool[e]
        slot = pool[self.dma_next[e]]
        self.dma_next[e] = (self.dma_next[e] + 1) % len(pool)
        key, val = slot
        deps = self._deps(r, w)
        if val > 0:
            deps.append((key, val))
        self._wait(e, deps)
        inst = self.engs[e].dma_start(out=out, in_=in_)
        slot[1] = val + 16
        inst.then_inc(self.semobj[key], 16)
        self.log[e].append(("inc", key, 16))
        self._mark((key, val + 16), r, w)
        return inst

    def wait_all(self, e, views):
        deps = []
        for v in views:
            for b in v.bufs:
                if b.w is not None:
                    deps.append(b.w)
        self._wait(e, deps)


def build_program(cfg):
    nl = cfg.get("layers", NL)
    nblk = cfg.get("blocks", NBLK)
    on = lambda k: cfg.get(k, True)
    dbg = cfg.get("dbg", False)
    nc = bass.Bass("TRN2", target_bir_lowering=False)
    es = ExitStack()
    dram_in = {}

    def din(name, shape, dt=F32):
        dram_in[name] = nc.dram_tensor(name, list(shape), dt, kind="ExternalInput").ap()
        return dram_in[name]

    xT_d = din("xT", [128, 8, SEQ])
    pT_d = din("pT", [NL, 128, 2, SEQ])
    w_in_d = din("w_in", [nl, D, PROJ])
    w_out_d = din("w_out", [nl, D, D])
    w_up_d = din("w_up", [nl, D, 2 * DFF])
    w_down_d = din("w_down", [nl, DFF, D])
    w_pg_d = din("w_pg", [nl, D, D])
    w_pe_d = din("w_pe", [nl, 256, D])
    w_glu_d = din("w_glu", [NL, 128, 2, 256])
    gains_d = din("gains", [128, 3 * NL + 1, 8])
    cw_d = din("cw", [128, NL, 4, 2 * NCT])
    s5row_d = din("s5row", [NL, 128, 3, 1024])
    s5b_d = din("s5b", [NL, 128, 2, 2, 512])
    s5c_d = din("s5c", [NL, 128, 2, 8, 128])
    s5db_d = din("s5db", [128, NL, 2, 2])
    rows_d = din("rows", [128, NL, 2, 384])
    lqk_d = din("lqk", [128, NL, 4, 32])
    cmat_d = din("cmat", [128, 2, 128])
    rtab_d = din("rtab", [128, 6 * 128 + 6 + 384 + 2])
    rmask_d = din("rmask", [128, 2, 3, 128])
    abias_d = din("abias", [128, 6, 20])
    yT_d = nc.dram_tensor("yT", [128, 8, SEQ], F32, kind="ExternalOutput").ap()
    dbg_d = {}
    if dbg:
        for l in range(nl):
            dbg_d["h%d" % l] = nc.dram_tensor("dbg_h%d" % l, [128, 8, SEQ], F32, kind="ExternalOutput").ap()
        dbg_d["mix"] = nc.dram_tensor("dbg_mix", [128, 8, SEQ], BF16, kind="ExternalOutput").ap()
    wb = {}
    for nm, shp in [("w_in", [nl, D, PROJ]), ("w_out", [nl, D, D]), ("w_up", [nl, D, 2 * DFF]),
                    ("w_down", [nl, DFF, D]), ("w_pg", [nl, D, D]), ("w_pe", [nl, 256, D])]:
        wb[nm] = nc.dram_tensor("wb_" + nm, shp, BF16, kind="Internal").ap()
    wsrc = {"w_in": w_in_d, "w_out": w_out_d, "w_up": w_up_d, "w_down": w_down_d, "w_pg": w_pg_d, "w_pe": w_pe_d}

    C = Ctx(nc, es)

    def sb(name, shape, dt):
        t = es.enter_context(nc.sbuf_tensor("sb_" + name, list(shape), dt))
        return V(t[:], [Buf()])

    def sub(v, ap):
        return V(ap, v.bufs)

    h = sb("h", [128, 8, SEQ], F32)
    hblk = [[Buf() for _ in range(NBLK)]]
    hn = sb("hn", [128, 8, BT], BF16)
    mix = sb("mix", [128, 8, BT], BF16)
    gains = sb("gains", [128, 3 * NL + 1, 8], F32)
    cw = sb("cw", [128, 4, 2 * NCT], F32)
    s5db = sb("s5db", [128, NL, 2, 2], F32)
    rows = sb("rows", [128, 2, 384], F32)
    lqk = sb("lqk", [128, 4, 32], F32)
    cmatf = sb("cmatf", [128, 2, 128], F32)
    cmatb = sb("cmatb", [128, 2, 128], BF16)
    rtab = sb("rtab", [128, 6 * 128 + 6 + 384 + 2], F32)
    rmaskb = sb("rmaskb", [128, 2, 3, 128], BF16)
    abias = sb("abias", [128, 6, 20], F32)
    onesb = sb("onesb", [128, 128], BF16)
    zerob = sb("zerob", [128, 260], BF16)
    halo = sb("halo", [128, 2 * NCT, 2], F32)
    winv = sb("winv", [128, 2, 1024], BF16)
    wtf = sb("wtf", [128, 2, 8, 128], BF16)
    wlast = sb("wlast", [128, 2, 8], F32)
    bbbd = sb("bbbd", [128, 2, 1024], BF16)
    ctb = sb("ctb", [128, 2, 8, 128], BF16)
    carry = [sb("carry%d" % i, [128, 2, 4], F32) for i in range(2)]
    wglu = sb("wglu", [128, 2, 256], BF16)
    rstate = sb("rstate", [128, 384], F32)
    rstate_b = sb("rstate_b", [128, 384], BF16)
    kTc = sb("kTc", [128, 4, SEQ], BF16)
    vaug = sb("vaug", [128, 16, 6, 65], BF16)
    lam = sb("lam", [128, 4], F32)
    subg = sb("subg", [128, 384], F32)
    NSLOT = 3
    SLOT_ELEMS = 4096
    slots = [sb("slot%d" % i, [128, SLOT_ELEMS], BF16) for i in range(NSLOT)]
    SCR_BYTES = 38 * 1024
    scr_t = es.enter_context(nc.sbuf_tensor("scr", [128, SCR_BYTES // 2], BF16))
    CELL = 1024
    scr_cells = [Buf() for _ in range(SCR_BYTES // CELL)]

    def carve(off, shape, dt):
        esz = 4 if dt in (F32, I32) else 2
        n = int(np.prod(shape[1:]))
        nb = n * esz
        assert off % 4 == 0 and off + nb <= SCR_BYTES, (off, nb)
        ap = scr_t[0:shape[0], off // 2:(off + nb) // 2]
        if esz == 4:
            ap = ap.bitcast(dt)
        if len(shape) == 3:
            ap = ap.rearrange("p (a b) -> p a b", a=shape[1])
        elif len(shape) == 4:
            ap = ap.rearrange("p (a b c) -> p a b c", a=shape[1], b=shape[2])
        return V(ap, scr_cells[off // CELL:(off + nb + CELL - 1) // CELL])

    ps = []
    for i in range(7):
        t = es.enter_context(nc.psum_tensor("ps%d" % i, [128, 512], F32))
        ps.append(V(t[:], [Buf()]))
    pst_t = es.enter_context(nc.psum_tensor("pst", [128, 1024], BF16))
    pst = V(pst_t[:], [Buf()])

    cast_sems = [es.enter_context(nc.semaphore("cast%d" % l)) for l in range(nl)]
    cast_tot = [0] * nl

    def emit_cast(l):
        for nm in ["w_in", "w_out", "w_up", "w_down", "w_pg", "w_pe"]:
            K, N = wsrc[nm].shape[1], wsrc[nm].shape[2]
            for r0 in range(0, K, 512):
                r1 = min(K, r0 + 512)
                for c0 in range(0, N, 2048):
                    c1 = min(N, c0 + 2048)
                    nc.gpsimd.dma_start(out=wb[nm][l, r0:r1, c0:c1], in_=wsrc[nm][l, r0:r1, c0:c1]).then_inc(cast_sems[l], 16)

    for l in range(nl):
        for nm in ["w_in", "w_out", "w_up", "w_down", "w_pg", "w_pe"]:
            K_, N_ = wsrc[nm].shape[1], wsrc[nm].shape[2]
            cast_tot[l] += 16 * len(range(0, K_, 512)) * len(range(0, N_, 2048))
    emit_cast(0)

    def load(dst, src):
        C.dma("sp", dst.ap, src, w=[dst])

    load(gains, gains_d)
    load(s5db, s5db_d)
    load(cmatf, cmat_d)
    load(rtab, rtab_d)
    for kt in range(8):
        C.dma("sp", h.ap[:, kt, :], xT_d[:, kt, :], w=[h])
    C.op("dve", lambda e: e.tensor_copy(out=cmatb.ap, in_=cmatf.ap), r=[cmatf], w=[cmatb])
    C.op("dve", lambda e: e.memset(onesb.ap, 1.0), w=[onesb])
    C.op("dve", lambda e: e.memset(zerob.ap, 0.0), w=[zerob])
    C.op("dve", lambda e: e.memset(vaug.ap, 1.0), w=[vaug])
    identb = sub(cmatb, cmatb.ap[:, 0, :])
    trib = sub(cmatb, cmatb.ap[:, 1, :])
    identf = sub(cmatf, cmatf.ap[:, 0, :])
    load(abias, abias_d)
    tmp = carve(16384, [128, 2, 3, 128], F32)
    C.dma("sp", tmp.ap, rmask_d, w=[tmp])
    C.op("dve", lambda e: e.tensor_copy(out=rmaskb.ap, in_=tmp.ap), r=[tmp], w=[rmaskb])
    o0 = 0
    ret_qtab = sub(rtab, rtab.ap[:, o0:o0 + 768].rearrange("p (a b) -> p a b", a=6)); o0 += 768
    ret_ktab = sub(rtab, rtab.ap[:, o0:o0 + 6]); o0 += 6
    ret_mask = rmaskb
    ret_cd = sub(rtab, rtab.ap[:, o0:o0 + 384]); o0 += 384
    tpos = sub(rtab, rtab.ap[:, o0:o0 + 1])
    tneg = sub(rtab, rtab.ap[:, o0 + 1:o0 + 2])

    slab_state = {"i": 0, "queue": [], "issued": 0}

    def slab_specs(l):
        sp = []
        if on("s5"):
            sp.append(("w_in", l, 8, 0, 256))
        if on("ret"):
            for c0 in (256, 640, 1024, 1408):
                sp.append(("w_in", l, 8, c0, c0 + 384))
        if on("diff"):
            for c0 in (1792, 2176, 2560):
                sp.append(("w_in", l, 8, c0, c0 + 384))
        for c0 in (0, 512):
            sp.append(("w_out", l, 8, c0, c0 + 512))
        if on("ffn"):
            for g in range(NCT // 2):
                sp.append(("w_up", l, 8, g, None))
            for dp in range(4):
                for hf in range(2):
                    sp.append(("w_down", l, 11, dp, hf))
        if on("ple"):
            for c0 in (0, 512):
                sp.append(("w_pg", l, 8, c0, c0 + 512))
                sp.append(("w_pe", l, 2, c0, c0 + 512))
        return sp

    all_specs = []
    for l in range(nl):
        for b in range(nblk):
            all_specs.extend(slab_specs(l))
    cast_waited = set()

    def issue_slab(idx):
        nm, l, kt, a, b = all_specs[idx]
        slot = slots[idx % NSLOT]
        if l not in cast_waited:
            nc.sync.wait_ge(cast_sems[l], cast_tot[l])
            cast_waited.add(l)
        if nm == "w_up":
            g = a
            dst = slot.ap[:, 0:8 * 512].rearrange("p (k m c) -> p k m c", k=8, m=2)
            for m in range(2):
                c0 = m * DFF + g * 256
                src = wb[nm][l, :, c0:c0 + 256].rearrange("(k p) n -> p k n", p=128)
                C.dma("sp", dst[:, :, m, :], src, w=[slot])
        elif nm == "w_down":
            dp, hf = a, b
            dst = slot.ap[:, 0:11 * 256].rearrange("p (k c) -> p k c", k=11)
            src = wb[nm][l, hf * 1408:(hf + 1) * 1408, dp * 256:(dp + 1) * 256].rearrange("(k p) n -> p k n", p=128)
            C.dma("sp", dst, src, w=[slot])
        else:
            n = b - a
            dst = slot.ap[:, 0:kt * n].rearrange("p (k c) -> p k c", k=kt)
            src = wb[nm][l, :, a:b].rearrange("(k p) n -> p k n", p=128)
            C.dma("sp", dst, src, w=[slot])

    def next_slab(expect):
        i = slab_state["i"]
        assert all_specs[i][0] == expect, (all_specs[i], expect)
        assert i < slab_state["issued"], "slab not issued (too many held)"
        slab_state["i"] = i + 1
        nm, l, kt, a, b = all_specs[i]
        slot = slots[i % NSLOT]
        if nm == "w_up":
            return sub(slot, slot.ap[:, 0:8 * 512].rearrange("p (k m c) -> p k m c", k=8, m=2))
        if nm == "w_down":
            return sub(slot, slot.ap[:, 0:11 * 256].rearrange("p (k c) -> p k c", k=11))
        n = b - a
        return sub(slot, slot.ap[:, 0:kt * n].rearrange("p (k c) -> p k c", k=kt))

    def slab_done():
        if slab_state["issued"] < len(all_specs):
            issue_slab(slab_state["issued"])
            slab_state["issued"] += 1

    for _ in range(min(NSLOT, len(all_specs))):
        issue_slab(slab_state["issued"])
        slab_state["issued"] += 1

    def mm(out_v, out_ap, lhsT_v, lhsT_ap, rhs_v, rhs_ap, start, stop, signal=None, selfwait=False):
        if signal is None:
            signal = True
        C.op("pe", lambda e: e.matmul(out_ap, lhsT_ap, rhs_ap, start=start, stop=stop),
             r=[lhsT_v, rhs_v], w=[out_v], signal=signal, selfwait=selfwait)

    def hsl(b):
        return slice(b * BT, (b + 1) * BT)

    def rmsnorm(b, gidx, dst, scrA, scrB):
        pss = ps[6]
        sq = [carve(scrA, [128, BT], BF16), carve(scrA + 1024, [128, BT], BF16)]
        for kt in range(8):
            s = sq[kt % 2]
            C.op("act", lambda e, s=s, kt=kt: e.activation(out=s.ap, in_=h.ap[:, kt, hsl(b)], func=AF.Square), r=[h], w=[s])
            mm(pss, pss.ap, onesb, onesb.ap, s, s.ap, kt == 0, kt == 7)
        rs = carve(scrB, [128, BT], F32)
        C.op("act", lambda e: e.activation(out=rs.ap, in_=pss.ap, func=AF.Ln, scale=1.0 / D, bias=EPS), r=[pss], w=[rs])
        C.op("act", lambda e: e.activation(out=rs.ap, in_=rs.ap, func=AF.Exp, scale=-0.5), r=[rs], w=[rs])
        for kt in range(8):
            C.op("dve", lambda e, kt=kt: e.scalar_tensor_tensor(out=dst.ap[:, kt, :], in0=h.ap[:, kt, hsl(b)],
                                                               scalar=gains.ap[:, gidx, kt:kt + 1], in1=rs.ap,
                                                               op0=ALU.mult, op1=ALU.mult), r=[h, gains, rs], w=[dst])

    def proj_fm(w, cols, evac, width=128):
        for j, c0 in enumerate(cols):
            p = ps[4 + (j % 2)]
            for kt in range(8):
                mm(p, p.ap[0:width, :], w, w.ap[:, kt, c0:c0 + width], hn, hn.ap[:, kt, :], kt == 0, kt == 7)
            evac(j, p)

    def proj_tm(w, ncols, evac):
        for c in range(4):
            p = ps[4 + (c % 2)]
            for kt in range(8):
                mm(p, p.ap[:, 0:ncols], hn, hn.ap[:, kt, c * 128:(c + 1) * 128], w, w.ap[:, kt, 0:ncols], kt == 0, kt == 7)
            evac(c, p)

    def h_add(b, dt, p):
        C.op("dve", lambda e: e.tensor_tensor(out=h.ap[:, dt, hsl(b)], in0=p.ap, in1=h.ap[:, dt, hsl(b)], op=ALU.add), r=[p, h], w=[h])

    def s5_setup(l):
        CTf = carve(0, [128, 2, 8, 128], F32)
        C.dma("sp", CTf.ap, s5c_d[l], w=[CTf])
        C.op("dve", lambda e: e.tensor_copy(out=ctb.ap[:, 0], in_=CTf.ap[:, 0]), r=[CTf], w=[ctb])
        C.op("dve", lambda e: e.tensor_scalar(out=ctb.ap[:, 1], in0=CTf.ap[:, 1], scalar1=-1.0, scalar2=None, op0=ALU.mult), r=[CTf], w=[ctb])
        WG = carve(8192, [128, 2, 256], F32)
        C.dma("sp", WG.ap, w_glu_d[l], w=[WG])
        C.op("dve", lambda e: e.tensor_copy(out=wglu.ap, in_=WG.ap), r=[WG], w=[wglu])
        for i in range(2):
            C.op("dve", lambda e, i=i: e.memset(carry[i].ap, 0.0), w=[carry[i]])
        for hf in range(2):
            s5_setup_half(l, hf)

    def s5_setup_half(l, hf):
        HS = slice(hf * 512, (hf + 1) * 512)
        T = [carve(i * 2048, [128, 512], F32) for i in range(9)]
        LR, LI, A, B, Cc, Dd, E, Fv, G = T
        TI = carve(9 * 2048, [128, 512], I32)
        MK = carve(10 * 2048, [128, 512], F32)
        C.dma("sp", LR.ap, s5row_d[l, :, 0, HS], w=[LR])
        C.dma("sp", LI.ap, s5row_d[l, :, 1, HS], w=[LI])
        C.dma("sp", A.ap, s5row_d[l, :, 2, HS], w=[A])

        def tt(eng, o, a, b_, op):
            C.op(eng, lambda e: e.tensor_tensor(out=o.ap, in0=a.ap, in1=b_.ap, op=op), r=[a, b_], w=[o])

        def act(o, a, func, **kw):
            C.op("act", lambda e: e.activation(out=o.ap, in_=a.ap, func=func, **kw), r=[a], w=[o])

        def sincos(phi, s_out, c_out):
            C.op("dve", lambda e: e.tensor_scalar(out=TI.ap, in0=phi.ap, scalar1=1.0 / TWO_PI, scalar2=None, op0=ALU.mult), r=[phi], w=[TI])
            C.op("dve", lambda e: e.tensor_copy(out=MK.ap, in_=TI.ap), r=[TI], w=[MK])
            C.op("dve", lambda e: e.scalar_tensor_tensor(out=s_out.ap, in0=MK.ap, scalar=-TWO_PI, in1=phi.ap, op0=ALU.mult, op1=ALU.add), r=[MK, phi], w=[s_out])
            C.op("dve", lambda e: e.tensor_scalar(out=MK.ap, in0=s_out.ap, scalar1=math.pi, scalar2=-TWO_PI, op0=ALU.is_gt, op1=ALU.mult), r=[s_out], w=[MK])
            tt("dve", s_out, s_out, MK, ALU.add)
            C.op("dve", lambda e: e.tensor_scalar(out=MK.ap, in0=s_out.ap, scalar1=-math.pi, scalar2=TWO_PI, op0=ALU.is_lt, op1=ALU.mult), r=[s_out], w=[MK])
            tt("dve", s_out, s_out, MK, ALU.add)
            C.op("dve", lambda e: e.tensor_scalar(out=MK.ap, in0=s_out.ap, scalar1=math.pi / 2, scalar2=-TWO_PI, op0=ALU.is_gt, op1=ALU.mult), r=[s_out], w=[MK])
            C.op("dve", lambda e: e.scalar_tensor_tensor(out=c_out.ap, in0=s_out.ap, scalar=math.pi / 2, in1=MK.ap, op0=ALU.add, op1=ALU.add), r=[s_out, MK], w=[c_out])
            act(s_out, s_out, AF.Sin)
            act(c_out, c_out, AF.Sin)

        act(A, A, AF.Exp)
        tt("dve", B, LR, A, ALU.mult)
        tt("dve", Cc, LI, A, ALU.mult)
        sincos(Cc, A, Dd)
        act(E, B, AF.Exp)
        tt("dve", Dd, E, Dd, ALU.mult)
        tt("dve", A, E, A, ALU.mult)
        C.op("dve", lambda e: e.tensor_scalar(out=Dd.ap, in0=Dd.ap, scalar1=-1.0, scalar2=None, op0=ALU.add), r=[Dd], w=[Dd])
        tt("dve", E, LR, LR, ALU.mult)
        tt("dve", Fv, LI, LI, ALU.mult)
        tt("dve", E, E, Fv, ALU.add)
        C.op("dve", lambda e: e.reciprocal(out=E.ap, in_=E.ap), r=[E], w=[E])
        tt("dve", Fv, Dd, LR, ALU.mult)
        tt("dve", G, A, LI, ALU.mult)
        tt("dve", Fv, Fv, G, ALU.add)
        tt("dve", Fv, Fv, E, ALU.mult)
        tt("dve", G, A, LR, ALU.mult)
        tt("dve", A, Dd, LI, ALU.mult)
        tt("dve", G, G, A, ALU.subtract)
        tt("dve", G, G, E, ALU.mult)
        BB = carve(0, [128, 2, 512], F32)
        C.dma("sp", BB.ap, s5b_d[l, :, :, hf, :], w=[BB])
        t1, t2 = A, Dd
        C.op("dve", lambda e: e.tensor_tensor(out=t1.ap, in0=Fv.ap, in1=BB.ap[:, 0, :], op=ALU.mult), r=[Fv, BB], w=[t1])
        C.op("dve", lambda e: e.tensor_tensor(out=t2.ap, in0=G.ap, in1=BB.ap[:, 1, :], op=ALU.mult), r=[G, BB], w=[t2])
        C.op("dve", lambda e: e.tensor_tensor(out=bbbd.ap[:, hf, 0:512], in0=t1.ap, in1=t2.ap, op=ALU.subtract), r=[t1, t2], w=[bbbd])
        C.op("dve", lambda e: e.tensor_tensor(out=t1.ap, in0=Fv.ap, in1=BB.ap[:, 1, :], op=ALU.mult), r=[Fv, BB], w=[t1])
        C.op("dve", lambda e: e.tensor_tensor(out=t2.ap, in0=G.ap, in1=BB.ap[:, 0, :], op=ALU.mult), r=[G, BB], w=[t2])
        C.op("dve", lambda e: e.tensor_tensor(out=bbbd.ap[:, hf, 512:1024], in0=t1.ap, in1=t2.ap, op=ALU.add), r=[t1, t2], w=[bbbd])
        C.op("dve", lambda e: e.tensor_scalar(out=A.ap, in0=Cc.ap, scalar1=tpos.ap, scalar2=None, op0=ALU.mult), r=[Cc, rtab], w=[A])
        sincos(A, Dd, E)
        C.op("act", lambda e: e.activation(out=Fv.ap, in_=B.ap, func=AF.Exp, scale=tpos.ap), r=[B, rtab], w=[Fv])
        C.op("act", lambda e: e.activation(out=G.ap, in_=B.ap, func=AF.Exp, scale=tneg.ap), r=[B, rtab], w=[G])
        C.op("dve", lambda e: e.tensor_tensor(out=winv.ap[:, 0, HS], in0=G.ap, in1=E.ap, op=ALU.mult), r=[G, E], w=[winv])
        C.op("dve", lambda e: e.scalar_tensor_tensor(out=winv.ap[:, 1, HS], in0=G.ap, scalar=-1.0, in1=Dd.ap, op0=ALU.mult, op1=ALU.mult), r=[G, Dd], w=[winv])
        tt("dve", E, Fv, E, ALU.mult)
        tt("dve", Dd, Fv, Dd, ALU.mult)
        for ri, src in enumerate((E, Dd)):
            for jl in range(4):
                j = 4 * hf + jl
                p = ps[jl % 2]
                C.op("pe", lambda e, p=p, src=src, jl=jl: e.transpose(p.ap[:, 0:128], src.ap[:, jl * 128:(jl + 1) * 128], identf.ap), r=[src, cmatf], w=[p])
                C.op("act", lambda e, p=p, ri=ri, j=j: e.activation(out=wtf.ap[:, ri, j, :], in_=p.ap[:, 0:128], func=AF.Copy), r=[p], w=[wtf])
                C.op("dve", lambda e, p=p, ri=ri, j=j: e.tensor_copy(out=wlast.ap[:, ri, j:j + 1], in_=p.ap[:, 127:128]), r=[p], w=[wlast])

    def s5_block(l, b):
        w = next_slab("w_in")
        uT = carve(0, [128, 2, BT], BF16)
        Tm = [carve(2048 + k * 1024, [128, BT], BF16) for k in range(4)]
        Z = carve(6144, [128, 2, BT], BF16)
        SC = carve(8192, [128, 2, 4, 128], F32)
        U = [carve(12288 + k * 1024, [128, 4, 128], BF16) for k in range(4)]
        CT4 = [carve(16384 + k * 1024, [128, 4], F32) for k in range(4)]
        X = carve(20480, [128, 2, 4, 128], BF16)
        YV = carve(22528, [128, BT], F32)
        G1 = carve(24576, [128, BT], F32)
        ZB = carve(26624, [128, 2, BT], BF16)

        def ev(j, p):
            C.op("act", lambda e: e.activation(out=uT.ap[:, j, :], in_=p.ap, func=AF.Copy), r=[p], w=[uT])
        proj_fm(w, [0, 128], ev)
        slab_done()
        for c in range(4):
            cs = slice(c * 128, (c + 1) * 128)
            for i in range(2):
                mm(ps[0], ps[0].ap, uT, uT.ap[:, i, cs], bbbd, bbbd.ap[:, i, 0:512], True, True)
                mm(ps[1], ps[1].ap, uT, uT.ap[:, i, cs], bbbd, bbbd.ap[:, i, 512:1024], True, True)
                isl = slice(i * 512, (i + 1) * 512)
                for k, (pp, ri) in enumerate(((0, 0), (1, 1), (0, 1), (1, 0))):
                    C.op("dve", lambda e, k=k, pp=pp, ri=ri: e.tensor_tensor(out=Tm[k].ap, in0=ps[pp].ap, in1=winv.ap[:, ri, isl], op=ALU.mult),
                         r=[ps[pp], winv], w=[Tm[k]])
                C.op("dve", lambda e: e.tensor_tensor(out=Z.ap[:, 0, :], in0=Tm[0].ap, in1=Tm[1].ap, op=ALU.subtract), r=[Tm[0], Tm[1]], w=[Z])
                C.op("dve", lambda e: e.tensor_tensor(out=Z.ap[:, 1, :], in0=Tm[2].ap, in1=Tm[3].ap, op=ALU.add), r=[Tm[2], Tm[3]], w=[Z])
                for ri in range(2):
                    for jl in range(4):
                        mm(ps[2 + ri], ps[2 + ri].ap[:, jl * 128:(jl + 1) * 128], Z, Z.ap[:, ri, jl * 128:(jl + 1) * 128], trib, trib.ap, True, True)
                for ri in range(2):
                    C.op("dve", lambda e, ri=ri: e.tensor_tensor(out=SC.ap[:, ri], in0=ps[2 + ri].ap.rearrange("p (a b) -> p a b", a=4),
                                                                 in1=carry[i].ap[:, ri, :].unsqueeze(2).broadcast_to([128, 4, 128]), op=ALU.add),
                         r=[ps[2 + ri], carry[i]], w=[SC])
                wr = wtf.ap[:, 0, 4 * i:4 * i + 4, :]
                wi = wtf.ap[:, 1, 4 * i:4 * i + 4, :]
                C.op("dve", lambda e: e.tensor_tensor(out=U[0].ap, in0=SC.ap[:, 0], in1=wr, op=ALU.mult), r=[SC, wtf], w=[U[0]])
                C.op("dve", lambda e: e.tensor_tensor(out=U[1].ap, in0=SC.ap[:, 1], in1=wi, op=ALU.mult), r=[SC, wtf], w=[U[1]])
                C.op("dve", lambda e: e.tensor_tensor(out=U[2].ap, in0=SC.ap[:, 0], in1=wi, op=ALU.mult), r=[SC, wtf], w=[U[2]])
                C.op("dve", lambda e: e.tensor_tensor(out=U[3].ap, in0=SC.ap[:, 1], in1=wr, op=ALU.mult), r=[SC, wtf], w=[U[3]])
                C.op("pool", lambda e: e.tensor_tensor(out=X.ap[:, 0], in0=U[0].ap, in1=U[1].ap, op=ALU.subtract), r=[U[0], U[1]], w=[X])
                C.op("pool", lambda e: e.tensor_tensor(out=X.ap[:, 1], in0=U[2].ap, in1=U[3].ap, op=ALU.add), r=[U[2], U[3]], w=[X])
                for k, (ra, rb) in enumerate(((0, 0), (1, 1), (0, 1), (1, 0))):
                    C.op("dve", lambda e, k=k, ra=ra, rb=rb: e.tensor_tensor(out=CT4[k].ap, in0=SC.ap[:, ra, :, 127], in1=wlast.ap[:, rb, 4 * i:4 * i + 4], op=ALU.mult), r=[SC, wlast], w=[CT4[k]])
                C.op("dve", lambda e: e.tensor_tensor(out=carry[i].ap[:, 0, :], in0=CT4[0].ap, in1=CT4[1].ap, op=ALU.subtract), r=[CT4[0], CT4[1]], w=[carry[i]])
                C.op("dve", lambda e: e.tensor_tensor(out=carry[i].ap[:, 1, :], in0=CT4[2].ap, in1=CT4[3].ap, op=ALU.add), r=[CT4[2], CT4[3]], w=[carry[i]])
                py = ps[4 + i]
                for jl in range(4):
                    for ri in range(2):
                        mm(py, py.ap[:, cs], ctb, ctb.ap[:, ri, 4 * i + jl, :], X, X.ap[:, ri, jl, :], jl == 0 and ri == 0, jl == 3 and ri == 1)
        for i in range(2):
            py = ps[4 + i]
            C.op("dve", lambda e: e.scalar_tensor_tensor(out=YV.ap, in0=uT.ap[:, i, :], scalar=s5db.ap[:, l, 0, i:i + 1], in1=py.ap, op0=ALU.mult, op1=ALU.add), r=[uT, s5db, py], w=[YV])
            C.op("act", lambda e: e.activation(out=G1.ap, in_=YV.ap, func=AF.Square), r=[YV], w=[G1])
            C.op("dve", lambda e: e.tensor_scalar(out=G1.ap, in0=G1.ap, scalar1=0.044715, scalar2=1.0, op0=ALU.mult, op1=ALU.add), r=[G1], w=[G1])
            C.op("dve", lambda e: e.tensor_tensor(out=G1.ap, in0=G1.ap, in1=YV.ap, op=ALU.mult), r=[G1, YV], w=[G1])
            C.op("act", lambda e: e.activation(out=G1.ap, in_=G1.ap, func=AF.Sigmoid, scale=2.0 * 0.7978845608028654), r=[G1], w=[G1])
            C.op("dve", lambda e: e.tensor_tensor(out=ZB.ap[:, i, :], in0=YV.ap, in1=G1.ap, op=ALU.mult), r=[YV, G1], w=[ZB])
        for io in range(2):
            pg = ps[4 + io]
            for i in range(2):
                mm(pg, pg.ap, wglu, wglu.ap[:, i, io * 128:(io + 1) * 128], ZB, ZB.ap[:, i, :], i == 0, i == 1)
            C.op("act", lambda e: e.activation(out=G1.ap, in_=pg.ap, func=AF.Sigmoid, bias=s5db.ap[:, l, 1, io:io + 1]), r=[pg, s5db], w=[G1])
            C.op("dve", lambda e: e.tensor_tensor(out=mix.ap[:, io, :], in0=ZB.ap[:, io, :], in1=G1.ap, op=ALU.mult), r=[ZB, G1], w=[mix])

    def ret_block(l, b):
        qd = carve(0, [128, 6, BT], BF16)
        kT = carve(6144, [128, 6, BT], BF16)
        kdec = carve(12288, [128, 4, 384], BF16)
        vt = carve(15360, [128, 4, 384], BF16)
        gg = carve(18432, [128, 4, 384], BF16)
        inn = carve(21504, [128, 2, 3, 128], BF16)
        OC = carve(23040, [128, 6, 64], F32)
        SQ = carve(24576, [128, 6, 64], F32)
        GT = carve(26112, [128, 384], F32)
        ROUT = carve(27648, [128, 384], BF16)
        M1 = carve(28672, [128, 6], F32)
        M2 = carve(29696, [128, 6], F32)
        gn = rows.ap[:, 0, :]
        wq = next_slab("w_in")

        def evq(j, p):
            C.op("dve", lambda e: e.tensor_tensor(out=qd.ap[0:64, j, :].rearrange("p (c t) -> p c t", c=4), in0=p.ap[0:64, :].rearrange("p (c t) -> p c t", c=4),
                                                  in1=ret_qtab.ap[0:64, j, :].unsqueeze(1).broadcast_to([64, 4, 128]), op=ALU.mult), r=[p, rtab], w=[qd])
        proj_fm(wq, [0, 64, 128, 192, 256, 320], evq, width=64)
        slab_done()
        wk = next_slab("w_in")

        def evk(j, p):
            C.op("act", lambda e: e.activation(out=kT.ap[0:64, j, :], in_=p.ap[0:64, :], func=AF.Copy), r=[p], w=[kT])
        proj_fm(wk, [0, 64, 128, 192, 256, 320], evk, width=64)

        def evkt(c, p):
            C.op("dve", lambda e: e.tensor_tensor(out=kdec.ap[:, c, :].rearrange("p (h d) -> p h d", h=6), in0=p.ap[:, 0:384].rearrange("p (h d) -> p h d", h=6),
                                                  in1=ret_ktab.ap.unsqueeze(2).broadcast_to([128, 6, 64]), op=ALU.mult), r=[p, rtab], w=[kdec])
        proj_tm(wk, 384, evkt)
        slab_done()
        wv = next_slab("w_in")

        def evv(c, p):
            C.op("act", lambda e: e.activation(out=vt.ap[:, c, :], in_=p.ap[:, 0:384], func=AF.Copy), r=[p], w=[vt])
        proj_tm(wv, 384, evv)
        slab_done()
        wg = next_slab("w_in")

        def evg(c, p):
            C.op("act", lambda e: e.activation(out=GT.ap, in_=p.ap[:, 0:384], func=AF.Silu), r=[p], w=[GT])
            C.op("dve", lambda e: e.tensor_tensor(out=gg.ap[:, c, :], in0=GT.ap, in1=gn, op=ALU.mult), r=[GT, rows], w=[gg])
        proj_tm(wg, 384, evg)
        slab_done()
        if b == 0:
            C.op("dve", lambda e: e.memset(rstate.ap, 0.0), w=[rstate])
            C.op("dve", lambda e: e.memset(rstate_b.ap, 0.0), w=[rstate_b])
        def epilogue(c, po):
            cs = slice(c * 128, (c + 1) * 128)
            po3 = po.ap[:, 0:384].rearrange("p (h d) -> p h d", h=6)
            C.op("dve", lambda e: e.tensor_reduce(out=M1.ap, in_=po3, axis=AX.X, op=ALU.add), r=[po], w=[M1])
            C.op("dve", lambda e: e.tensor_scalar(out=M1.ap, in0=M1.ap, scalar1=-1.0 / 64, scalar2=None, op0=ALU.mult), r=[M1], w=[M1])
            C.op("dve", lambda e: e.tensor_tensor(out=OC.ap, in0=po3, in1=M1.ap.unsqueeze(2).broadcast_to([128, 6, 64]), op=ALU.add), r=[po, M1], w=[OC])
            C.op("act", lambda e: e.activation(out=SQ.ap, in_=OC.ap, func=AF.Square), r=[OC], w=[SQ])
            C.op("dve", lambda e: e.tensor_reduce(out=M2.ap, in_=SQ.ap, axis=AX.X, op=ALU.add), r=[SQ], w=[M2])
            C.op("dve", lambda e: e.tensor_scalar(out=M2.ap, in0=M2.ap, scalar1=1.0 / 64, scalar2=EPS, op0=ALU.mult, op1=ALU.add), r=[M2], w=[M2])
            C.op("act", lambda e: e.activation(out=M2.ap, in_=M2.ap, func=AF.Sqrt), r=[M2], w=[M2])
            C.op("dve", lambda e: e.reciprocal(out=M2.ap, in_=M2.ap), r=[M2], w=[M2])
            C.op("dve", lambda e: e.tensor_tensor(out=OC.ap, in0=OC.ap, in1=M2.ap.unsqueeze(2).broadcast_to([128, 6, 64]), op=ALU.mult), r=[OC, M2], w=[OC])
            C.op("dve", lambda e: e.tensor_tensor(out=ROUT.ap, in0=OC.ap.rearrange("p h d -> p (h d)"), in1=gg.ap[:, c, :], op=ALU.mult), r=[OC, gg], w=[ROUT])
            for j in range(3):
                C.op("pe", lambda e, j=j: e.transpose(pst.ap[:, j * 128:(j + 1) * 128], ROUT.ap[:, j * 128:(j + 1) * 128], identb.ap), r=[ROUT, cmatb], w=[pst], signal=(j == 2))
            C.op("act", lambda e: e.activation(out=mix.ap[:, 2:5, cs], in_=pst.ap[:, 0:384].rearrange("p (a b) -> p a b", a=3), func=AF.Copy), r=[pst], w=[mix])


        pend = []
        for c in range(4):
            cs = slice(c * 128, (c + 1) * 128)
            for hh in range(6):
                pp = ps[hh % 2]
                o = (hh // 2) * 128
                mm(pp, pp.ap[:, o:o + 128], kT, kT.ap[0:64, hh, cs], qd, qd.ap[0:64, hh, cs], True, True)
            for par in range(2):
                C.op("dve", lambda e, par=par: e.tensor_tensor(out=inn.ap[:, par], in0=ps[par].ap[:, 0:384].rearrange("p (a b) -> p a b", a=3), in1=ret_mask.ap[:, par], op=ALU.mult), r=[ps[par], rmaskb], w=[inn])
            po = ps[2] if c % 2 == 0 else ps[6]
            for hh in range(6):
                hs = slice(hh * 64, (hh + 1) * 64)
                mm(po, po.ap[:, hs], inn, inn.ap[:, hh % 2, hh // 2, :], vt, vt.ap[:, c, hs], True, False, signal=False)
                mm(po, po.ap[:, hs], qd, qd.ap[0:64, hh, cs], rstate_b, rstate_b.ap[0:64, hs], False, True, signal=(hh == 5))
            pS = ps[3]
            for hh in range(6):
                hs = slice(hh * 64, (hh + 1) * 64)
                mm(pS, pS.ap[0:64, hs], kdec, kdec.ap[:, c, hs], vt, vt.ap[:, c, hs], True, True, signal=(hh == 5))
            C.op("dve", lambda e: e.tensor_tensor(out=rstate.ap[0:64], in0=rstate.ap[0:64], in1=ret_cd.ap[0:64], op=ALU.mult), r=[rstate, rtab], w=[rstate])
            C.op("dve", lambda e: e.tensor_tensor(out=rstate.ap[0:64], in0=pS.ap[0:64, 0:384], in1=rstate.ap[0:64], op=ALU.add), r=[pS, rstate], w=[rstate])
            C.op("dve", lambda e: e.tensor_copy(out=rstate_b.ap[0:64], in_=rstate.ap[0:64]), r=[rstate], w=[rstate_b])
            if pend:
                pend.pop()()
            pend.append(lambda c=c, po=po: epilogue(c, po))
        pend.pop()()

    def diff_setup(l):
        lam_init = 0.8 - 0.6 * math.exp(-0.3 * l)
        T1 = carve(0, [128, 2, 32], F32)
        C.op("dve", lambda e: e.tensor_tensor(out=T1.ap[:, 0, :], in0=lqk.ap[:, 0, :], in1=lqk.ap[:, 1, :], op=ALU.mult), r=[lqk], w=[T1])
        C.op("dve", lambda e: e.tensor_tensor(out=T1.ap[:, 1, :], in0=lqk.ap[:, 2, :], in1=lqk.ap[:, 3, :], op=ALU.mult), r=[lqk], w=[T1])
        C.op("dve", lambda e: e.tensor_reduce(out=lam.ap[:, 0:2], in_=T1.ap, axis=AX.X, op=ALU.add), r=[T1], w=[lam])
        C.op("act", lambda e: e.activation(out=lam.ap[:, 0:2], in_=lam.ap[:, 0:2], func=AF.Exp), r=[lam], w=[lam])
        C.op("dve", lambda e: e.tensor_tensor(out=lam.ap[:, 2:3], in0=lam.ap[:, 0:1], in1=lam.ap[:, 1:2], op=ALU.subtract), r=[lam], w=[lam])
        C.op("dve", lambda e: e.tensor_scalar(out=lam.ap[:, 3:4], in0=lam.ap[:, 2:3], scalar1=lam_init, scalar2=-1.0, op0=ALU.add, op1=ALU.mult), r=[lam], w=[lam])
        C.op("dve", lambda e: e.tensor_scalar(out=subg.ap, in0=rows.ap[:, 1, :], scalar1=1.0 - lam_init, scalar2=None, op0=ALU.mult), r=[rows], w=[subg])

    def diff_block(l, b):
        qT = carve(0, [128, 4, BT], BF16)
        P = [carve(4096 + k * 2048, [128, 2, BT], BF16) for k in range(2)]
        dtok = carve(8192, [128, 4, 384], BF16)
        UU = carve(11264, [128, 4, 64], F32)
        TT = carve(12288, [128, 4, 64], F32)
        R0 = carve(13312, [128, 4], F32)
        R1 = carve(14336, [128, 4], F32)
        wq = next_slab("w_in")

        def evq(j, p):
            C.op("act", lambda e: e.activation(out=qT.ap[0:96, j, :], in_=p.ap[0:96, :], func=AF.Copy, scale=32.0 ** -0.5), r=[p], w=[qT])
        proj_fm(wq, [0, 96, 192, 288], evq, width=96)
        slab_done()
        wk = next_slab("w_in")

        def evk(j, p):
            C.op("act", lambda e: e.activation(out=kTc.ap[0:96, j, hsl(b)], in_=p.ap[0:96, :], func=AF.Copy), r=[p], w=[kTc])
        proj_fm(wk, [0, 96, 192, 288], evk, width=96)
        slab_done()
        wv = next_slab("w_in")

        def evv(c, p):
            C.op("act", lambda e: e.activation(out=vaug.ap[:, 4 * b + c, :, 0:64], in_=p.ap[:, 0:384].rearrange("p (h d) -> p h d", h=6), func=AF.Copy), r=[p], w=[vaug])
        proj_tm(wv, 384, evv)
        slab_done()
        nkt = 4 * b + 4
        its = [(hh, kt) for hh in range(6) for kt in range(nkt)]

        def scores(it):
            hh, kt = its[it]
            j = kt - 4 * b
            q0 = max(j, 0) * 128
            ks = slice(kt * 128, (kt + 1) * 128)
            pss = [ps[0], ps[1]] if it % 2 == 0 else [ps[4], ps[5]]
            for m in range(2):
                mi = 2 * hh + m
                tj, bp = mi // 3, 32 * (mi % 3)
                mm(pss[m], pss[m].ap[:, q0:BT], kTc, kTc.ap[bp:bp + 32, tj, ks], qT, qT.ap[bp:bp + 32, tj, q0:BT], True, True)

        acc = [ps[2], ps[3]]
        scores(0)
        for it, (hh, kt) in enumerate(its):
            W = (128, 256, 512, 512, 512, 512)[hh]
            j = kt - 4 * b
            q0 = max(j, 0) * 128
            Pk = P[it % 2]
            pss = [ps[0], ps[1]] if it % 2 == 0 else [ps[4], ps[5]]
            if it + 1 < len(its):
                scores(it + 1)
            if kt == 0:
                for m in range(2):
                    mm(acc[m], acc[m].ap[:, 0:260], zerob, zerob.ap[:, 0:128], zerob, zerob.ap[:, 0:260], True, False)
            for m in range(2):
                for s0 in range(0, BT, W):
                    lo = max(s0, q0)
                    if lo >= s0 + W:
                        continue
                    oi = kt - 4 * b - s0 // 128 + 16
                    C.op("act", lambda e, m=m, lo=lo, s0=s0, oi=oi, hh=hh, W=W, Pk=Pk, pss=pss: e.activation(out=Pk.ap[:, m, lo:s0 + W], in_=pss[m].ap[:, lo:s0 + W], func=AF.Exp,
                                                                                 bias=abias.ap[:, hh, oi:oi + 1]), r=[pss[m], abias], w=[Pk])
            if j >= 0:
                C.op("pool", lambda e, Pk=Pk, q0=q0: e.affine_select(out=Pk.ap[:, :, q0:q0 + 128], in_=Pk.ap[:, :, q0:q0 + 128], pattern=[[0, 2], [1, 128]],
                                                       compare_op=ALU.is_ge, fill=0.0, base=0, channel_multiplier=-1), r=[Pk], w=[Pk])
            last = (kt == nkt - 1)
            for m in range(2):
                for qt in range(max(j, 0), 4):
                    mm(acc[m], acc[m].ap[:, qt * 65:(qt + 1) * 65], Pk, Pk.ap[:, m, qt * 128:(qt + 1) * 128], vaug, vaug.ap[:, kt, hh, :], False, last and qt == 3,
                       signal=(qt == 3))
            if not last:
                continue
            a0 = acc[0].ap[:, 0:260].rearrange("p (q e) -> p q e", q=4)
            a1 = acc[1].ap[:, 0:260].rearrange("p (q e) -> p q e", q=4)
            C.op("dve", lambda e: e.reciprocal(out=R0.ap, in_=a0[:, :, 64]), r=[acc[0]], w=[R0])
            C.op("dve", lambda e: e.reciprocal(out=R1.ap, in_=a1[:, :, 64]), r=[acc[1]], w=[R1])
            C.op("dve", lambda e: e.tensor_scalar(out=R1.ap, in0=R1.ap, scalar1=lam.ap[:, 3:4], scalar2=None, op0=ALU.mult), r=[R1, lam], w=[R1])
            C.op("dve", lambda e: e.tensor_tensor(out=TT.ap, in0=a1[:, :, 0:64], in1=R1.ap.unsqueeze(2).broadcast_to([128, 4, 64]), op=ALU.mult), r=[acc[1], R1], w=[TT])
            C.op("dve", lambda e: e.tensor_tensor(out=UU.ap, in0=a0[:, :, 0:64], in1=R0.ap.unsqueeze(2).broadcast_to([128, 4, 64]), op=ALU.mult), r=[acc[0], R0], w=[UU])
            C.op("dve", lambda e: e.tensor_tensor(out=UU.ap, in0=UU.ap, in1=TT.ap, op=ALU.add), r=[UU, TT], w=[UU])
            C.op("act", lambda e: e.activation(out=TT.ap, in_=UU.ap, func=AF.Square), r=[UU], w=[TT])
            C.op("dve", lambda e: e.tensor_reduce(out=R0.ap, in_=TT.ap, axis=AX.X, op=ALU.add), r=[TT], w=[R0])
            C.op("dve", lambda e: e.tensor_scalar(out=R0.ap, in0=R0.ap, scalar1=1.0 / 64, scalar2=EPS, op0=ALU.mult, op1=ALU.add), r=[R0], w=[R0])
            C.op("act", lambda e: e.activation(out=R0.ap, in_=R0.ap, func=AF.Sqrt), r=[R0], w=[R0])
            C.op("dve", lambda e: e.reciprocal(out=R0.ap, in_=R0.ap), r=[R0], w=[R0])
            C.op("dve", lambda e: e.tensor_tensor(out=UU.ap, in0=UU.ap, in1=R0.ap.unsqueeze(2).broadcast_to([128, 4, 64]), op=ALU.mult), r=[UU, R0], w=[UU])
            C.op("dve", lambda e, hh=hh: e.tensor_tensor(out=dtok.ap[:, :, hh * 64:(hh + 1) * 64], in0=UU.ap,
                                                        in1=subg.ap[:, hh * 64:(hh + 1) * 64].unsqueeze(1).broadcast_to([128, 4, 64]), op=ALU.mult), r=[UU, subg], w=[dtok])
        for qt in range(4):
            for j in range(3):
                C.op("pe", lambda e, j=j, qt=qt: e.transpose(pst.ap[:, j * 128:(j + 1) * 128], dtok.ap[:, qt, j * 128:(j + 1) * 128], identb.ap), r=[dtok, cmatb], w=[pst], signal=(j == 2))
            C.op("act", lambda e, qt=qt: e.activation(out=mix.ap[:, 5:8, qt * 128:(qt + 1) * 128], in_=pst.ap[:, 0:384].rearrange("p (a b) -> p a b", a=3), func=AF.Copy), r=[pst], w=[mix])

    def ffn_block(l, b):
        rmsnorm(b, NL + l, hn, 22528, 24576)
        act = carve(0, [128, NCT, BT], BF16)
        c0b = [carve(22528 + k * 2048, [128, BT], F32) for k in range(3)] + [carve(28672, [128, BT], F32)]
        tb = [carve(30720 + k * 2048, [128, BT], F32) for k in range(4)]
        pend = []
        for g in range(NCT // 2):
            w = next_slab("w_up")
            for cl in range(2):
                ct = 2 * g + cl
                pa, pb = (ps[0], ps[1]) if ct % 2 == 0 else (ps[2], ps[3])
                cbuf = c0b[2 * (ct % 2):2 * (ct % 2) + 2]
                for m, p in enumerate((pa, pb)):
                    for kt in range(8):
                        mm(p, p.ap, w, w.ap[:, kt, m, cl * 128:(cl + 1) * 128], hn, hn.ap[:, kt, :], kt == 0, kt == 7)
                for m, p in enumerate((pa, pb)):
                    ci = m * NCT + ct
                    cc = cbuf[m]
                    C.op("act", lambda e, p=p, cc=cc, ci=ci: e.activation(out=cc.ap, in_=p.ap, func=AF.Identity, scale=cw.ap[:, 2, ci:ci + 1], bias=cw.ap[:, 3, ci:ci + 1]), r=[p, cw], w=[cc])
                    if m == 0:
                        C.op("dve", lambda e, p=p, cc=cc, ci=ci: e.scalar_tensor_tensor(out=cc.ap[:, 1:BT], in0=p.ap[:, 0:BT - 1], scalar=cw.ap[:, 1, ci:ci + 1], in1=cc.ap[:, 1:BT], op0=ALU.mult, op1=ALU.add), r=[p, cw, cc], w=[cc])
                        C.op("dve", lambda e, p=p, cc=cc, ci=ci: e.scalar_tensor_tensor(out=cc.ap[:, 2:BT], in0=p.ap[:, 0:BT - 2], scalar=cw.ap[:, 0, ci:ci + 1], in1=cc.ap[:, 2:BT], op0=ALU.mult, op1=ALU.add), r=[p, cw, cc], w=[cc])
                        if b > 0:
                            C.op("dve", lambda e, cc=cc, ci=ci: e.scalar_tensor_tensor(out=cc.ap[:, 0:2], in0=halo.ap[:, ci, 0:2], scalar=cw.ap[:, 0, ci:ci + 1], in1=cc.ap[:, 0:2], op0=ALU.mult, op1=ALU.add), r=[halo, cw, cc], w=[cc])
                            C.op("dve", lambda e, cc=cc, ci=ci: e.scalar_tensor_tensor(out=cc.ap[:, 0:1], in0=halo.ap[:, ci, 1:2], scalar=cw.ap[:, 1, ci:ci + 1], in1=cc.ap[:, 0:1], op0=ALU.mult, op1=ALU.add), r=[halo, cw, cc], w=[cc])
                    else:
                        t1, t0 = tb[2 * (ct % 2)], tb[2 * (ct % 2) + 1]
                        C.op("act", lambda e, p=p, t1=t1, ci=ci: e.activation(out=t1.ap[:, 1:BT], in_=p.ap[:, 0:BT - 1], func=AF.Identity, scale=cw.ap[:, 1, ci:ci + 1]), r=[p, cw], w=[t1])
                        C.op("act", lambda e, p=p, t0=t0, ci=ci: e.activation(out=t0.ap[:, 2:BT], in_=p.ap[:, 0:BT - 2], func=AF.Identity, scale=cw.ap[:, 0, ci:ci + 1]), r=[p, cw], w=[t0])
                        if b > 0:
                            C.op("dve", lambda e, t1=t1, ci=ci: e.tensor_scalar(out=t1.ap[:, 0:1], in0=halo.ap[:, ci, 1:2], scalar1=cw.ap[:, 1, ci:ci + 1], scalar2=None, op0=ALU.mult), r=[halo, cw], w=[t1])
                            C.op("dve", lambda e, t0=t0, ci=ci: e.tensor_scalar(out=t0.ap[:, 0:2], in0=halo.ap[:, ci, 0:2], scalar1=cw.ap[:, 0, ci:ci + 1], scalar2=None, op0=ALU.mult), r=[halo, cw], w=[t0])
                        else:
                            C.op("dve", lambda e, t1=t1: e.memset(t1.ap[:, 0:1], 0.0), w=[t1])
                            C.op("dve", lambda e, t0=t0: e.memset(t0.ap[:, 0:2], 0.0), w=[t0])
                        C.op("pool", lambda e, cc=cc, t1=t1: e.tensor_tensor(out=cc.ap, in0=cc.ap, in1=t1.ap, op=ALU.add), r=[cc, t1], w=[cc])
                        C.op("pool", lambda e, cc=cc, t0=t0: e.tensor_tensor(out=cc.ap, in0=cc.ap, in1=t0.ap, op=ALU.add), r=[cc, t0], w=[cc])
                    if b < NBLK - 1:
                        C.op("dve", lambda e, p=p, ci=ci: e.tensor_copy(out=halo.ap[:, ci, :], in_=p.ap[:, BT - 2:BT]), r=[p], w=[halo])
                if pend:
                    pend.pop()()

                def fin(cbuf=cbuf, ct=ct):
                    C.op("act", lambda e: e.activation(out=cbuf[0].ap, in_=cbuf[0].ap, func=AF.Silu), r=[cbuf[0]], w=[cbuf[0]])
                    C.op("dve", lambda e: e.tensor_tensor(out=act.ap[:, ct, :], in0=cbuf[0].ap, in1=cbuf[1].ap, op=ALU.mult), r=[cbuf[0], cbuf[1]], w=[act])
                pend.append(fin)
            slab_done()
        pend.pop()()
        for dp in range(4):
            pd = [ps[4], ps[5]]
            for hf in range(2):
                w = next_slab("w_down")
                for dl in range(2):
                    for k in range(11):
                        mm(pd[dl], pd[dl].ap, w, w.ap[:, k, dl * 128:(dl + 1) * 128], act, act.ap[:, hf * 11 + k, :], hf == 0 and k == 0, hf == 1 and k == 10,
                           signal=(k == 10))
                slab_done()
            for dl in range(2):
                h_add(b, 2 * dp + dl, pd[dl])

    def ple_block(l, b):
        rmsnorm(b, 2 * NL + l, hn, 0, 2048)
        pf = carve(4096, [128, 2, BT], F32)
        pb = carve(8192, [128, 2, BT], BF16)
        sg = [carve(10240 + k * 2048, [128, BT], F32) for k in range(2)]
        C.dma("sp", pf.ap, pT_d[l, :, :, hsl(b)], w=[pf])
        C.op("dve", lambda e: e.tensor_copy(out=pb.ap, in_=pf.ap), r=[pf], w=[pb])
        for half in range(2):
            w = next_slab("w_pg")
            wpe = next_slab("w_pe")
            for dl in range(4):
                dt = 4 * half + dl
                pg, pp = (ps[0], ps[1]) if dt % 2 == 0 else (ps[2], ps[3])
                c0 = dl * 128
                for kt in range(8):
                    mm(pg, pg.ap, w, w.ap[:, kt, c0:c0 + 128], hn, hn.ap[:, kt, :], kt == 0, kt == 7)
                for kt in range(2):
                    mm(pp, pp.ap, wpe, wpe.ap[:, kt, c0:c0 + 128], pb, pb.ap[:, kt, :], kt == 0, kt == 1)
                s_ = sg[dt % 2]
                C.op("act", lambda e, s_=s_, pg=pg: e.activation(out=s_.ap, in_=pg.ap, func=AF.Sigmoid), r=[pg], w=[s_])
                C.op("dve", lambda e, s_=s_, pp=pp: e.tensor_tensor(out=s_.ap, in0=pp.ap, in1=s_.ap, op=ALU.mult), r=[pp, s_], w=[s_])
                C.op("dve", lambda e, s_=s_, dt=dt: e.tensor_tensor(out=h.ap[:, dt, hsl(b)], in0=s_.ap, in1=h.ap[:, dt, hsl(b)], op=ALU.add), r=[s_, h], w=[h])
            slab_done()
            slab_done()

    for l in range(nl):
        if l > 0:
            C.new_epoch()
        C.dma("sp", cw.ap, cw_d[:, l], w=[cw])
        C.dma("sp", rows.ap, rows_d[:, l], w=[rows])
        C.dma("sp", lqk.ap, lqk_d[:, l], w=[lqk])
        C.mark("setup")
        if on("s5"):
            s5_setup(l)
        if on("diff"):
            diff_setup(l)
        for b in range(nblk):
            C.mark("norm1")
            rmsnorm(b, l, hn, 26624, 28672)
            C.mark("s5")
            if on("s5"):
                s5_block(l, b)
            else:
                C.op("dve", lambda e: e.memset(mix.ap[:, 0:2, :], 0.0), w=[mix])
            C.mark("ret")
            if on("ret"):
                ret_block(l, b)
            else:
                C.op("dve", lambda e: e.memset(mix.ap[:, 2:5, :], 0.0), w=[mix])
            if b == 1 and l + 1 < nl:
                emit_cast(l + 1)
            C.mark("diff")
            if on("diff"):
                diff_block(l, b)
            else:
                C.op("dve", lambda e: e.memset(mix.ap[:, 5:8, :], 0.0), w=[mix])
            if dbg and l == nl - 1:
                C.dma("sp", dbg_d["mix"][:, :, hsl(b)], mix.ap, r=[mix])
            C.mark("wout")
            for half in range(2):
                w = next_slab("w_out")
                for dl in range(4):
                    dt = 4 * half + dl
                    p = ps[4 + dt % 2]
                    c0 = dl * 128
                    for kt in range(8):
                        mm(p, p.ap, w, w.ap[:, kt, c0:c0 + 128], mix, mix.ap[:, kt, :], kt == 0, kt == 7)
                    h_add(b, dt, p)
                slab_done()
            C.mark("ffn")
            if on("ffn"):
                ffn_block(l, b)
            C.mark("ple")
            if on("ple"):
                ple_block(l, b)
            C.mark("end")
            if dbg:
                C.dma("sp", dbg_d["h%d" % l][:, :, hsl(b)], h.ap[:, :, hsl(b)], r=[h])
            if l == nl - 1:
                yo = carve(0, [128, 8, BT], F32)
                rmsnorm(b, 3 * NL, yo, 16384, 18432)
                C.dma("sp", yT_d[:, :, hsl(b)], yo.ap, r=[yo])
    deps = [(s_[0], s_[1]) for pl in C.dma_pool.values() for s_ in pl if s_[1] > 0]
    C._wait("sp", deps)
    build_program.marks = C.marks
    stuck, counts = C.simulate()
    print("instr counts", counts, "stuck", stuck)
    assert not stuck, stuck
    es.close()
    return nc


def host_layouts(inp):
    f = np.float32
    L = NL
    shared = {}
    for nm in ["w_in", "w_out", "w_up", "w_down", "w_pg", "w_pe"]:
        shared[nm] = np.ascontiguousarray(inp[nm], dtype=f)
    shared["w_glu"] = np.ascontiguousarray(np.asarray(inp["ssm_w_glu"], f).reshape(L, 2, 128, 256).transpose(0, 2, 1, 3))
    gl = [np.asarray(inp["norm1_g"], f), np.asarray(inp["norm2_g"], f), np.asarray(inp["norm3_g"], f)]
    gains = np.concatenate([g.reshape(L, 8, 128) for g in gl] + [np.asarray(inp["final_g"], f).reshape(1, 8, 128)], 0)
    shared["gains"] = np.ascontiguousarray(gains.transpose(2, 0, 1))
    cwv = np.concatenate([np.asarray(inp["conv_w"], f), np.asarray(inp["conv_b"], f)[:, None, :]], 1)
    shared["cw"] = np.ascontiguousarray(cwv.reshape(L, 4, 2 * NCT, 128).transpose(3, 0, 1, 2))
    lre = np.asarray(inp["ssm_lam_re"], f).reshape(L, 1024)
    lim = np.asarray(inp["ssm_lam_im"], f).reshape(L, 1024)
    ldt = np.repeat(np.asarray(inp["ssm_log_dt"], f), 64, axis=1)
    row = np.stack([lre, lim, ldt], 1)
    shared["s5row"] = np.ascontiguousarray(np.broadcast_to(row[:, None], (L, 128, 3, 1024)))
    bre = np.asarray(inp["ssm_b_re"], f)
    bim = np.asarray(inp["ssm_b_im"], f)
    s5b = np.zeros((L, 128, 2, 2, 512), f)
    for ri, bb in enumerate((bre, bim)):
        for g in range(16):
            i, gl_ = g // 8, g % 8
            s5b[:, gl_ * 16:(gl_ + 1) * 16, ri, i, gl_ * 64:(gl_ + 1) * 64] = bb[:, g].transpose(0, 2, 1)
    shared["s5b"] = s5b
    cre = np.asarray(inp["ssm_c_re"], f)
    cim = np.asarray(inp["ssm_c_im"], f)
    s5c = np.zeros((L, 128, 2, 8, 128), f)
    for ri, cc in enumerate((cre, cim)):
        for g in range(16):
            j, g2, gl_ = g // 2, g % 2, g % 8
            s5c[:, g2 * 64:(g2 + 1) * 64, ri, j, gl_ * 16:(gl_ + 1) * 16] = cc[:, g].transpose(0, 2, 1)
    shared["s5c"] = s5c
    dsk = np.asarray(inp["ssm_d"], f).reshape(L, 2, 128)
    bgl = np.asarray(inp["ssm_b_glu"], f).reshape(L, 2, 128)
    shared["s5db"] = np.ascontiguousarray(np.stack([dsk, bgl], 1).transpose(3, 0, 1, 2))
    rws = np.stack([np.asarray(inp["ret_gn_g"], f), np.asarray(inp["diff_subln_g"], f)], 1)
    shared["rows"] = np.ascontiguousarray(np.broadcast_to(rws[None], (128, L, 2, 384)))
    lq = np.stack([np.asarray(inp[k], f) for k in ("diff_lq1", "diff_lk1", "diff_lq2", "diff_lk2")], 1)
    shared["lqk"] = np.ascontiguousarray(np.broadcast_to(lq[None], (128, L, 4, 32)))
    cm = np.zeros((128, 2, 128), f)
    cm[:, 0] = np.eye(128, dtype=f)
    cm[:, 1] = np.triu(np.ones((128, 128), f))
    shared["cmat"] = cm
    lg = RET_LOG_GAMMA.astype(np.float64)
    pos = np.arange(128, dtype=np.float64)
    qtab = np.zeros((128, 6, 128))
    for j in range(6):
        qtab[:, j] = np.exp((pos + 1.0) * lg[j])[None, :]
    ktab = np.exp((127.0 - pos)[:, None] * lg[None, :]) * 0.125
    mask = np.zeros((128, 2, 3, 128))
    for hh in range(6):
        mask[:, hh % 2, hh // 2, :] = np.where(pos[None, :] >= pos[:, None], np.exp(-(pos[:, None] + 1.0) * lg[hh]) * 0.125, 0.0)
    cd = np.zeros((128, 384))
    for j in range(6):
        cd[:, j * 64:(j + 1) * 64] = np.exp(128.0 * lg[j])
    tp = np.stack([pos + 1.0, -(pos + 1.0)], 1)
    shared["rtab"] = np.concatenate([qtab.reshape(128, -1), ktab, cd, tp], 1).astype(f)
    shared["rmask"] = mask.astype(f)
    ab = np.zeros((128, 6, 20), np.float64)
    for hh in range(6):
        for o in range(20):
            ab[:, hh, o] = np.float64(ALIBI_SLOPES[hh]) * (128.0 * (o - 16) + np.arange(128))
    shared["abias"] = ab.astype(f)
    return shared


def kernel(**inputs):
    cfg = inputs.pop("_cfg", {})
    x = np.asarray(inputs["x"], np.float32)
    p = np.asarray(inputs["p"], np.float32)
    shared = host_layouts(inputs)
    nl_ = cfg.get("layers", NL)
    for nm in ["w_in", "w_out", "w_up", "w_down", "w_pg", "w_pe"]:
        shared[nm] = shared[nm][:nl_]
    nc = build_program(cfg)
    in_maps = []
    for c in range(8):
        m = dict(shared)
        m["xT"] = np.ascontiguousarray(x[c].T.reshape(8, 128, SEQ).transpose(1, 0, 2))
        m["pT"] = np.ascontiguousarray(p[:, c].transpose(0, 2, 1).reshape(NL, 2, 128, SEQ).transpose(0, 2, 1, 3))
        in_maps.append(m)
    res = run_bass_kernel_spmd(nc, in_maps, core_ids=list(range(8)))
    out = np.empty((8, SEQ, D), np.float32)
    for c in range(8):
        yT = np.asarray(res.results[c]["yT"], np.float32)
        out[c] = yT.transpose(1, 0, 2).reshape(D, SEQ).T
    if cfg.get("dbg"):
        kernel.last = res.results
    return out
```
